# Optimizing a Trainium2 kernel written in Bass

```python
import math
import jax, jax.numpy as jnp
from jax import lax
import numpy as np

D_MODEL = 1024
BATCH = 8
SEQ = 4096
DEPTH = 1

S5_GROUP = 16
S5_WIDTH = D_MODEL // 2
S5_GROUPS = S5_WIDTH // S5_GROUP
S5_STATE = 64
S5_DT_MIN = 0.001
S5_DT_MAX = 0.1
GLA_HEADS = 4
GLA_VAL_WIDTH = D_MODEL // 2
GLA_DV = GLA_VAL_WIDTH // GLA_HEADS
GLA_DK = GLA_DV // 2
GLA_KEY_WIDTH = GLA_HEADS * GLA_DK
GLA_GATE_RANK = 16
GLA_TAU = 16.0
GLA_CHUNK = 64
D_FF = ((8 * D_MODEL // 3 + 255) // 256) * 256
EPS = 1e-6
IN_SIZES = (S5_WIDTH, GLA_KEY_WIDTH, GLA_KEY_WIDTH, GLA_VAL_WIDTH, GLA_VAL_WIDTH,
            GLA_GATE_RANK, D_MODEL, D_MODEL)
IN_COLS = sum(IN_SIZES)

kernel_name = "hybrid_s5_gla_macaron_block"


def rms_norm(x, g):
    xf = x.astype(jnp.float32)
    y = xf * lax.rsqrt(jnp.mean(xf * xf, axis=-1, keepdims=True) + EPS)
    return (y * g.astype(jnp.float32)).astype(x.dtype)


def swiglu(x, w1, w3, w2):
    return (jax.nn.silu(x @ w1) * (x @ w3)) @ w2


def _ssm_combine(e1, e2):
    ar1, ai1, br1, bi1 = e1
    ar2, ai2, br2, bi2 = e2
    return (ar1 * ar2 - ai1 * ai2,
            ar1 * ai2 + ai1 * ar2,
            ar2 * br1 - ai2 * bi1 + br2,
            ar2 * bi1 + ai2 * br1 + bi2)


def s5_mixer(u, lam_re, lam_im, log_dt, b_re, b_im, c_re, c_im, d_skip, w_glu, b_glu):
    bsz, seq, _ = u.shape
    f32 = jnp.float32
    lam_re = lam_re.astype(f32)
    lam_im = lam_im.astype(f32)
    dt = jnp.exp(log_dt.astype(f32))[:, None]
    mag = jnp.exp(lam_re * dt)
    ar = mag * jnp.cos(lam_im * dt)
    ai = mag * jnp.sin(lam_im * dt)
    den = lam_re * lam_re + lam_im * lam_im
    nr = ar - 1.0
    fr = (nr * lam_re + ai * lam_im) / den
    fi = (ai * lam_re - nr * lam_im) / den
    b_re = b_re.astype(f32)
    b_im = b_im.astype(f32)
    bbar_re = fr[:, :, None] * b_re - fi[:, :, None] * b_im
    bbar_im = fr[:, :, None] * b_im + fi[:, :, None] * b_re
    ug = u.astype(f32).reshape(bsz, seq, S5_GROUPS, S5_GROUP)
    bu_re = jnp.einsum('blgh,gph->lbgp', ug, bbar_re)
    bu_im = jnp.einsum('blgh,gph->lbgp', ug, bbar_im)
    a_re = jnp.broadcast_to(ar, (seq, 1, S5_GROUPS, S5_STATE))
    a_im = jnp.broadcast_to(ai, (seq, 1, S5_GROUPS, S5_STATE))
    _, _, xr, xi = lax.associative_scan(_ssm_combine, (a_re, a_im, bu_re, bu_im), axis=0)
    y = (jnp.einsum('ghp,lbgp->blgh', c_re.astype(f32), xr)
         - jnp.einsum('ghp,lbgp->blgh', c_im.astype(f32), xi)
         + d_skip.astype(f32) * ug)
    y = y.reshape(bsz, seq, S5_WIDTH).astype(u.dtype)
    z = jax.nn.gelu(y)
    return z * jax.nn.sigmoid(z @ w_glu + b_glu)


def gla_mixer(q, k, v, r, a_low, w_a_up, b_a_up, g_norm):
    bsz, seq, _ = q.shape
    n_chunks = seq // GLA_CHUNK
    f32 = jnp.float32
    shp_k = (bsz, n_chunks, GLA_CHUNK, GLA_HEADS, GLA_DK)
    shp_v = (bsz, n_chunks, GLA_CHUNK, GLA_HEADS, GLA_DV)
    qc = q.astype(f32).reshape(shp_k) * (GLA_DK ** -0.5)
    kc = k.astype(f32).reshape(shp_k)
    vc = v.astype(f32).reshape(shp_v)
    log_a = jax.nn.log_sigmoid((a_low @ w_a_up + b_a_up).astype(f32)) / GLA_TAU
    bcum = jnp.cumsum(log_a.reshape(shp_k), axis=2)
    b_last = bcum[:, :, -1]
    q_t = qc * jnp.exp(bcum)
    k_t = kc * jnp.exp(-bcum)
    scores = jnp.einsum('bnthd,bnshd->bnhts', q_t, k_t)
    causal = jnp.tril(jnp.ones((GLA_CHUNK, GLA_CHUNK), dtype=bool))
    scores = jnp.where(causal, scores, 0.0)
    o_intra = jnp.einsum('bnhts,bnshv->bnthv', scores, vc)
    k_end = kc * jnp.exp(b_last[:, :, None] - bcum)
    d_state = jnp.einsum('bnshd,bnshv->bnhdv', k_end, vc)
    decay = jnp.exp(b_last)

    def step(state, inp):
        dec, ds = inp
        return dec[..., None] * state + ds, state

    s0 = jnp.zeros((bsz, GLA_HEADS, GLA_DK, GLA_DV), f32)
    _, s_prev = lax.scan(step, s0, (jnp.moveaxis(decay, 1, 0), jnp.moveaxis(d_state, 1, 0)))
    s_prev = jnp.moveaxis(s_prev, 0, 1)
    o_inter = jnp.einsum('bnthd,bnhdv->bnthv', q_t, s_prev)
    o = (o_intra + o_inter).reshape(bsz, seq, GLA_HEADS, GLA_DV)
    o = o * lax.rsqrt(jnp.mean(o * o, axis=-1, keepdims=True) + EPS)
    o = o.reshape(bsz, seq, GLA_VAL_WIDTH) * g_norm.astype(f32)
    return (o * jax.nn.silu(r.astype(f32))).astype(q.dtype)


def setup_inputs(seed: int = 0) -> dict:
    key = jax.random.key(seed)
    ks = jax.random.split(key, 32)
    f32 = jnp.float32
    L, G, P, H = DEPTH, S5_GROUPS, S5_STATE, S5_GROUP

    def nrm(k, shape, scale):
        return jax.random.normal(k, shape, f32) * scale

    def gain(k, shape):
        return 1.0 + 0.01 * jax.random.normal(k, shape, f32)

    n_idx = jnp.arange(P, dtype=f32)
    return {
        "x": nrm(ks[0], (BATCH, SEQ, D_MODEL), 1.0),
        "ffn1_norm": gain(ks[1], (L, D_MODEL)),
        "ffn1_w1": nrm(ks[2], (L, D_MODEL, D_FF), D_MODEL ** -0.5),
        "ffn1_w3": nrm(ks[3], (L, D_MODEL, D_FF), D_MODEL ** -0.5),
        "ffn1_w2": nrm(ks[4], (L, D_FF, D_MODEL), D_FF ** -0.5),
        "mix_norm": gain(ks[5], (L, D_MODEL)),
        "w_in": nrm(ks[6], (L, D_MODEL, IN_COLS), D_MODEL ** -0.5),
        "s5_lambda_re": -0.5 + 0.01 * jax.random.normal(ks[7], (L, G, P), f32),
        "s5_lambda_im": math.pi * n_idx + 0.01 * jax.random.normal(ks[8], (L, G, P), f32),
        "s5_log_dt": jax.random.uniform(ks[9], (L, G), f32, math.log(S5_DT_MIN), math.log(S5_DT_MAX)),
        "s5_b_re": nrm(ks[10], (L, G, P, H), (2.0 * H) ** -0.5),
        "s5_b_im": nrm(ks[11], (L, G, P, H), (2.0 * H) ** -0.5),
        "s5_c_re": nrm(ks[12], (L, G, H, P), (2.0 * P) ** -0.5),
        "s5_c_im": nrm(ks[13], (L, G, H, P), (2.0 * P) ** -0.5),
        "s5_d": nrm(ks[14], (L, G, H), 1.0),
        "s5_glu_w": nrm(ks[15], (L, S5_WIDTH, S5_WIDTH), S5_WIDTH ** -0.5),
        "s5_glu_b": nrm(ks[16], (L, S5_WIDTH), 0.01),
        "gla_a_up_w": nrm(ks[17], (L, GLA_GATE_RANK, GLA_KEY_WIDTH), GLA_GATE_RANK ** -0.5),
        "gla_a_up_b": nrm(ks[18], (L, GLA_KEY_WIDTH), 0.1),
        "gla_out_norm": gain(ks[19], (L, GLA_VAL_WIDTH)),
        "proj_s5": nrm(ks[20], (L, S5_WIDTH, D_MODEL), S5_WIDTH ** -0.5),
        "proj_gla": nrm(ks[21], (L, GLA_VAL_WIDTH, D_MODEL), GLA_VAL_WIDTH ** -0.5),
        "w_out": nrm(ks[22], (L, D_MODEL, D_MODEL), D_MODEL ** -0.5),
        "ffn2_norm": gain(ks[23], (L, D_MODEL)),
        "ffn2_w1": nrm(ks[24], (L, D_MODEL, D_FF), D_MODEL ** -0.5),
        "ffn2_w3": nrm(ks[25], (L, D_MODEL, D_FF), D_MODEL ** -0.5),
        "ffn2_w2": nrm(ks[26], (L, D_FF, D_MODEL), D_FF ** -0.5),
        "final_norm": gain(ks[27], (D_MODEL,)),
    }


def reference(x, ffn1_norm, ffn1_w1, ffn1_w3, ffn1_w2, mix_norm, w_in,
              s5_lambda_re, s5_lambda_im, s5_log_dt, s5_b_re, s5_b_im, s5_c_re, s5_c_im,
              s5_d, s5_glu_w, s5_glu_b, gla_a_up_w, gla_a_up_b, gla_out_norm,
              proj_s5, proj_gla, w_out, ffn2_norm, ffn2_w1, ffn2_w3, ffn2_w2, final_norm):
    split_idx = list(np.cumsum(IN_SIZES)[:-1])
    h = x
    for l in range(DEPTH):
        h = h + 0.5 * swiglu(rms_norm(h, ffn1_norm[l]), ffn1_w1[l], ffn1_w3[l], ffn1_w2[l])
        u = rms_norm(h, mix_norm[l])
        s5_in, q, k, v, r, a_low, g_s5, g_gla = jnp.split(u @ w_in[l], split_idx, axis=-1)
        y_s5 = s5_mixer(s5_in, s5_lambda_re[l], s5_lambda_im[l], s5_log_dt[l],
                        s5_b_re[l], s5_b_im[l], s5_c_re[l], s5_c_im[l], s5_d[l],
                        s5_glu_w[l], s5_glu_b[l])
        y_gla = gla_mixer(q, k, v, r, a_low, gla_a_up_w[l], gla_a_up_b[l], gla_out_norm[l])
        merged = (jax.nn.sigmoid(g_s5) * (y_s5 @ proj_s5[l])
                  + jax.nn.sigmoid(g_gla) * (y_gla @ proj_gla[l]))
        h = h + merged @ w_out[l]
        h = h + 0.5 * swiglu(rms_norm(h, ffn2_norm[l]), ffn2_w1[l], ffn2_w3[l], ffn2_w2[l])
    return rms_norm(h, final_norm).astype(x.dtype)
```

```python
from concourse.bass_utils import run_bass_kernel_spmd
import numpy as np
import concourse.bass as bass
import concourse.mybir as mybir

F32 = mybir.dt.float32
BF16 = mybir.dt.bfloat16
AF = mybir.ActivationFunctionType
ALU = mybir.AluOpType
AX = mybir.AxisListType

ENGS = ("pe", "act", "dve", "pool", "sp")
CE = ("pe", "act", "dve", "pool")


class Op:
    __slots__ = ("eng", "fn", "reads", "writes", "dma", "semkey", "idx", "signal",
                 "sigidx", "deps", "dmacount")

    def __init__(self, eng, fn, reads, writes, dma, semkey):
        self.eng = eng
        self.fn = fn
        self.reads = tuple(reads)
        self.writes = tuple(writes)
        self.dma = dma
        self.semkey = semkey
        self.signal = False
        self.sigidx = 0
        self.deps = ()
        self.dmacount = 0


class Prog:
    def __init__(self, nc, es, same_engine_sync=True):
        self.nc = nc
        self.es = es
        self.ops = []
        self.same_engine_sync = same_engine_sync
        self.csem = {e: es.enter_context(nc.semaphore("c_" + e)) for e in CE}
        self.dsem = {}
        self.ccount = {e: 0 for e in CE}
        self.dcount = {}
        self.waited = {e: {} for e in ENGS}
        self.nops = 0

    def op(self, eng, fn, reads=(), writes=()):
        o = Op(eng, fn, reads, writes, False, None)
        self.ops.append(o)
        return o

    def dma(self, eng, fn, reads=(), writes=(), semkey=None):
        assert eng in ("sp", "act", "pool")
        if semkey is None:
            semkey = writes[0] if writes else reads[0]
        o = Op(eng, fn, reads, writes, True, semkey)
        self.ops.append(o)
        return o

    def _analyze(self):
        last_writer = {}
        readers = {}
        for i, o in enumerate(self.ops):
            o.idx = i
            deps = {}
            for h in o.reads:
                w = last_writer.get(h)
                if w is not None:
                    deps[w.idx] = w
            for h in o.writes:
                w = last_writer.get(h)
                if w is not None:
                    deps[w.idx] = w
                for r in readers.get(h, ()):
                    deps[r.idx] = r
            deps.pop(i, None)
            o.deps = tuple(deps.values())
            for h in o.reads:
                lst = readers.setdefault(h, [])
                if not o.dma:
                    lst[:] = [r for r in lst if r.dma or r.eng != o.eng]
                lst.append(o)
            for h in o.writes:
                last_writer[h] = o
                readers[h] = []
            if o.dma:
                if o.semkey not in self.dsem:
                    self.dsem[o.semkey] = self.es.enter_context(
                        self.nc.semaphore("d_%d" % len(self.dsem)))
                    self.dcount[o.semkey] = 0
                c = self.dcount[o.semkey] + 16
                self.dcount[o.semkey] = c
                o.dmacount = c
        for o in self.ops:
            for d in o.deps:
                if d.dma:
                    continue
                if d.eng == o.eng and not o.dma:
                    if d.eng == "pe" or not self.same_engine_sync:
                        continue
                d.signal = True
        last = {}
        for o in self.ops:
            if not o.dma:
                last[o.eng] = o
        for o in last.values():
            o.signal = True
        for o in self.ops:
            if not o.dma and o.signal:
                self.ccount[o.eng] += 1
                o.sigidx = self.ccount[o.eng]

    def emit_phase(self):
        nc = self.nc
        self._analyze()
        ops = self.ops
        self.nops += len(ops)
        same_sync = self.same_engine_sync
        csem, dsem = self.csem, self.dsem
        cend = dict(self.ccount)
        dend = dict(self.dcount)

        def stream(engname):
            def body(eng):
                waited = self.waited[engname]
                for o in ops:
                    if o.eng != engname:
                        continue
                    need = {}
                    for d in o.deps:
                        if d.dma:
                            key = ("d", d.semkey)
                            val = d.dmacount
                        else:
                            if d.eng == engname and not o.dma:
                                if engname == "pe" or not same_sync:
                                    continue
                            key = ("c", d.eng)
                            val = d.sigidx
                        if val > need.get(key, 0):
                            need[key] = val
                    for key, val in need.items():
                        if waited.get(key, 0) >= val:
                            continue
                        waited[key] = val
                        sem = dsem[key[1]] if key[0] == "d" else csem[key[1]]
                        eng.wait_ge(sem, val)
                    ins = o.fn(eng)
                    if o.dma:
                        ins.then_inc(dsem[o.semkey], 16)
                    elif o.signal:
                        ins.then_inc(csem[o.eng], 1)
                for e2 in CE:
                    key = ("c", e2)
                    if cend[e2] > waited.get(key, 0):
                        waited[key] = cend[e2]
                        eng.wait_ge(csem[e2], cend[e2])
                for k2, v2 in dend.items():
                    key = ("d", k2)
                    if v2 > waited.get(key, 0):
                        waited[key] = v2
                        eng.wait_ge(dsem[k2], v2)
            return body

        with nc.Block() as block:
            block.tensor(stream("pe"))
            block.scalar(stream("act"))
            block.vector(stream("dve"))
            block.gpsimd(stream("pool"))
            block.sync(stream("sp"))
        self.ops = []

from contextlib import ExitStack
import math

D = 1024
FF = 2816
NF = 22
SEQ = 4096
NT = 512
NTILES = 8
NCH = 512
INC = 4112
C_S5, C_Q, C_K, C_V, C_R, C_AL, C_GS, C_GG = 0, 512, 768, 1024, 1536, 2048, 2064, 3088
EPS = 1e-6
PI = math.pi

PARAM_SHAPES = {
    "ffn1_norm": [D], "ffn1_w1": [D, FF], "ffn1_w3": [D, FF], "ffn1_w2": [FF, D],
    "mix_norm": [D], "w_in": [D, INC],
    "s5_lambda_re": [32, 64], "s5_lambda_im": [32, 64], "s5_log_dt": [32],
    "s5_b_re": [32, 64, 16], "s5_b_im": [32, 64, 16], "s5_c_re": [32, 16, 64], "s5_c_im": [32, 16, 64],
    "s5_d": [32, 16], "s5_glu_w": [512, 512], "s5_glu_b": [512],
    "gla_a_up_w": [16, 256], "gla_a_up_b": [256], "gla_out_norm": [512],
    "proj_s5": [512, D], "proj_gla": [512, D], "w_out": [D, D],
    "ffn2_norm": [D], "ffn2_w1": [D, FF], "ffn2_w3": [D, FF], "ffn2_w2": [FF, D],
    "final_norm": [D],
}


def build_program(dbg=(), stop_after=None, ntiles=NTILES):
    nc = bass.Bass("TRN2", target_bir_lowering=False)
    I = {}
    I["x"] = nc.dram_tensor("x", [SEQ, D], F32, kind="ExternalInput").ap()
    for k, shp in PARAM_SHAPES.items():
        I[k] = nc.dram_tensor(k, shp, F32, kind="ExternalInput").ap()
    out = nc.dram_tensor("out", [SEQ, D], F32, kind="ExternalOutput").ap()

    def dscr(name, shape, dt):
        kind = "ExternalOutput" if name in dbg else "Internal"
        return nc.dram_tensor(name, shape, dt, kind=kind).ap()

    WC = {}
    for nm in ("ffn1_w1", "ffn1_w3", "ffn1_w2", "ffn2_w1", "ffn2_w3", "ffn2_w2", "w_in",
               "s5_glu_w", "proj_s5", "proj_gla", "w_out"):
        WC[nm] = dscr(nm + "_c", PARAM_SHAPES[nm], BF16)
    h_scr = dscr("h_scr", [SEQ, D], F32)
    uT_scr = dscr("uT_scr", [NTILES, 128, 8 * NT], BF16)
    dbgout = {}

    with ExitStack() as es:
        import os as _os
        P = Prog(nc, es, same_engine_sync=not _os.environ.get("NO_SES"))

        def sb(scope, name, shape, dt):
            return scope.enter_context(nc.sbuf_tensor(name, shape, dt))

        banks = [es.enter_context(nc.psum_tensor("bank%d" % i, [128, 512], F32)) for i in range(8)]
        bstate = {"i": 0}

        def nb():
            i = bstate["i"]
            bstate["i"] = (i + 1) % 8
            return banks[i], ("ps", i)

        def mm(out_, lhsT, rhs, start, stop, r, w, **kw):
            P.op("pe", lambda e: e.matmul(out_, lhsT=lhsT, rhs=rhs, start=start, stop=stop, **kw),
                 reads=r, writes=w)

        def act(out_, in_, func, r, w, **kw):
            P.op("act", lambda e: e.activation(out=out_, in_=in_, func=func, **kw), reads=r, writes=w)

        def tt(eng, out_, in0, in1, op, r, w):
            P.op(eng, lambda e: e.tensor_tensor(out=out_, in0=in0, in1=in1, op=op), reads=r, writes=w)

        def ts(eng, out_, in0, s1, s2, op0, op1, r, w):
            if s2 is None:
                P.op(eng, lambda e: e.tensor_scalar(out=out_, in0=in0, scalar1=s1, scalar2=None, op0=op0),
                     reads=r, writes=w)
            else:
                P.op(eng, lambda e: e.tensor_scalar(out=out_, in0=in0, scalar1=s1, scalar2=s2, op0=op0, op1=op1),
                     reads=r, writes=w)

        def stt(eng, out_, in0, scalar, in1, op0, op1, r, w):
            P.op(eng, lambda e: e.scalar_tensor_tensor(out=out_, in0=in0, scalar=scalar, in1=in1, op0=op0, op1=op1),
                 reads=r, writes=w)

        def cp(eng, out_, in_, r, w):
            if eng == "act":
                P.op("act", lambda e: e.activation(out=out_, in_=in_, func=AF.Copy), reads=r, writes=w)
            else:
                P.op(eng, lambda e: e.tensor_copy(out=out_, in_=in_), reads=r, writes=w)

        def mset(eng, ap, val, w):
            P.op(eng, lambda e: e.memset(ap, val), writes=w)

        def dma(eng, out_, in_, r, w, semkey=None, slow=False):
            if slow:
                return P.dma(eng, lambda e: e.dma_start(out=out_, in_=in_, allow_slow_non_contiguous=True),
                             reads=r, writes=w, semkey=semkey)
            return P.dma(eng, lambda e: e.dma_start(out=out_, in_=in_), reads=r, writes=w, semkey=semkey)

        def dump(name, ap, handles, shape, dt):
            d = nc.dram_tensor("dbg_" + name, shape, dt, kind="ExternalOutput").ap()
            dma("sp", d, ap, handles, [("dbg", name)], semkey="dbg")

        CAST1 = ("ffn1_w1", "ffn1_w3", "ffn1_w2", "w_in")

        def issue_casts(names, after=()):
            for nm in names:
                if "nocast" in dbg:
                    break
                R = PARAM_SHAPES[nm][0]
                for r0 in range(0, R, 128):
                    r1 = min(R, r0 + 128)
                    dma("pool", WC[nm][r0:r1, :], I[nm][r0:r1, :], list(after), [("cast", nm, r0)], semkey="cast")

        CAST2 = []
        for nm_ in WC:
            if nm_ not in CAST1:
                for r0_ in range(0, PARAM_SHAPES[nm_][0], 128):
                    CAST2.append((nm_, r0_, min(PARAM_SHAPES[nm_][0], r0_ + 128)))

        def issue_casts2(i):
            half = len(CAST2) // 2
            if i == -1:
                lst = CAST2[half:]
            elif i >= 2:
                per = (half + 5) // 6
                lst = CAST2[:half][(i - 2) * per:(i - 1) * per]
            else:
                lst = []
            for (nm, r0, r1) in lst:
                if "nocast" not in dbg:
                    dma("pool", WC[nm][r0:r1, :], I[nm][r0:r1, :], [], [("cast", nm, r0)], semkey="cast")

        identf = sb(es, "identf", [128, 128], F32)
        identb = sb(es, "identb", [128, 128], BF16)
        gains = sb(es, "gains", [128, 3, 8], F32)
        ybuf = sb(es, "ybuf", [128, 4, SEQ], BF16)

        mset("pool", identf[:], 1.0, ["identf"])
        P.op("pool", lambda e: e.affine_select(out=identf[:], in_=identf[:], pattern=[[-1, 128]],
                                               compare_op=ALU.is_equal, fill=0.0, base=0, channel_multiplier=1),
             reads=["identf"], writes=["identf"])
        cp("dve", identb[:], identf[:], ["identf"], ["identb"])
        for gi_, nm in enumerate(("ffn1_norm", "mix_norm", "ffn2_norm")):
            dma("sp", gains[:, gi_, :], I[nm].rearrange("(k p) -> p k", p=128), [], [("gains", gi_)],
                semkey="small", slow=True)

        mid = es.enter_context(ExitStack())
        s5in = sb(mid, "s5in", [128, 4, 8, NCH], BF16)
        W2c = sb(mid, "W2c", [128, 4, 8, 2, 128], BF16)
        W3c = sb(mid, "W3c", [128, 16, 8, 2, 32], BF16)
        BD = sb(mid, "BD", [128, 4, 8, 128], BF16)
        Apow = sb(mid, "Apow", [128, 9, 3, 16], F32)

        with ExitStack() as p0:
            def t16(name):
                return sb(p0, name, [128, 16], F32)

            def t1616(name):
                return sb(p0, name, [128, 16, 16], F32)

            lre, lim, ldt = t16("lre"), t16("lim"), t16("ldt")
            dtv, mag, th, thr, tmp, sn, cs = [t16(n) for n in ("dtv", "mag", "th", "thr", "tmp", "sn", "cs")]
            ar, ai, den, rden, nr, fr, fi, tq = [t16(n) for n in ("ar", "ai", "den", "rden", "nr", "fr", "fi", "tq")]
            bre, bim, cre, cim, bbre, bbim = [t1616(n) for n in ("bre", "bim", "cre", "cim", "bbre", "bbim")]
            mcre, mcim, t1, t2 = [t1616(n) for n in ("mcre", "mcim", "t1", "t2")]
            pw = sb(p0, "pw", [128, 9, 2, 16], F32)
            MPre = sb(p0, "MPre", [128, 16, 128], F32)
            MPim = sb(p0, "MPim", [128, 16, 128], F32)
            CPre = sb(p0, "CPre", [128, 16, 128], F32)
            CPnim = sb(p0, "CPnim", [128, 16, 128], F32)
            dvec = sb(p0, "dvec", [128, 4], F32)

            for gi in range(2):
                rows = slice(gi * 64, gi * 64 + 64)
                dma("sp", lre[rows, :], I["s5_lambda_re"].rearrange("(q g) p -> g p q", g=2)[gi], [], [("lre", gi)], semkey="small", slow=True)
                dma("sp", lim[rows, :], I["s5_lambda_im"].rearrange("(q g) p -> g p q", g=2)[gi], [], [("lim", gi)], semkey="small", slow=True)
                dma("sp", ldt[rows, :], I["s5_log_dt"].rearrange("(q g) -> g q", g=2)[gi:gi + 1, :].to_broadcast([64, 16]),
                    [], [("ldt", gi)], semkey="small", slow=True)
                dma("sp", bre[rows], I["s5_b_re"].rearrange("(q g) p h -> g p q h", g=2)[gi], [], [("bre", gi)], semkey="small", slow=True)
                dma("sp", bim[rows], I["s5_b_im"].rearrange("(q g) p h -> g p q h", g=2)[gi], [], [("bim", gi)], semkey="small", slow=True)
                for q_ in range(16):
                    dma("sp", cre[rows, q_, :], I["s5_c_re"][2 * q_ + gi].rearrange("h p -> p h"), [], [("cre", gi, q_)], semkey="small", slow=True)
                    dma("act", cim[rows, q_, :], I["s5_c_im"][2 * q_ + gi].rearrange("h p -> p h"), [], [("cim", gi, q_)], semkey="small2", slow=True)
            dma("sp", dvec[:], I["s5_d"].rearrange("(t g) h -> (g h) t", t=4), [], ["dvec"], semkey="small", slow=True)
            L = lambda n: [(n, 0), (n, 1)]
            if stop_after == "prep_loads":
                P.emit_phase()
                return nc
            small_h = ([(n_, g_) for n_ in ("lre", "lim", "ldt", "bre", "bim", "cre", "cim") for g_ in range(2)]
                       + [(n_, g_, q_) for n_ in ("cre", "cim") for g_ in range(2) for q_ in range(16)]
                       + ["dvec"] + [("gains", g_) for g_ in range(3)])
            sync0 = sb(p0, "sync0", [128, 1], F32)
            P.op("dve", lambda e: e.memset(sync0[:], 0.0), reads=small_h, writes=small_h + ["sync0"])
            issue_casts(CAST1, after=["sync0"])
            ts("dve", tmp[:], ldt[:], 1.0, None, ALU.mult, None, L("ldt"), ["tmp"])
            act(dtv[:], tmp[:], AF.Exp, ["tmp"], ["dtv"])
            tt("dve", tmp[:], lre[:], dtv[:], ALU.mult, L("lre") + ["dtv"], ["tmp"])
            act(mag[:], tmp[:], AF.Exp, ["tmp"], ["mag"])
            tt("dve", th[:], lim[:], dtv[:], ALU.mult, L("lim") + ["dtv"], ["th"])

            def range_reduce(dst, src_shift):
                ts("dve", dst[:], th[:], src_shift, None, ALU.add, None, ["th"], ["rr"])
                ts("dve", tq[:], th[:], src_shift, None, ALU.add, None, ["th"], ["tq"])
                for m in range(1, 10):
                    ts("dve", tmp[:], tq[:], (2 * m - 1) * PI, -2.0 * PI, ALU.is_ge, ALU.mult, ["tq"], ["tmp"])
                    tt("dve", dst[:], dst[:], tmp[:], ALU.add, ["rr", "tmp"], ["rr"])

            range_reduce(thr, 0.0)
            act(sn[:], thr[:], AF.Sin, ["rr"], ["sn"])
            range_reduce(thr, PI / 2)
            act(cs[:], thr[:], AF.Sin, ["rr"], ["cs"])
            tt("dve", ar[:], mag[:], cs[:], ALU.mult, ["mag", "cs"], ["ar"])
            tt("dve", ai[:], mag[:], sn[:], ALU.mult, ["mag", "sn"], ["ai"])
            tt("dve", den[:], lre[:], lre[:], ALU.mult, L("lre"), ["den"])
            tt("dve", tmp[:], lim[:], lim[:], ALU.mult, L("lim"), ["tmp"])
            tt("dve", den[:], den[:], tmp[:], ALU.add, ["den", "tmp"], ["den"])
            P.op("dve", lambda e: e.reciprocal(out=rden[:], in_=den[:]), reads=["den"], writes=["rden"])
            ts("dve", nr[:], ar[:], -1.0, None, ALU.add, None, ["ar"], ["nr"])
            tt("dve", fr[:], nr[:], lre[:], ALU.mult, ["nr"] + L("lre"), ["fr"])
            tt("dve", tmp[:], ai[:], lim[:], ALU.mult, ["ai"] + L("lim"), ["tmp"])
            tt("dve", fr[:], fr[:], tmp[:], ALU.add, ["fr", "tmp"], ["fr"])
            tt("dve", fr[:], fr[:], rden[:], ALU.mult, ["fr", "rden"], ["fr"])
            tt("dve", fi[:], ai[:], lre[:], ALU.mult, ["ai"] + L("lre"), ["fi"])
            tt("dve", tmp[:], nr[:], lim[:], ALU.mult, ["nr"] + L("lim"), ["tmp"])
            tt("dve", fi[:], fi[:], tmp[:], ALU.subtract, ["fi", "tmp"], ["fi"])
            tt("dve", fi[:], fi[:], rden[:], ALU.mult, ["fi", "rden"], ["fi"])

            def bc16(a):
                return a.unsqueeze(2).to_broadcast([128, 16, 16])

            def cmul(ore, oim, are_, aim_, bre_, bim_, ra, rb, wo):
                tt("dve", t1[:], bre_, are_, ALU.mult, ra + rb, ["t1"])
                tt("dve", t2[:], bim_, aim_, ALU.mult, ra + rb, ["t2"])
                tt("dve", ore, t1[:], t2[:], ALU.subtract, ["t1", "t2"], [wo + "re"])
                tt("dve", t1[:], bim_, are_, ALU.mult, ra + rb, ["t1"])
                tt("dve", t2[:], bre_, aim_, ALU.mult, ra + rb, ["t2"])
                tt("dve", oim, t1[:], t2[:], ALU.add, ["t1", "t2"], [wo + "im"])

            cmul(bbre[:], bbim[:], bc16(fr[:]), bc16(fi[:]), bre[:], bim[:], ["fr", "fi"], L("bre") + L("bim"), "bb")
            mset("dve", pw[:, 0, 0, :], 1.0, [("pw", 0)])
            mset("dve", pw[:, 0, 1, :], 0.0, [("pw", 0)])
            for j in range(1, 9):
                pr, pi_ = pw[:, j - 1, 0, :], pw[:, j - 1, 1, :]
                tt("dve", tmp[:], pr, ar[:], ALU.mult, [("pw", j - 1), "ar"], ["tmp"])
                tt("dve", tq[:], pi_, ai[:], ALU.mult, [("pw", j - 1), "ai"], ["tq"])
                tt("dve", pw[:, j, 0, :], tmp[:], tq[:], ALU.subtract, ["tmp", "tq"], [("pw", j)])
                tt("dve", tmp[:], pr, ai[:], ALU.mult, [("pw", j - 1), "ai"], ["tmp"])
                tt("dve", tq[:], pi_, ar[:], ALU.mult, [("pw", j - 1), "ar"], ["tq"])
                tt("dve", pw[:, j, 1, :], tmp[:], tq[:], ALU.add, ["tmp", "tq"], [("pw", j)])
            cp("dve", Apow[:, 0, 0, :], cs[:], ["cs"], [("Apow", 0)])
            cp("dve", Apow[:, 0, 1, :], sn[:], ["sn"], [("Apow", 0)])
            cp("dve", Apow[:, 0, 2, :], mag[:], ["mag"], [("Apow", 0)])

            if stop_after == "prep_elem":
                for nm_, t_, hs_ in (("ldt", ldt, L("ldt")), ("lre", lre, L("lre")), ("lim", lim, L("lim")), ("dtv", dtv, ["dtv"]), ("mag", mag, ["mag"]), ("th", th, ["th"]),
                                     ("sn", sn, ["sn"]), ("cs", cs, ["cs"]), ("ar", ar, ["ar"]), ("ai", ai, ["ai"]), ("fr", fr, ["fr"]), ("fi", fi, ["fi"])):
                    dump(nm_, t_[:], hs_, [128, 16], F32)
                dump("pw", pw[:], [("pw", j) for j in range(9)], [128, 9, 2, 16], F32)
                dump("bbre", bbre[:], ["bbre"], [128, 16, 16], F32)
                dump("Apow", Apow[:], [("Apow", l) for l in range(9)] + [("ApowN", l) for l in range(9)], [128, 9, 3, 16], F32)
                P.emit_phase()
                return nc
            mset("pool", MPre[:], 0.0, ["MPre"])
            mset("pool", MPim[:], 0.0, ["MPim"])
            mset("pool", CPre[:], 0.0, ["CPre"])
            mset("pool", CPnim[:], 0.0, ["CPnim"])
            mset("pool", W3c[:], 0.0, ["W3c"])

            def scatter(dstpad, src, rs, w, neg=False, eng="pool"):
                for gi in range(2):
                    rows = slice(gi * 64, gi * 64 + 64)
                    for ql in range(4):
                        c0 = 32 * ql + 16 * gi
                        o_ = dstpad[rows, ql::4, c0:c0 + 16]
                        i_ = src[rows, ql::4, :]
                        if neg:
                            ts(eng, o_, i_, -1.0, None, ALU.mult, None, rs, [w])
                        else:
                            cp(eng, o_, i_, rs, [w])

            scatter(CPre, cre, L("cre"), "CPre")
            scatter(CPnim, cim, L("cim"), "CPnim", neg=True)

            for t in range(8):
                cmul(mcre[:], mcim[:], bc16(pw[:, t + 1, 0, :]), bc16(pw[:, t + 1, 1, :]), cre[:], cim[:],
                     [("pw", t + 1)], L("cre") + L("cim"), "mc")
                for gi in range(2):
                    rows = slice(gi * 64, gi * 64 + 64)
                    cp("pool", W3c[rows, :, t, 0, 16 * gi:16 * gi + 16], mcre[rows], ["mcre"], ["W3c"])
                    ts("pool", W3c[rows, :, t, 1, 16 * gi:16 * gi + 16], mcim[rows], -1.0, None, ALU.mult, None,
                       ["mcim"], ["W3c"])

            if stop_after == "prep_pad":
                dump("W3c", W3c[:], ["W3c"], [128, 16, 8, 2, 32], BF16)
                P.emit_phase()
                return nc
            for j in range(8):
                cmul(mcre[:], mcim[:], bc16(pw[:, j, 0, :]), bc16(pw[:, j, 1, :]), bbre[:], bbim[:],
                     [("pw", j)], ["bbre", "bbim"], "mc")
                scatter(MPre, mcre, ["mcre"], "MPre", eng="pool")
                scatter(MPim, mcim, ["mcim"], "MPim", eng="dve")
                s = 7 - j
                for T in range(4):
                    if "noK" in dbg:
                        break
                    bk, bh = nb()
                    n = 0
                    for ql in range(4):
                        q = 4 * T + ql
                        for (mp, cpd, hm, hc) in ((MPre, CPre, "MPre", "CPre"), (MPim, CPnim, "MPim", "CPnim")):
                            mm(bk[:, 0:128], mp[:, q, :], cpd[:, q, :], n == 0, n == 7, [hm, hc], [bh])
                            n += 1
                    if j == 0:
                        stt("dve", BD[:, T, 0, :], identf[:], dvec[:, T:T + 1], bk[:, 0:128], ALU.mult, ALU.add,
                            [bh, "identf", "dvec"], [("BD", T, 0)])
                    else:
                        cp("dve", BD[:, T, j, :], bk[:, 0:128], [bh], [("BD", T, j)])
                    if "noW2" in dbg:
                        continue
                    bk2, bh2 = nb()
                    for pi2, (mp, hm) in enumerate(((MPre, "MPre"), (MPim, "MPim"))):
                        for ql in range(4):
                            q = 4 * T + ql
                            mm(bk2[:, pi2 * 128:(pi2 + 1) * 128], mp[:, q, :], identf[:], ql == 0, ql == 3,
                               [hm, "identf"], [bh2])
                    cp("act", W2c[:, T, s, :, :], bk2[:, 0:256].rearrange("p (a b) -> p a b", a=2), [bh2], [("W2c", T, s)])

            if "prep" in dbg:
                dump("BD", BD[:], [("BD", T, j) for T in range(4) for j in range(8)], [128, 4, 8, 128], BF16)
                dump("W2c", W2c[:], [("W2c", T, s) for T in range(4) for s in range(8)], [128, 4, 8, 2, 128], BF16)
                dump("W3c", W3c[:], ["W3c"], [128, 16, 8, 2, 32], BF16)
                dump("Apow", Apow[:], [("Apow", l) for l in range(9)] + [("ApowN", l) for l in range(9)], [128, 9, 3, 16], F32)
            P.emit_phase()
        if stop_after == "prep":
            return nc
        build_rest(nc, P, I, WC, out, h_scr, uT_scr, es, mid, sb, banks, nb, mm, act, tt, ts, stt, cp, mset, dma, dump,
                   identf, identb, gains, ybuf, s5in, W2c, W3c, BD, Apow, dbg, stop_after, ntiles,
                   issue_casts2)
    return nc


class WStream:
    def __init__(self, slots, specs, dma, pf=3):
        self.slots, self.specs, self.dma, self.pf = slots, specs, dma, pf
        self.issued = 0
        self.pos = 0

    def _issue(self, i):
        si = i % len(self.slots)
        pairs = self.specs[i](self.slots[si])
        assert len(pairs) == 2
        import os as _os
        if _os.environ.get("WS_NODMA") and i >= len(self.slots):
            return
        for pi, (o, a) in enumerate(pairs):
            self.dma("sp", o, a, [], [("ring", si, pi)], semkey=("ring", si, pi))

    def next(self, n=1):
        lim = self.pos + len(self.slots) - 1
        while self.issued < min(len(self.specs), lim):
            self._issue(self.issued)
            self.issued += 1
        res = []
        for _ in range(n):
            si = self.pos % len(self.slots)
            self.pos += 1
            res.append((self.slots[si], [("ring", si, 0), ("ring", si, 1)]))
        return res[0] if n == 1 else res


def build_rest(nc, P, I, WC, out, h_scr, uT_scr, es, mid, sb, banks, nb, mm, act, tt, ts, stt, cp, mset, dma, dump,
               identf, identb, gains, ybuf, s5in, W2c, W3c, BD, Apow, dbg, stop_after, ntiles, issue_casts2):
    kp = lambda a: a.rearrange("(k p) c -> p k c", p=128)

    def spec_w13(w1c, w3c, g):
        def f(slot):
            return [(slot[:, 0:2048].rearrange("p (k c) -> p k c", k=8), kp(w1c)[:, :, g * 256:(g + 1) * 256]),
                    (slot[:, 2048:4096].rearrange("p (k c) -> p k c", k=8), kp(w3c)[:, :, g * 256:(g + 1) * 256])]
        return f

    def spec_w2(w2c, g2):
        nf = 4 if g2 < 5 else 2
        hlf = nf // 2

        def f(slot):
            v = slot.rearrange("p (f c) -> p f c", f=4)
            src = w2c[g2 * 512:g2 * 512 + nf * 128, :].rearrange("(f p) c -> p f c", p=128)
            return [(v[:, 0:hlf, :], src[:, 0:hlf, :]), (v[:, hlf:nf, :], src[:, hlf:nf, :])]
        return f

    def spec_cols(wc, c0, nk):
        def f(slot):
            v = slot[:, 0:nk * 512].rearrange("p (k c) -> p k c", k=nk)
            src = kp(wc)[:, :, c0:c0 + 512]
            h2 = nk // 2
            return [(v[:, 0:h2, :], src[:, 0:h2, :]), (v[:, h2:nk, :], src[:, h2:nk, :])]
        return f

    def spec_rows(wc, k0):
        def f(slot):
            v = slot.rearrange("p (k c) -> p k c", k=4)
            src = kp(wc)[:, k0:k0 + 4, :]
            return [(v[:, 0:2, :], src[:, 0:2, :]), (v[:, 2:4, :], src[:, 2:4, :])]
        return f

    def ffn_specs(pre):
        return ([spec_w13(WC[pre + "_w1"], WC[pre + "_w3"], g) for g in range(11)] +
                [spec_w2(WC[pre + "_w2"], g2) for g2 in range(6)] * 2)

    def rms_stats(B, gidx):
        h, xn, ss, rstd = B["h"], B["xn"], B["ss"], B["rstd"]
        HH = B["hh"]
        mset("dve", ss[:], 0.0, ["ss"])
        for sub in range(4):
            act(xn[:, sub, :], h[:, sub, :], AF.Square, [HH(sub), "ss"], [("xn", sub), ("ssv", sub)],
                accum_out=ss[:, sub:sub + 1])
        ts("dve", rstd[:], ss[:], 1.0 / D, EPS, ALU.mult, ALU.add, [("ssv", s_) for s_ in range(4)], ["rstd"])
        act(rstd[:], rstd[:], AF.Sqrt, ["rstd"], ["rstd"])
        P.op("dve", lambda e: e.reciprocal(out=rstd[:], in_=rstd[:]), reads=["rstd"], writes=["rstd"])
        for sub in range(4):
            act(xn[:, sub, :], h[:, sub, :], AF.Copy, [HH(sub), "rstd", ("xn", sub)], [("xn", sub)],
                scale=rstd[:, sub:sub + 1])

    def rms_transpose(B, gidx):
        xn, uT = B["xn"], B["uT"]
        for k in range(8):
            bk, bh = nb()
            bkb = bk[:].bitcast(BF16)
            for sub in range(4):
                P.op("pe", lambda e, k=k, sub=sub, bkb=bkb: e.transpose(
                    out=bkb[:, sub * 128:(sub + 1) * 128], in_=xn[:, sub, k * 128:(k + 1) * 128], identity=identb[:]),
                    reads=[("xn", sub), "identb"], writes=[bh])
            if k % 2 == 0:
                ts("dve", uT[:, k, :], bkb[:, 0:512], gains[:, gidx, k:k + 1], None, ALU.mult, None,
                   [bh, ("gains", gidx)], [(B["uTh"], k)])
            else:
                act(uT[:, k, :], bkb[:, 0:512], AF.Copy, [bh, ("gains", gidx)], [(B["uTh"], k)],
                    scale=gains[:, gidx, k:k + 1])

    def rmsnorm_T(B, gidx):
        rms_stats(B, gidx)
        rms_transpose(B, gidx)

    UT = [("uT", k) for k in range(8)]

    def ffn_h13(B, ws, g_lo, g_hi, uT=None):
        gT, sg = B["gT"], B["sg"]
        uT = B["uT"] if uT is None else uT
        for g in range(g_lo, g_hi):
            slot, sh = ws.next()
            w1v = slot[:, 0:2048].rearrange("p (k c) -> p k c", k=8)
            w3v = slot[:, 2048:4096].rearrange("p (k c) -> p k c", k=8)
            for fi in range(2):
                f = 2 * g + fi
                bA, hA = nb()
                for k in range(8):
                    mm(bA[:], w1v[:, k, fi * 128:(fi + 1) * 128], uT[:, k, :], k == 0, k == 7, sh + [(B["uTh"], k)], [hA])
                bB, hB = nb()
                for k in range(8):
                    mm(bB[:], w3v[:, k, fi * 128:(fi + 1) * 128], uT[:, k, :], k == 0, k == 7, sh + [(B["uTh"], k)], [hB])
                act(sg[:, f % 2, :], bA[:], AF.Silu, [hA], [("sg", f % 2)])
                tt("dve", gT[:, f, :], sg[:, f % 2, :], bB[:], ALU.mult, [("sg", f % 2), hB], [("gT", f)])

    def ffn_w2(B, ws):
        h, gT = B["h"], B["gT"]
        HH = B["hh"]
        for rnd in range(2):
            subs = (2 * rnd, 2 * rnd + 1)
            accs = {(sub, half): nb() for sub in subs for half in range(2)}
            for g2 in range(6):
                slot, sh = ws.next()
                w2v = slot.rearrange("p (f c) -> p f c", f=4)
                nf = 4 if g2 < 5 else 2
                for fi in range(nf):
                    f = 4 * g2 + fi
                    for sub in subs:
                        for half in range(2):
                            bk, bh = accs[(sub, half)]
                            mm(bk[:], gT[:, f, sub * 128:(sub + 1) * 128], w2v[:, fi, half * 512:(half + 1) * 512],
                               f == 0, f == NF - 1, sh + [("gT", f)], [bh])
            for sub in subs:
                for half in range(2):
                    bk, bh = accs[(sub, half)]
                    hv = h[:, sub, half * 512:(half + 1) * 512]
                    stt("dve", hv, bk[:], 0.5, hv, ALU.mult, ALU.add, [bh, HH(sub)], [HH(sub)])

    def ffn(B, ws, after_h13=None):
        ffn_h13(B, ws, 0, 11)
        if after_h13 is not None:
            after_h13()
        ffn_w2(B, ws)

    xv = I["x"].rearrange("(i s p) d -> i s p d", s=4, p=128)
    hsv = h_scr.rearrange("(i s p) d -> i s p d", s=4, p=128)
    outv = out.rearrange("(i s p) d -> i s p d", s=4, p=128)

    with ExitStack() as p1:
        B = {}
        hbuf = [sb(p1, "h1a", [128, 4, D], F32), sb(p1, "h1b", [128, 4, D], F32)]
        B["xn"] = sb(p1, "xn1", [128, 4, D], BF16)
        uTb = [sb(p1, "uT1a", [128, 8, NT], BF16), sb(p1, "uT1b", [128, 8, NT], BF16)]
        B["ss"] = sb(p1, "ss1", [128, 4], F32)
        B["rstd"] = sb(p1, "rstd1", [128, 4], F32)
        B["sg"] = sb(p1, "sg1", [128, 2, NT], F32)
        B["gT"] = ybuf[:].rearrange("p a b -> p (a b)")[:, 0:NF * NT].rearrange("p (f t) -> p f t", f=NF)
        ring = [sb(p1, "ring1_%d" % i, [128, 4096], BF16) for i in range(4)]
        wins5 = sb(p1, "wins5", [128, 8, 512], BF16)
        dma("sp", wins5[:, 0:4, :], kp(WC["w_in"])[:, 0:4, 0:512], [], ["wins5a"])
        dma("sp", wins5[:, 4:8, :], kp(WC["w_in"])[:, 4:8, 0:512], [], ["wins5b"])
        specs = []
        for i in range(ntiles):
            specs += ffn_specs("ffn1")
        ws = WStream([r[:] for r in ring], specs, dma)
        print("SBUF remaining pass1:", nc.sbuf_bytes_remaining)
        def load_x(i):
            for sub in range(4):
                dma("pool", hbuf[i % 2][:, sub, :], xv[i, sub], [], [("h", i % 2, sub)])

        def setB(i):
            par = i % 2
            B["h"] = hbuf[par]
            B["hh"] = lambda sub, par=par: ("h", par, sub)
            B["uT"] = uTb[par]
            B["uTh"] = ("uT", par)

        def mix_stats(i):
            setB(i)
            rms_stats(B, 1)

        def mix_and_proj(i):
            setB(i)
            par = i % 2
            uT = uTb[par]
            rms_transpose(B, 1)
            for half in range(2):
                dma("pool", uT_scr[i][:, half * 2048:(half + 1) * 2048].rearrange("p (k t) -> p k t", k=4),
                    uT[:, half * 4:(half + 1) * 4, :], [(("uT", par), k) for k in range(half * 4, half * 4 + 4)],
                    [("utscr", half)])
            for m in range(4):
                bk, bh = nb()
                for k in range(8):
                    mm(bk[:], wins5[:, k, m * 128:(m + 1) * 128], uT[:, k, :], k == 0, k == 7,
                       ["wins5a", "wins5b", (("uT", par), k)], [bh])
                o_ = s5in[:, m, :, i * 64:(i + 1) * 64]
                i_ = bk[:].rearrange("p (c s) -> p s c", s=8)
                if m % 2 == 0:
                    cp("dve", o_, i_, [bh], [("s5in", m)])
                else:
                    cp("act", o_, i_, [bh], [("s5in", m)])

        load_x(0)
        if ntiles > 1:
            load_x(1)
        setB(0)
        rmsnorm_T(B, 0)
        for i in range(ntiles):
            issue_casts2(i)
            if i > 0:
                mix_stats(i - 1)
            setB(i)
            ffn_h13(B, ws, 0, 4)
            if i > 0:
                mix_and_proj(i - 1)
                if i + 1 < ntiles:
                    load_x(i + 1)
            if i + 1 < ntiles:
                setB(i + 1)
                rms_stats(B, 0)
            setB(i)
            ffn_h13(B, ws, 4, 8)
            if i + 1 < ntiles:
                setB(i + 1)
                rms_transpose(B, 0)
            setB(i)
            ffn_h13(B, ws, 8, 11)
            ffn_w2(B, ws)
            for sub in range(4):
                dma("pool", hsv[i, sub], hbuf[i % 2][:, sub, :], [("h", i % 2, sub)], [("hscr", sub)])
        mix_stats(ntiles - 1)
        mix_and_proj(ntiles - 1)
        if "p1" in dbg:
            dump("s5in", s5in[:], [("s5in", m) for m in range(4)], [128, 4, 8, NCH], BF16)
        P.emit_phase()
    if stop_after == "p1":
        return

    with ExitStack() as p2:
        Eh = sb(p2, "Eh", [128, 9, 2, 16], F32)
        r8 = sb(p2, "r8", [128, 16], F32)
        w16 = [sb(p2, "w16_%d" % i_, [128, 16], F32) for i_ in range(6)]
        Etab = [sb(p2, "Etab%d" % i_, [128, 4, NCH], F32) for i_ in range(2)]
        et = [sb(p2, "et%d" % i_, [128, 4, NCH // 2], F32) for i_ in range(2)]
        SC = [[sb(p2, "sc%d_%d" % (a_, b_), [128, NCH], F32) for b_ in range(6)] for a_ in range(2)]
        X0 = [sb(p2, "X0_%d" % t_, [128, 4, 2, NCH], BF16) for t_ in range(2)]
        mset("pool", X0[0][:], 0.0, [("X0", 0, ql, pt) for ql in range(4) for pt in range(2)])
        mset("pool", X0[1][:], 0.0, [("X0", 1, ql, pt) for ql in range(4) for pt in range(2)])
        yv = ybuf[:].rearrange("p a (c t) -> p a c t", t=8)
        print("SBUF remaining s5:", nc.sbuf_bytes_remaining)
        issue_casts2(-1)

        cre_, cim_, m2_, t1_, t2_, rm_ = [w[:] for w in w16]

        def normalize(re, im, hre):
            tt("dve", m2_, re, re, ALU.mult, hre, ["w_m2"])
            tt("dve", t1_, im, im, ALU.mult, hre, ["w_t1"])
            tt("dve", m2_, m2_, t1_, ALU.add, ["w_m2", "w_t1"], ["w_m2"])
            act(m2_, m2_, AF.Sqrt, ["w_m2"], ["w_m2"])
            P.op("dve", lambda e: e.reciprocal(out=rm_, in_=m2_), reads=["w_m2"], writes=["w_rm"])
            tt("dve", re, re, rm_, ALU.mult, hre + ["w_rm"], hre)
            tt("dve", im, im, rm_, ALU.mult, hre + ["w_rm"], hre)

        def square(ore, oim, re, im, hin, hout):
            tt("dve", t1_, re, re, ALU.mult, hin, ["w_t1"])
            tt("dve", t2_, im, im, ALU.mult, hin, ["w_t2"])
            tt("dve", m2_, re, im, ALU.mult, hin, ["w_m2"])
            tt("dve", ore, t1_, t2_, ALU.subtract, ["w_t1", "w_t2"], hout)
            ts("dve", oim, m2_, 2.0, None, ALU.mult, None, ["w_m2"], hout)

        cp("dve", cre_, Apow[:, 0, 0, :], [], ["w_c"])
        cp("dve", cim_, Apow[:, 0, 1, :], [], ["w_c"])
        normalize(cre_, cim_, ["w_c"])
        for it in range(3):
            if it < 2:
                square(Eh[:, 8, 0, :], Eh[:, 8, 1, :], cre_, cim_, ["w_c"], ["w_d"])
                normalize(Eh[:, 8, 0, :], Eh[:, 8, 1, :], ["w_d"])
                cp("dve", cre_, Eh[:, 8, 0, :], ["w_d"], ["w_c"])
                cp("dve", cim_, Eh[:, 8, 1, :], ["w_d"], ["w_c"])
            else:
                square(Eh[:, 0, 0, :], Eh[:, 0, 1, :], cre_, cim_, ["w_c"], [("Eh", 0)])
                normalize(Eh[:, 0, 0, :], Eh[:, 0, 1, :], [("Eh", 0)])
        for l in range(1, 9):
            square(Eh[:, l, 0, :], Eh[:, l, 1, :], Eh[:, l - 1, 0, :], Eh[:, l - 1, 1, :], [("Eh", l - 1)], [("Eh", l)])
            normalize(Eh[:, l, 0, :], Eh[:, l, 1, :], [("Eh", l)])
        tt("dve", r8[:], Apow[:, 0, 2, :], Apow[:, 0, 2, :], ALU.mult, [], ["r8"])
        tt("dve", r8[:], r8[:], r8[:], ALU.mult, ["r8"], ["r8"])
        tt("dve", r8[:], r8[:], r8[:], ALU.mult, ["r8"], ["r8"])
        EHALL = [("Eh", l) for l in range(9)]

        def s5_part2(T):
            bl = {}
            for part in range(2):
                for ql in range(4):
                    bl[(ql, part)] = nb()
                for s in range(8):
                    for ql in range(4):
                        bk, bh = bl[(ql, part)]
                        rows = slice(32 * ql, 32 * ql + 32)
                        mm(bk[:], W2c[rows, T, s, part, :], s5in[rows, T, s, :], s == 0, s == 7,
                           [("s5in", T), "W2c"], [bh], tile_position=(32 * ql, 0))
            return bl

        def s5_tables(T):
            Ere, Eim = Etab
            mset("dve", Ere[:, :, 0:1], 1.0, ["Etab"])
            mset("dve", Eim[:, :, 0:1], 0.0, ["Etab"])
            for l in range(9):
                n = 1 << l
                ere = Eh[:, l, 0, 4 * T:4 * T + 4].unsqueeze(2).to_broadcast([128, 4, n])
                eim = Eh[:, l, 1, 4 * T:4 * T + 4].unsqueeze(2).to_broadcast([128, 4, n])
                a0, a1 = et[0][:, :, 0:n], et[1][:, :, 0:n]
                tt("dve", a0, Ere[:, :, 0:n], ere, ALU.mult, ["Etab", ("Eh", l)], ["et0"])
                tt("dve", a1, Eim[:, :, 0:n], eim, ALU.mult, ["Etab", ("Eh", l)], ["et1"])
                tt("dve", Ere[:, :, n:2 * n], a0, a1, ALU.subtract, ["et0", "et1"], ["Etab"])
                tt("dve", a0, Ere[:, :, 0:n], eim, ALU.mult, ["Etab", ("Eh", l)], ["et0"])
                tt("dve", a1, Eim[:, :, 0:n], ere, ALU.mult, ["Etab", ("Eh", l)], ["et1"])
                tt("dve", Eim[:, :, n:2 * n], a0, a1, ALU.add, ["et0", "et1"], ["Etab"])

        def s5_scan(T, bl):
            x0 = X0[T % 2]
            Ere, Eim = Etab
            seqs = []
            for ql in range(4):
                q = 4 * T + ql
                st = ql % 2
                t1, t2, zre, zim, wre, wim = [b[:] for b in SC[st]]
                H = lambda nm, st=st: ("sc", st, nm)
                (bre_, hre_), (bim_, him_) = bl[(ql, 0)], bl[(ql, 1)]
                er, ei = Ere[:, ql, :], Eim[:, ql, :]
                rb = r8[:, q:q + 1].to_broadcast([128, NCH])
                n1 = NCH - 1
                ops = [
                    ("tt", t1, bre_[:], er, ALU.mult, [hre_, "Etab"], [H("t1")]),
                    ("tt", t2, bim_[:], ei, ALU.mult, [him_, "Etab"], [H("t2")]),
                    ("tt", zre, t1, t2, ALU.add, [H("t1"), H("t2")], [H("zre")]),
                    ("tt", t1, bim_[:], er, ALU.mult, [him_, "Etab"], [H("t1")]),
                    ("tt", t2, bre_[:], ei, ALU.mult, [hre_, "Etab"], [H("t2")]),
                    ("tt", zim, t1, t2, ALU.subtract, [H("t1"), H("t2")], [H("zim")]),
                    ("scan", wre, rb, zre, [H("zre"), "r8"], [H("wre")]),
                    ("scan", wim, rb, zim, [H("zim"), "r8"], [H("wim")]),
                    ("tt", t1, wre, er, ALU.mult, [H("wre"), "Etab"], [H("t1")]),
                    ("tt", t2, wim, ei, ALU.mult, [H("wim"), "Etab"], [H("t2")]),
                    ("tt", x0[:, ql, 0, 1:NCH], t1[:, 0:n1], t2[:, 0:n1], ALU.subtract, [H("t1"), H("t2")], [("X0", T % 2, ql, 0)]),
                    ("tt", t1, wre, ei, ALU.mult, [H("wre"), "Etab"], [H("t1")]),
                    ("tt", t2, wim, er, ALU.mult, [H("wim"), "Etab"], [H("t2")]),
                    ("tt", x0[:, ql, 1, 1:NCH], t1[:, 0:n1], t2[:, 0:n1], ALU.add, [H("t1"), H("t2")], [("X0", T % 2, ql, 1)]),
                ]
                seqs.append(ops)
            for grp in ((0, 1), (2, 3)):
                for k in range(len(seqs[0])):
                    for ql in grp:
                        o = seqs[ql][k]
                        if o[0] == "tt":
                            tt("dve", o[1], o[2], o[3], o[4], o[5], o[6])
                        else:
                            P.op("dve", lambda e, o=o: e.tensor_tensor_scan(out=o[1], data0=o[2], data1=o[3], initial=0.0,
                                                                            op0=ALU.mult, op1=ALU.add), reads=o[4], writes=o[5])

        def s5_part13(T):
            x0 = X0[T % 2]
            for t in range(8):
                bk, bh = nb()
                for s in range(t + 1):
                    mm(bk[:], BD[:, T, t - s, :], s5in[:, T, s, :], s == 0, False, [("BD", T), ("s5in", T)], [bh])
                n = 0
                for part in range(2):
                    for ql in range(4):
                        q = 4 * T + ql
                        n += 1
                        mm(bk[32 * ql:32 * ql + 32, :], W3c[:, q, t, part, :], x0[:, ql, part, :], False, part == 1,
                           ["W3c", ("X0", T % 2, ql, part)], [bh], tile_position=(0, 32 * ql))
                cp("act", yv[:, T, :, t], bk[:], [bh], [("y", T)])

        for T in range(4):
            bl = s5_part2(T)
            s5_tables(T)
            s5_scan(T, bl)
            if T > 0:
                s5_part13(T - 1)
        s5_part13(3)
        if "s5" in dbg:
            dump("y", ybuf[:], [("y", T) for T in range(4)], [128, 4, SEQ], BF16)
        P.emit_phase()
    if stop_after == "s5":
        return
    mid.close()
    build_pass2(nc, P, I, WC, out, h_scr, uT_scr, es, sb, banks, nb, mm, act, tt, ts, stt, cp, mset, dma, dump,
                identf, identb, gains, ybuf, dbg, stop_after, ntiles, WStream, rmsnorm_T, ffn, ffn_specs,
                spec_cols, spec_rows, kp, hsv, outv, UT)


def build_pass2(nc, P, I, WC, out, h_scr, uT_scr, es, sb, banks, nb, mm, act, tt, ts, stt, cp, mset, dma, dump,
                identf, identb, gains, ybuf, dbg, stop_after, ntiles, WStream, rmsnorm_T, ffn, ffn_specs,
                spec_cols, spec_rows, kp, hsv, outv, UT):
    with ExitStack() as p3:
        B = {}
        hbuf = [sb(p3, "h2a", [128, 4, D], F32), sb(p3, "h2b", [128, 4, D], F32)]
        B["xn"] = sb(p3, "xn2", [128, 4, D], BF16)
        B["uT"] = sb(p3, "uT2", [128, 8, NT], BF16)
        B["ss"] = sb(p3, "ss2", [128, 4], F32)
        B["rstd"] = sb(p3, "rstd2", [128, 4], F32)
        B["sg"] = sb(p3, "sg2", [128, 2, NT], F32)
        big = sb(p3, "big2", [128, 24 * NT], BF16)
        B["gT"] = big[:, 0:NF * NT].rearrange("p (f t) -> p f t", f=NF)
        uT = B["uT"]
        ring = [sb(p3, "ring2_%d" % i, [128, 4096], BF16) for i in range(3)]
        gfin = sb(p3, "gfin", [128, D], F32)
        wal = sb(p3, "wal", [128, 8, 16], BF16)
        wglu = sb(p3, "wglu", [128, 4, 512], BF16)
        bglu = sb(p3, "bglu", [128, 4], F32)
        gnorm = sb(p3, "gnorm", [128, 4], F32)
        wup = sb(p3, "wup", [33, 256], F32)
        Tri = sb(p3, "Tri", [128, 128], F32)
        Um = sb(p3, "Um", [128, 128], F32)
        MaskT = sb(p3, "MaskT", [128, 128], F32)
        onesb = sb(p3, "onesb", [128, 128], BF16)
        qT = sb(p3, "qT", [128, 2, NT], F32)
        kT = sb(p3, "kT", [128, 2, NT], F32)
        ktok = sb(p3, "ktok", [128, 4, 256], F32)
        vtok = sb(p3, "vtok", [128, 4, 512], BF16)
        rT = sb(p3, "rT", [128, 4, NT], BF16)
        alT = sb(p3, "alT", [33, NT], F32)
        sgt = big[:, 0:16 * NT].rearrange("p (f t) -> p f t", f=16)
        yglaT = sb(p3, "yglaT", [128, 4, NT], BF16)
        ys5T = sb(p3, "ys5T", [128, 4, NT], BF16)
        zT = sb(p3, "zT", [128, 4, NT], BF16)
        mT = big[:, 16 * NT:24 * NT].rearrange("p (f t) -> p f t", f=8)
        mtmp = B["sg"]
        ex = sb(p3, "ex", [128, 256], F32)
        nl = sb(p3, "nl", [128, 256], F32)
        Eq = sb(p3, "Eq", [128, 2, 128], F32)
        Ek = sb(p3, "Ek", [128, 2, 128], F32)
        Eke = sb(p3, "Eke", [128, 256], F32)
        qs = sb(p3, "qs", [128, 2, 128], BF16)
        ksz = sb(p3, "ksz", [128, 4, 128], BF16)
        kend = sb(p3, "kend", [128, 256], BF16)
        PT = sb(p3, "PT", [128, 4, 128], BF16)
        Sf = sb(p3, "Sf", [128, 4, 128], F32)
        Sbz = sb(p3, "Sbz", [128, 4, 128], BF16)
        osq = sb(p3, "osq", [128, 512], BF16)
        rsd = sb(p3, "rsd", [128, 512], F32)
        ot = sb(p3, "ot", [128, 512], F32)

        print("SBUF remaining pass2:", nc.sbuf_bytes_remaining)
        dma("sp", gfin[:], I["final_norm"].rearrange("(o d) -> o d", o=1).to_broadcast([128, D]), [], ["c0"], semkey="small", slow=True)
        dma("sp", wal[:], kp(WC["w_in"])[:, :, C_AL:C_AL + 16], [], ["c1"], semkey="small", slow=True)
        dma("sp", wglu[:], kp(WC["s5_glu_w"]), [], ["c2"], semkey="small")
        dma("sp", bglu[:], I["s5_glu_b"].rearrange("(m p) -> p m", p=128), [], ["c3"], semkey="small", slow=True)
        dma("sp", gnorm[:], I["gla_out_norm"].rearrange("(m p) -> p m", p=128), [], ["c4"], semkey="small", slow=True)
        mset("pool", wup[:], 0.0, ["wup"])
        mset("pool", alT[:], 0.0, ["alT"])
        mset("pool", alT[32:33, :], 1.0, ["alT"])
        mset("pool", Sf[:], 0.0, ["Sf"])
        mset("pool", Sbz[:], 0.0, ["Sbz"])
        mset("pool", ksz[:], 0.0, ["ksz"])
        mset("pool", onesb[:], 1.0 / 128, ["onesb"])
        P.emit_phase()
        dma("sp", wup[0:16, :], I["gla_a_up_w"], [], ["c5"], semkey="small")
        dma("sp", wup[32:33, :], I["gla_a_up_b"].rearrange("(o d) -> o d", o=1), [], ["c6"], semkey="small")
        mset("pool", Tri[:], -1.0 / 16, ["Tri"])
        P.op("pool", lambda e: e.affine_select(out=Tri[:], in_=Tri[:], pattern=[[1, 128]], compare_op=ALU.is_ge,
                                               fill=0.0, base=0, channel_multiplier=-1), reads=["Tri"], writes=["Tri"])
        mset("pool", Um[:], -1.0 / 16, ["Um"])
        P.op("pool", lambda e: e.affine_select(out=Um[:], in_=Um[:], pattern=[[-1, 128]], compare_op=ALU.is_ge,
                                               fill=0.0, base=-1, channel_multiplier=1), reads=["Um"], writes=["Um"])
        mset("pool", MaskT[:], 1.0, ["MaskT"])
        P.op("pool", lambda e: e.affine_select(out=MaskT[:], in_=MaskT[:], pattern=[[1, 128]], compare_op=ALU.is_ge,
                                               fill=0.0, base=0, channel_multiplier=-1), reads=["MaskT"], writes=["MaskT"])
        P.emit_phase()

        win = WC["w_in"]
        specs = []
        for i in range(ntiles):
            specs += [spec_cols(win, C_Q, 8), spec_cols(win, C_V, 8), spec_cols(win, C_R, 8),
                      spec_cols(win, C_GS, 8), spec_cols(win, C_GS + 512, 8),
                      spec_cols(win, C_GG, 8), spec_cols(win, C_GG + 512, 8),
                      spec_rows(WC["proj_s5"], 0), spec_rows(WC["proj_gla"], 0),
                      spec_rows(WC["w_out"], 0), spec_rows(WC["w_out"], 4)]
            specs += ffn_specs("ffn2")
        ws = WStream([r[:] for r in ring], specs, dma, pf=2)

        def proj_fm(slot, sh, m, nk=8):
            v = slot.rearrange("p (k c) -> p k c", k=nk)
            bk, bh = nb()
            for k in range(nk):
                mm(bk[:], v[:, k, m * 128:(m + 1) * 128], uT[:, k, :], k == 0, k == nk - 1, sh + [("uT", k)], [bh])
            return bk, bh

        def load_tile(i):
            for sub in range(4):
                dma("pool", hbuf[i % 2][:, sub, :], hsv[i, sub], [], [("h", i % 2, sub)])
            for half in range(2):
                dma("pool", uT[:, half * 4:(half + 1) * 4, :],
                    uT_scr[i][:, half * 2048:(half + 1) * 2048].rearrange("p (k t) -> p k t", k=4),
                    [], [("uT", k) for k in range(half * 4, half * 4 + 4)], semkey=("uTld", half))

        load_tile(0)
        for i in range(ntiles):
            t0 = i * NT
            h = hbuf[i % 2]
            B["h"] = h
            B["hh"] = lambda sub, par=i % 2: ("h", par, sub)
            B["uTh"] = "uT"
            HH = B["hh"]
            slot, sh = ws.next()
            v8 = slot.rearrange("p (k c) -> p k c", k=8)
            for m in range(2):
                bk, bh = proj_fm(slot, sh, m)
                cp("act", qT[:, m, :], bk[:], [bh], [("qT", m)])
            for m in range(2):
                bk, bh = proj_fm(slot, sh, 2 + m)
                cp("dve", kT[:, m, :], bk[:], [bh], [("kT", m)])
            for sub in range(4):
                bk, bh = nb()
                for k in range(8):
                    mm(bk[:, 0:256], uT[:, k, sub * 128:(sub + 1) * 128], v8[:, k, 256:512], k == 0, k == 7,
                       sh + [("uT", k)], [bh])
                cp("act" if sub % 2 else "dve", ktok[:, sub, :], bk[:, 0:256], [bh], [("ktok", sub)])
            slot, sh = ws.next()
            v8 = slot.rearrange("p (k c) -> p k c", k=8)
            for sub in range(4):
                bk, bh = nb()
                for k in range(8):
                    mm(bk[:], uT[:, k, sub * 128:(sub + 1) * 128], v8[:, k, :], k == 0, k == 7, sh + [("uT", k)], [bh])
                cp("act" if sub % 2 else "dve", vtok[:, sub, :], bk[:], [bh], [("vtok", sub)])
            bk, bh = nb()
            for k in range(8):
                mm(bk[0:16, :], wal[:, k, :], uT[:, k, :], k == 0, k == 7, [("uT", k)], [bh])
            cp("dve", alT[0:16, :], bk[0:16, :], [bh], ["alT"])
            fillers = []
            fstate = {}

            def r_filler(m):
                if m == 0:
                    fstate["slot"], fstate["sh"] = ws.next()
                bk, bh = proj_fm(fstate["slot"], fstate["sh"], m)
                act(rT[:, m, :], bk[:], AF.Silu, [bh], [("rT", m)])

            def g_filler(gidx, m):
                if m == 0:
                    fstate["slot"], fstate["sh"] = ws.next()
                bk, bh = proj_fm(fstate["slot"], fstate["sh"], m)
                act(sgt[:, gidx * 4 + m, :], bk[:], AF.Sigmoid, [bh], [("sgt", gidx * 4 + m)])

            for m in range(4):
                fillers.append(lambda m=m: r_filler(m))
            for gidx in range(4):
                for m in range(4):
                    fillers.append(lambda gidx=gidx, m=m: g_filler(gidx, m))

            for m in range(4):
                act(zT[:, m, :], ybuf[:, m, t0:t0 + NT], AF.Gelu_apprx_tanh, [], [("zT", m)])

            def glu_filler(mo):
                bk, bh = nb()
                for k in range(4):
                    mm(bk[:], wglu[:, k, mo * 128:(mo + 1) * 128], zT[:, k, :], k == 0, k == 3, [("zT", k)], [bh])
                act(mtmp[:, 0, :], bk[:], AF.Sigmoid, [bh], [("mtmp", 0)], bias=bglu[:, mo:mo + 1])
                tt("dve", ys5T[:, mo, :], zT[:, mo, :], mtmp[:, 0, :], ALU.mult, [("zT", mo), ("mtmp", 0)], [("ys5", mo)])

            def ps5_filler(dj):
                if dj == 0:
                    fstate["p5"], fstate["p5h"] = ws.next()
                p5 = fstate["p5"].rearrange("p (k c) -> p k c", k=4)
                bA, hA = nb()
                for k in range(4):
                    mm(bA[:], p5[:, k, dj * 128:(dj + 1) * 128], ys5T[:, k, :], k == 0, k == 3, fstate["p5h"] + [("ys5", k)], [hA])
                tt("dve", mT[:, dj, :], sgt[:, dj, :], bA[:], ALU.mult, [("sgt", dj), hA], [("mT", dj)])

            for mo in range(4):
                fillers.append(lambda mo=mo: glu_filler(mo))
            for dj in range(8):
                fillers.append(lambda dj=dj: ps5_filler(dj))

            def fill(n=1):
                for _ in range(n):
                    if fillers:
                        fillers.pop(0)()

            for j in range(0 if "nogla" in dbg else 4):
                cs_ = slice(j * 128, (j + 1) * 128)
                bk, bh = nb()
                mm(bk[:, 0:256], alT[0:33, cs_], wup[0:33, :], True, True, ["alT", "wup"], [bh])
                fill(2)
                act(ex[:], bk[:, 0:256], AF.Exp, [bh], ["ex"], scale=-1.0)
                act(nl[:], ex[:], AF.Ln, ["ex"], ["nl"], bias=1.0)
                bk, bh = nb()
                for hp in range(2):
                    mm(bk[:, hp * 128:(hp + 1) * 128], nl[:, hp * 128:(hp + 1) * 128], Tri[:], True, True, ["nl", "Tri"], [bh])
                mm(bk[:, 256:512], Um[:], nl[:], True, True, ["nl", "Um"], [bh])
                fill(2)
                bcT = bk[:, 0:256].rearrange("p (a b) -> p a b", a=2)
                act(Eq[:], bcT, AF.Exp, [bh], ["Eq"])
                act(Ek[:], bcT, AF.Exp, [bh], ["Ek"], scale=-1.0)
                act(Eke[:], bk[:, 256:512], AF.Exp, [bh], ["Eke"])
                for hp in range(2):
                    stt("dve", qs[:, hp, :], qT[:, hp, cs_], 0.125, Eq[:, hp, :], ALU.mult, ALU.mult,
                        [("qT", hp), "Eq"], [("qs", hp)])
                for hd in range(4):
                    hp, hi = hd // 2, hd % 2
                    rows = slice(hi * 64, hi * 64 + 64)
                    tt("pool", ksz[rows, hd, :], kT[rows, hp, cs_], Ek[rows, hp, :], ALU.mult, [("kT", hp), "Ek"], [("ksz", hd)])
                tt("pool", kend[:], ktok[:, j, :], Eke[:], ALU.mult, [("ktok", j), "Eke"], ["kend"])
                bk, bh = nb()
                for hd in range(4):
                    hp = hd // 2
                    mm(bk[:, hd * 128:(hd + 1) * 128], ksz[:, hd, :], qs[:, hp, :], True, True, [("ksz", hd), ("qs", hp)], [bh])
                fill(2)
                tt("dve", PT[:], bk[:].rearrange("p (a b) -> p a b", a=4),
                   MaskT[:].unsqueeze(1).to_broadcast([128, 4, 128]), ALU.mult, [bh, "MaskT"], ["PT"])
                bo, bho = nb()
                for hd in range(4):
                    hp = hd // 2
                    mm(bo[:, hd * 128:(hd + 1) * 128], vtok[:, j, hd * 128:(hd + 1) * 128], PT[:, hd, :], True, False,
                       [("vtok", j), "PT"], [bho])
                    mm(bo[:, hd * 128:(hd + 1) * 128], Sbz[:, hd, :], qs[:, hp, :], False, True, [("Sbz", hd), ("qs", hp)], [bho])
                bk, bh = nb()
                for hd in range(4):
                    hp = hd // 2
                    mm(bk[:, hd * 128:(hd + 1) * 128], kend[:, hp * 128:(hp + 1) * 128], vtok[:, j, hd * 128:(hd + 1) * 128],
                       True, True, ["kend", ("vtok", j)], [bh])
                fill()
                for hd in range(4):
                    hp, hi = hd // 2, hd % 2
                    rows = slice(hi * 64, hi * 64 + 64)
                    stt("dve", Sf[rows, hd, :], Sf[rows, hd, :], Eq[rows, hp, 127:128], bk[rows, hd * 128:(hd + 1) * 128],
                        ALU.mult, ALU.add, [("Sf", hd), "Eq", bh], [("Sf", hd)])
                    cp("pool", Sbz[rows, hd, :], Sf[rows, hd, :], [("Sf", hd)], [("Sbz", hd)])
                act(osq[:], bo[:], AF.Square, [bho], ["osq"])
                bk, bh = nb()
                mm(bk[:], onesb[:], osq[:], True, True, ["onesb", "osq"], [bh])
                fill()
                act(rsd[:], bk[:], AF.Ln, [bh], ["rsd"], bias=EPS)
                act(rsd[:], rsd[:], AF.Exp, ["rsd"], ["rsd"], scale=-0.5)
                tt("dve", ot[:], bo[:], rsd[:], ALU.mult, [bho, "rsd"], ["ot"])
                for hd in range(4):
                    stt("dve", yglaT[:, hd, cs_], ot[:, hd * 128:(hd + 1) * 128], gnorm[:, hd:hd + 1], rT[:, hd, cs_],
                        ALU.mult, ALU.mult, ["ot", ("rT", hd)], [("ygla", hd)])

            fill(len(fillers))
            gls, glh = ws.next()
            pg = gls.rearrange("p (k c) -> p k c", k=4)
            for dj in range(8):
                bB, hB = nb()
                for k in range(4):
                    mm(bB[:], pg[:, k, dj * 128:(dj + 1) * 128], yglaT[:, k, :], k == 0, k == 3, glh + [("ygla", k)], [hB])
                tt("dve", mtmp[:, dj % 2, :], sgt[:, 8 + dj, :], bB[:], ALU.mult, [("sgt", 8 + dj), hB], [("mtmp", dj % 2)])
                tt("pool", mT[:, dj, :], mT[:, dj, :], mtmp[:, dj % 2, :], ALU.add, [("mT", dj), ("mtmp", dj % 2)], [("mT", dj)])
            (wo0, wh0), (wo1, wh1) = ws.next(2)
            wov = [wo0.rearrange("p (k c) -> p k c", k=4), wo1.rearrange("p (k c) -> p k c", k=4)]
            for sub in range(4):
                for half in range(2):
                    bk, bh = nb()
                    for k in range(8):
                        mm(bk[:], mT[:, k, sub * 128:(sub + 1) * 128], wov[k // 4][:, k % 4, half * 512:(half + 1) * 512],
                           k == 0, k == 7, wh0 + wh1 + [("mT", k)], [bh])
                    hv = h[:, sub, half * 512:(half + 1) * 512]
                    tt("dve", hv, hv, bk[:], ALU.add, [bh, HH(sub)], [HH(sub)])
            if "mix" in dbg and i == 0:
                dump("hmix", h[:], [HH(s_) for s_ in range(4)], [128, 4, D], F32)
                dump("ygla", yglaT[:], [("ygla", s_) for s_ in range(4)], [128, 4, NT], BF16)
                dump("ys5", ys5T[:], [("ys5", s_) for s_ in range(4)], [128, 4, NT], BF16)

            rmsnorm_T(B, 2)
            if "noffn2" in dbg:
                ws.pos += 17
            else:
                ffn(B, ws, after_h13=(lambda i=i: load_tile(i + 1)) if i + 1 < ntiles else None)
            ss, rstd, xn = B["ss"], B["rstd"], B["xn"]
            mset("dve", ss[:], 0.0, ["ss"])
            for sub in range(4):
                act(xn[:, sub, :], h[:, sub, :], AF.Square, [HH(sub), "ss"], [("xn", sub), ("ssv", sub)],
                    accum_out=ss[:, sub:sub + 1])
            ts("dve", rstd[:], ss[:], 1.0 / D, EPS, ALU.mult, ALU.add, [("ssv", s_) for s_ in range(4)], ["rstd"])
            act(rstd[:], rstd[:], AF.Sqrt, ["rstd"], ["rstd"])
            P.op("dve", lambda e: e.reciprocal(out=rstd[:], in_=rstd[:]), reads=["rstd"], writes=["rstd"])
            for sub in range(4):
                stt("dve", h[:, sub, :], h[:, sub, :], rstd[:, sub:sub + 1], gfin[:],
                    ALU.mult, ALU.mult, [HH(sub), "rstd"], [HH(sub)])
                dma("pool", outv[i, sub], h[:, sub, :], [HH(sub)], [("out", sub)], semkey=("out", sub))
        P.emit_phase()


def kernel(**inputs):
    nc = build_program()
    nb_ = int(np.asarray(inputs["x"]).shape[0])
    in_maps = []
    for b in range(nb_):
        m = {"x": np.ascontiguousarray(np.asarray(inputs["x"])[b], dtype=np.float32)}
        for k, shp in PARAM_SHAPES.items():
            m[k] = np.ascontiguousarray(np.asarray(inputs[k], dtype=np.float32).reshape(shp))
        in_maps.append(m)
    res = run_bass_kernel_spmd(nc, in_maps, core_ids=list(range(nb_)))
    return np.stack([np.asarray(r["out"], dtype=np.float32) for r in res.results], axis=0)
```

```python
from concourse.bass_utils import run_bass_kernel_spmd
import numpy as np
import concourse.bass as bass
import concourse.mybir as mybir

F32 = mybir.dt.float32
BF16 = mybir.dt.bfloat16
AF = mybir.ActivationFunctionType
ALU = mybir.AluOpType
AX = mybir.AxisListType

ENGS = ("pe", "act", "dve", "pool", "sp")
CE = ("pe", "act", "dve", "pool")


class Op:
    __slots__ = ("eng", "fn", "reads", "writes", "dma", "semkey", "idx", "signal",
                 "sigidx", "deps", "dmacount")

    def __init__(self, eng, fn, reads, writes, dma, semkey):
        self.eng = eng
        self.fn = fn
        self.reads = tuple(reads)
        self.writes = tuple(writes)
        self.dma = dma
        self.semkey = semkey
        self.signal = False
        self.sigidx = 0
        self.deps = ()
        self.dmacount = 0


class Prog:
    def __init__(self, nc, es, same_engine_sync=True):
        self.nc = nc
        self.es = es
        self.ops = []
        self.same_engine_sync = same_engine_sync
        self.csem = {e: es.enter_context(nc.semaphore("c_" + e)) for e in CE}
        self.dsem = {}
        self.ccount = {e: 0 for e in CE}
        self.dcount = {}
        self.waited = {e: {} for e in ENGS}
        self.nops = 0

    def op(self, eng, fn, reads=(), writes=()):
        o = Op(eng, fn, reads, writes, False, None)
        self.ops.append(o)
        return o

    def dma(self, eng, fn, reads=(), writes=(), semkey=None):
        assert eng in ("sp", "act", "pool")
        if semkey is None:
            semkey = writes[0] if writes else reads[0]
        o = Op(eng, fn, reads, writes, True, semkey)
        self.ops.append(o)
        return o

    def _analyze(self):
        last_writer = {}
        readers = {}
        for i, o in enumerate(self.ops):
            o.idx = i
            deps = {}
            for h in o.reads:
                w = last_writer.get(h)
                if w is not None:
                    deps[w.idx] = w
            for h in o.writes:
                w = last_writer.get(h)
                if w is not None:
                    deps[w.idx] = w
                for r in readers.get(h, ()):
                    deps[r.idx] = r
            deps.pop(i, None)
            o.deps = tuple(deps.values())
            for h in o.reads:
                lst = readers.setdefault(h, [])
                if not o.dma:
                    lst[:] = [r for r in lst if r.dma or r.eng != o.eng]
                lst.append(o)
            for h in o.writes:
                last_writer[h] = o
                readers[h] = []
            if o.dma:
                if o.semkey not in self.dsem:
                    self.dsem[o.semkey] = self.es.enter_context(
                        self.nc.semaphore("d_%d" % len(self.dsem)))
                    self.dcount[o.semkey] = 0
                c = self.dcount[o.semkey] + 16
                self.dcount[o.semkey] = c
                o.dmacount = c
        for o in self.ops:
            for d in o.deps:
                if d.dma:
                    continue
                if d.eng == o.eng and not o.dma:
                    if d.eng == "pe" or not self.same_engine_sync:
                        continue
                d.signal = True
        last = {}
        for o in self.ops:
            if not o.dma:
                last[o.eng] = o
        for o in last.values():
            o.signal = True
        for o in self.ops:
            if not o.dma and o.signal:
                self.ccount[o.eng] += 1
                o.sigidx = self.ccount[o.eng]

    def emit_phase(self):
        nc = self.nc
        self._analyze()
        ops = self.ops
        self.nops += len(ops)
        same_sync = self.same_engine_sync
        csem, dsem = self.csem, self.dsem
        cend = dict(self.ccount)
        dend = dict(self.dcount)

        def stream(engname):
            def body(eng):
                waited = self.waited[engname]
                for o in ops:
                    if o.eng != engname:
                        continue
                    need = {}
                    for d in o.deps:
                        if d.dma:
                            key = ("d", d.semkey)
                            val = d.dmacount
                        else:
                            if d.eng == engname and not o.dma:
                                if engname == "pe" or not same_sync:
                                    continue
                            key = ("c", d.eng)
                            val = d.sigidx
                        if val > need.get(key, 0):
                            need[key] = val
                    for key, val in need.items():
                        if waited.get(key, 0) >= val:
                            continue
                        waited[key] = val
                        sem = dsem[key[1]] if key[0] == "d" else csem[key[1]]
                        eng.wait_ge(sem, val)
                    ins = o.fn(eng)
                    if o.dma:
                        ins.then_inc(dsem[o.semkey], 16)
                    elif o.signal:
                        ins.then_inc(csem[o.eng], 1)
                for e2 in CE:
                    key = ("c", e2)
                    if cend[e2] > waited.get(key, 0):
                        waited[key] = cend[e2]
                        eng.wait_ge(csem[e2], cend[e2])
                for k2, v2 in dend.items():
                    key = ("d", k2)
                    if v2 > waited.get(key, 0):
                        waited[key] = v2
                        eng.wait_ge(dsem[k2], v2)
            return body

        with nc.Block() as block:
            block.tensor(stream("pe"))
            block.scalar(stream("act"))
            block.vector(stream("dve"))
            block.gpsimd(stream("pool"))
            block.sync(stream("sp"))
        self.ops = []

from contextlib import ExitStack
import math

D = 1024
FF = 2816
NF = 22
SEQ = 4096
NT = 512
NTILES = 8
NCH = 512
INC = 4112
C_S5, C_Q, C_K, C_V, C_R, C_AL, C_GS, C_GG = 0, 512, 768, 1024, 1536, 2048, 2064, 3088
EPS = 1e-6
PI = math.pi

PARAM_SHAPES = {
    "ffn1_norm": [D], "ffn1_w1": [D, FF], "ffn1_w3": [D, FF], "ffn1_w2": [FF, D],
    "mix_norm": [D], "w_in": [D, INC],
    "s5_lambda_re": [32, 64], "s5_lambda_im": [32, 64], "s5_log_dt": [32],
    "s5_b_re": [32, 64, 16], "s5_b_im": [32, 64, 16], "s5_c_re": [32, 16, 64], "s5_c_im": [32, 16, 64],
    "s5_d": [32, 16], "s5_glu_w": [512, 512], "s5_glu_b": [512],
    "gla_a_up_w": [16, 256], "gla_a_up_b": [256], "gla_out_norm": [512],
    "proj_s5": [512, D], "proj_gla": [512, D], "w_out": [D, D],
    "ffn2_norm": [D], "ffn2_w1": [D, FF], "ffn2_w3": [D, FF], "ffn2_w2": [FF, D],
    "final_norm": [D],
}


def build_program(dbg=(), stop_after=None, ntiles=NTILES):
    nc = bass.Bass("TRN2", target_bir_lowering=False)
    I = {}
    I["x"] = nc.dram_tensor("x", [SEQ, D], F32, kind="ExternalInput").ap()
    for k, shp in PARAM_SHAPES.items():
        I[k] = nc.dram_tensor(k, shp, F32, kind="ExternalInput").ap()
    out = nc.dram_tensor("out", [SEQ, D], F32, kind="ExternalOutput").ap()

    def dscr(name, shape, dt):
        kind = "ExternalOutput" if name in dbg else "Internal"
        return nc.dram_tensor(name, shape, dt, kind=kind).ap()

    WC = {}
    for nm in ("ffn1_w1", "ffn1_w3", "ffn1_w2", "ffn2_w1", "ffn2_w3", "ffn2_w2", "w_in",
               "s5_glu_w", "proj_s5", "proj_gla", "w_out"):
        WC[nm] = dscr(nm + "_c", PARAM_SHAPES[nm], BF16)
    h_scr = dscr("h_scr", [SEQ, D], F32)
    uT_scr = dscr("uT_scr", [NTILES, 128, 8 * NT], BF16)
    dbgout = {}

    with ExitStack() as es:
        import os as _os
        P = Prog(nc, es, same_engine_sync=not _os.environ.get("NO_SES"))

        def sb(scope, name, shape, dt):
            return scope.enter_context(nc.sbuf_tensor(name, shape, dt))

        banks = [es.enter_context(nc.psum_tensor("bank%d" % i, [128, 512], F32)) for i in range(8)]
        bstate = {"i": 0}

        bstate["res"] = set()

        def nb(reserve=False):
            while True:
                i = bstate["i"]
                bstate["i"] = (i + 1) % 8
                if i not in bstate["res"]:
                    break
            if reserve:
                bstate["res"].add(i)
            return banks[i], ("ps", i)

        nb.release = lambda h_: bstate["res"].discard(h_[1])

        def mm(out_, lhsT, rhs, start, stop, r, w, **kw):
            P.op("pe", lambda e: e.matmul(out_, lhsT=lhsT, rhs=rhs, start=start, stop=stop, **kw),
                 reads=r, writes=w)

        def act(out_, in_, func, r, w, **kw):
            P.op("act", lambda e: e.activation(out=out_, in_=in_, func=func, **kw), reads=r, writes=w)

        def tt(eng, out_, in0, in1, op, r, w):
            P.op(eng, lambda e: e.tensor_tensor(out=out_, in0=in0, in1=in1, op=op), reads=r, writes=w)

        def ts(eng, out_, in0, s1, s2, op0, op1, r, w):
            if s2 is None:
                P.op(eng, lambda e: e.tensor_scalar(out=out_, in0=in0, scalar1=s1, scalar2=None, op0=op0),
                     reads=r, writes=w)
            else:
                P.op(eng, lambda e: e.tensor_scalar(out=out_, in0=in0, scalar1=s1, scalar2=s2, op0=op0, op1=op1),
                     reads=r, writes=w)

        def stt(eng, out_, in0, scalar, in1, op0, op1, r, w):
            P.op(eng, lambda e: e.scalar_tensor_tensor(out=out_, in0=in0, scalar=scalar, in1=in1, op0=op0, op1=op1),
                 reads=r, writes=w)

        def cp(eng, out_, in_, r, w):
            if eng == "act":
                P.op("act", lambda e: e.activation(out=out_, in_=in_, func=AF.Copy), reads=r, writes=w)
            else:
                P.op(eng, lambda e: e.tensor_copy(out=out_, in_=in_), reads=r, writes=w)

        def mset(eng, ap, val, w):
            P.op(eng, lambda e: e.memset(ap, val), writes=w)

        def dma(eng, out_, in_, r, w, semkey=None, slow=False):
            if slow:
                return P.dma(eng, lambda e: e.dma_start(out=out_, in_=in_, allow_slow_non_contiguous=True),
                             reads=r, writes=w, semkey=semkey)
            return P.dma(eng, lambda e: e.dma_start(out=out_, in_=in_), reads=r, writes=w, semkey=semkey)

        def dump(name, ap, handles, shape, dt):
            d = nc.dram_tensor("dbg_" + name, shape, dt, kind="ExternalOutput").ap()
            dma("sp", d, ap, handles, [("dbg", name)], semkey="dbg")

        CAST1 = ("ffn1_w1", "ffn1_w3", "ffn1_w2", "w_in")

        def issue_casts(names, after=()):
            for nm in names:
                if "nocast" in dbg:
                    break
                R = PARAM_SHAPES[nm][0]
                for r0 in range(0, R, 128):
                    r1 = min(R, r0 + 128)
                    dma("pool", WC[nm][r0:r1, :], I[nm][r0:r1, :], list(after), [("cast", nm, r0)], semkey="cast")

        CAST2 = []
        for nm_ in WC:
            if nm_ not in CAST1:
                for r0_ in range(0, PARAM_SHAPES[nm_][0], 128):
                    CAST2.append((nm_, r0_, min(PARAM_SHAPES[nm_][0], r0_ + 128)))

        def issue_casts2(i):
            half = len(CAST2) // 2
            if i == -1:
                lst = CAST2[half:]
            elif i >= 2:
                per = (half + 5) // 6
                lst = CAST2[:half][(i - 2) * per:(i - 1) * per]
            else:
                lst = []
            for (nm, r0, r1) in lst:
                if "nocast" not in dbg:
                    dma("pool", WC[nm][r0:r1, :], I[nm][r0:r1, :], [], [("cast", nm, r0)], semkey="cast")

        identf = sb(es, "identf", [128, 128], F32)
        identb = sb(es, "identb", [128, 128], BF16)
        gains = sb(es, "gains", [128, 3, 8], F32)
        ybuf = sb(es, "ybuf", [128, 4, SEQ], BF16)

        mset("pool", identf[:], 1.0, ["identf"])
        P.op("pool", lambda e: e.affine_select(out=identf[:], in_=identf[:], pattern=[[-1, 128]],
                                               compare_op=ALU.is_equal, fill=0.0, base=0, channel_multiplier=1),
             reads=["identf"], writes=["identf"])
        cp("dve", identb[:], identf[:], ["identf"], ["identb"])
        for gi_, nm in enumerate(("ffn1_norm", "mix_norm", "ffn2_norm")):
            dma("sp", gains[:, gi_, :], I[nm].rearrange("(k p) -> p k", p=128), [], [("gains", gi_)],
                semkey="small", slow=True)

        mid = es.enter_context(ExitStack())
        s5in = sb(mid, "s5in", [128, 4, 8, NCH], BF16)
        W2c = sb(mid, "W2c", [128, 4, 8, 2, 128], BF16)
        W3c = sb(mid, "W3c", [128, 16, 8, 2, 32], BF16)
        BD = sb(mid, "BD", [128, 4, 8, 128], BF16)
        Apow = sb(mid, "Apow", [128, 9, 3, 16], F32)

        with ExitStack() as p0:
            def t16(name):
                return sb(p0, name, [128, 16], F32)

            def t1616(name):
                return sb(p0, name, [128, 16, 16], F32)

            lre, lim, ldt = t16("lre"), t16("lim"), t16("ldt")
            dtv, mag, th, thr, tmp, sn, cs = [t16(n) for n in ("dtv", "mag", "th", "thr", "tmp", "sn", "cs")]
            ar, ai, den, rden, nr, fr, fi, tq = [t16(n) for n in ("ar", "ai", "den", "rden", "nr", "fr", "fi", "tq")]
            bre, bim, cre, cim, bbre, bbim = [t1616(n) for n in ("bre", "bim", "cre", "cim", "bbre", "bbim")]
            mcre, mcim, t1, t2 = [t1616(n) for n in ("mcre", "mcim", "t1", "t2")]
            pw = sb(p0, "pw", [128, 9, 2, 16], F32)
            MPre = sb(p0, "MPre", [128, 16, 128], F32)
            MPim = sb(p0, "MPim", [128, 16, 128], F32)
            CPre = sb(p0, "CPre", [128, 16, 128], F32)
            CPnim = sb(p0, "CPnim", [128, 16, 128], F32)
            dvec = sb(p0, "dvec", [128, 4], F32)

            for gi in range(2):
                rows = slice(gi * 64, gi * 64 + 64)
                dma("sp", lre[rows, :], I["s5_lambda_re"].rearrange("(q g) p -> g p q", g=2)[gi], [], [("lre", gi)], semkey="small", slow=True)
                dma("sp", lim[rows, :], I["s5_lambda_im"].rearrange("(q g) p -> g p q", g=2)[gi], [], [("lim", gi)], semkey="small", slow=True)
                dma("sp", ldt[rows, :], I["s5_log_dt"].rearrange("(q g) -> g q", g=2)[gi:gi + 1, :].to_broadcast([64, 16]),
                    [], [("ldt", gi)], semkey="small", slow=True)
                dma("sp", bre[rows], I["s5_b_re"].rearrange("(q g) p h -> g p q h", g=2)[gi], [], [("bre", gi)], semkey="small", slow=True)
                dma("sp", bim[rows], I["s5_b_im"].rearrange("(q g) p h -> g p q h", g=2)[gi], [], [("bim", gi)], semkey="small", slow=True)
                for q_ in range(16):
                    dma("sp", cre[rows, q_, :], I["s5_c_re"][2 * q_ + gi].rearrange("h p -> p h"), [], [("cre", gi, q_)], semkey="small", slow=True)
                    dma("act", cim[rows, q_, :], I["s5_c_im"][2 * q_ + gi].rearrange("h p -> p h"), [], [("cim", gi, q_)], semkey="small2", slow=True)
            dma("sp", dvec[:], I["s5_d"].rearrange("(t g) h -> (g h) t", t=4), [], ["dvec"], semkey="small", slow=True)
            L = lambda n: [(n, 0), (n, 1)]
            if stop_after == "prep_loads":
                P.emit_phase()
                return nc
            small_h = ([(n_, g_) for n_ in ("lre", "lim", "ldt", "bre", "bim", "cre", "cim") for g_ in range(2)]
                       + [(n_, g_, q_) for n_ in ("cre", "cim") for g_ in range(2) for q_ in range(16)]
                       + ["dvec"] + [("gains", g_) for g_ in range(3)])
            sync0 = sb(p0, "sync0", [128, 1], F32)
            P.op("dve", lambda e: e.memset(sync0[:], 0.0), reads=small_h, writes=small_h + ["sync0"])
            issue_casts(CAST1, after=["sync0"])
            ts("dve", tmp[:], ldt[:], 1.0, None, ALU.mult, None, L("ldt"), ["tmp"])
            act(dtv[:], tmp[:], AF.Exp, ["tmp"], ["dtv"])
            tt("dve", tmp[:], lre[:], dtv[:], ALU.mult, L("lre") + ["dtv"], ["tmp"])
            act(mag[:], tmp[:], AF.Exp, ["tmp"], ["mag"])
            tt("dve", th[:], lim[:], dtv[:], ALU.mult, L("lim") + ["dtv"], ["th"])

            def range_reduce(dst, src_shift):
                ts("dve", dst[:], th[:], src_shift, None, ALU.add, None, ["th"], ["rr"])
                ts("dve", tq[:], th[:], src_shift, None, ALU.add, None, ["th"], ["tq"])
                for m in range(1, 10):
                    ts("dve", tmp[:], tq[:], (2 * m - 1) * PI, -2.0 * PI, ALU.is_ge, ALU.mult, ["tq"], ["tmp"])
                    tt("dve", dst[:], dst[:], tmp[:], ALU.add, ["rr", "tmp"], ["rr"])

            range_reduce(thr, 0.0)
            act(sn[:], thr[:], AF.Sin, ["rr"], ["sn"])
            range_reduce(thr, PI / 2)
            act(cs[:], thr[:], AF.Sin, ["rr"], ["cs"])
            tt("dve", ar[:], mag[:], cs[:], ALU.mult, ["mag", "cs"], ["ar"])
            tt("dve", ai[:], mag[:], sn[:], ALU.mult, ["mag", "sn"], ["ai"])
            tt("dve", den[:], lre[:], lre[:], ALU.mult, L("lre"), ["den"])
            tt("dve", tmp[:], lim[:], lim[:], ALU.mult, L("lim"), ["tmp"])
            tt("dve", den[:], den[:], tmp[:], ALU.add, ["den", "tmp"], ["den"])
            P.op("dve", lambda e: e.reciprocal(out=rden[:], in_=den[:]), reads=["den"], writes=["rden"])
            ts("dve", nr[:], ar[:], -1.0, None, ALU.add, None, ["ar"], ["nr"])
            tt("dve", fr[:], nr[:], lre[:], ALU.mult, ["nr"] + L("lre"), ["fr"])
            tt("dve", tmp[:], ai[:], lim[:], ALU.mult, ["ai"] + L("lim"), ["tmp"])
            tt("dve", fr[:], fr[:], tmp[:], ALU.add, ["fr", "tmp"], ["fr"])
            tt("dve", fr[:], fr[:], rden[:], ALU.mult, ["fr", "rden"], ["fr"])
            tt("dve", fi[:], ai[:], lre[:], ALU.mult, ["ai"] + L("lre"), ["fi"])
            tt("dve", tmp[:], nr[:], lim[:], ALU.mult, ["nr"] + L("lim"), ["tmp"])
            tt("dve", fi[:], fi[:], tmp[:], ALU.subtract, ["fi", "tmp"], ["fi"])
            tt("dve", fi[:], fi[:], rden[:], ALU.mult, ["fi", "rden"], ["fi"])

            def bc16(a):
                return a.unsqueeze(2).to_broadcast([128, 16, 16])

            def cmul(ore, oim, are_, aim_, bre_, bim_, ra, rb, wo):
                tt("dve", t1[:], bre_, are_, ALU.mult, ra + rb, ["t1"])
                tt("dve", t2[:], bim_, aim_, ALU.mult, ra + rb, ["t2"])
                tt("dve", ore, t1[:], t2[:], ALU.subtract, ["t1", "t2"], [wo + "re"])
                tt("dve", t1[:], bim_, are_, ALU.mult, ra + rb, ["t1"])
                tt("dve", t2[:], bre_, aim_, ALU.mult, ra + rb, ["t2"])
                tt("dve", oim, t1[:], t2[:], ALU.add, ["t1", "t2"], [wo + "im"])

            cmul(bbre[:], bbim[:], bc16(fr[:]), bc16(fi[:]), bre[:], bim[:], ["fr", "fi"], L("bre") + L("bim"), "bb")
            mset("dve", pw[:, 0, 0, :], 1.0, [("pw", 0)])
            mset("dve", pw[:, 0, 1, :], 0.0, [("pw", 0)])
            for j in range(1, 9):
                pr, pi_ = pw[:, j - 1, 0, :], pw[:, j - 1, 1, :]
                tt("dve", tmp[:], pr, ar[:], ALU.mult, [("pw", j - 1), "ar"], ["tmp"])
                tt("dve", tq[:], pi_, ai[:], ALU.mult, [("pw", j - 1), "ai"], ["tq"])
                tt("dve", pw[:, j, 0, :], tmp[:], tq[:], ALU.subtract, ["tmp", "tq"], [("pw", j)])
                tt("dve", tmp[:], pr, ai[:], ALU.mult, [("pw", j - 1), "ai"], ["tmp"])
                tt("dve", tq[:], pi_, ar[:], ALU.mult, [("pw", j - 1), "ar"], ["tq"])
                tt("dve", pw[:, j, 1, :], tmp[:], tq[:], ALU.add, ["tmp", "tq"], [("pw", j)])
            cp("dve", Apow[:, 0, 0, :], cs[:], ["cs"], [("Apow", 0)])
            cp("dve", Apow[:, 0, 1, :], sn[:], ["sn"], [("Apow", 0)])
            cp("dve", Apow[:, 0, 2, :], mag[:], ["mag"], [("Apow", 0)])

            if stop_after == "prep_elem":
                for nm_, t_, hs_ in (("ldt", ldt, L("ldt")), ("lre", lre, L("lre")), ("lim", lim, L("lim")), ("dtv", dtv, ["dtv"]), ("mag", mag, ["mag"]), ("th", th, ["th"]),
                                     ("sn", sn, ["sn"]), ("cs", cs, ["cs"]), ("ar", ar, ["ar"]), ("ai", ai, ["ai"]), ("fr", fr, ["fr"]), ("fi", fi, ["fi"])):
                    dump(nm_, t_[:], hs_, [128, 16], F32)
                dump("pw", pw[:], [("pw", j) for j in range(9)], [128, 9, 2, 16], F32)
                dump("bbre", bbre[:], ["bbre"], [128, 16, 16], F32)
                dump("Apow", Apow[:], [("Apow", l) for l in range(9)] + [("ApowN", l) for l in range(9)], [128, 9, 3, 16], F32)
                P.emit_phase()
                return nc
            mset("pool", MPre[:], 0.0, ["MPre"])
            mset("pool", MPim[:], 0.0, ["MPim"])
            mset("pool", CPre[:], 0.0, ["CPre"])
            mset("pool", CPnim[:], 0.0, ["CPnim"])
            mset("pool", W3c[:], 0.0, ["W3c"])

            def scatter(dstpad, src, rs, w, neg=False, eng="pool"):
                for gi in range(2):
                    rows = slice(gi * 64, gi * 64 + 64)
                    for ql in range(4):
                        c0 = 32 * ql + 16 * gi
                        o_ = dstpad[rows, ql::4, c0:c0 + 16]
                        i_ = src[rows, ql::4, :]
                        if neg:
                            ts(eng, o_, i_, -1.0, None, ALU.mult, None, rs, [w])
                        else:
                            cp(eng, o_, i_, rs, [w])

            scatter(CPre, cre, L("cre"), "CPre")
            scatter(CPnim, cim, L("cim"), "CPnim", neg=True)

            for t in range(8):
                cmul(mcre[:], mcim[:], bc16(pw[:, t + 1, 0, :]), bc16(pw[:, t + 1, 1, :]), cre[:], cim[:],
                     [("pw", t + 1)], L("cre") + L("cim"), "mc")
                for gi in range(2):
                    rows = slice(gi * 64, gi * 64 + 64)
                    cp("pool", W3c[rows, :, t, 0, 16 * gi:16 * gi + 16], mcre[rows], ["mcre"], ["W3c"])
                    ts("pool", W3c[rows, :, t, 1, 16 * gi:16 * gi + 16], mcim[rows], -1.0, None, ALU.mult, None,
                       ["mcim"], ["W3c"])

            if stop_after == "prep_pad":
                dump("W3c", W3c[:], ["W3c"], [128, 16, 8, 2, 32], BF16)
                P.emit_phase()
                return nc
            for j in range(8):
                cmul(mcre[:], mcim[:], bc16(pw[:, j, 0, :]), bc16(pw[:, j, 1, :]), bbre[:], bbim[:],
                     [("pw", j)], ["bbre", "bbim"], "mc")
                scatter(MPre, mcre, ["mcre"], "MPre", eng="pool")
                scatter(MPim, mcim, ["mcim"], "MPim", eng="dve")
                s = 7 - j
                for T in range(4):
                    if "noK" in dbg:
                        break
                    bk, bh = nb()
                    n = 0
                    for ql in range(4):
                        q = 4 * T + ql
                        for (mp, cpd, hm, hc) in ((MPre, CPre, "MPre", "CPre"), (MPim, CPnim, "MPim", "CPnim")):
                            mm(bk[:, 0:128], mp[:, q, :], cpd[:, q, :], n == 0, n == 7, [hm, hc], [bh])
                            n += 1
                    if j == 0:
                        stt("dve", BD[:, T, 0, :], identf[:], dvec[:, T:T + 1], bk[:, 0:128], ALU.mult, ALU.add,
                            [bh, "identf", "dvec"], [("BD", T, 0)])
                    else:
                        cp("dve", BD[:, T, j, :], bk[:, 0:128], [bh], [("BD", T, j)])
                    if "noW2" in dbg:
                        continue
                    bk2, bh2 = nb()
                    for pi2, (mp, hm) in enumerate(((MPre, "MPre"), (MPim, "MPim"))):
                        for ql in range(4):
                            q = 4 * T + ql
                            mm(bk2[:, pi2 * 128:(pi2 + 1) * 128], mp[:, q, :], identf[:], ql == 0, ql == 3,
                               [hm, "identf"], [bh2])
                    cp("act", W2c[:, T, s, :, :], bk2[:, 0:256].rearrange("p (a b) -> p a b", a=2), [bh2], [("W2c", T, s)])

            if "prep" in dbg:
                dump("BD", BD[:], [("BD", T, j) for T in range(4) for j in range(8)], [128, 4, 8, 128], BF16)
                dump("W2c", W2c[:], [("W2c", T, s) for T in range(4) for s in range(8)], [128, 4, 8, 2, 128], BF16)
                dump("W3c", W3c[:], ["W3c"], [128, 16, 8, 2, 32], BF16)
                dump("Apow", Apow[:], [("Apow", l) for l in range(9)] + [("ApowN", l) for l in range(9)], [128, 9, 3, 16], F32)
            P.emit_phase()
        if stop_after == "prep":
            return nc
        build_rest(nc, P, I, WC, out, h_scr, uT_scr, es, mid, sb, banks, nb, mm, act, tt, ts, stt, cp, mset, dma, dump,
                   identf, identb, gains, ybuf, s5in, W2c, W3c, BD, Apow, dbg, stop_after, ntiles,
                   issue_casts2)
    return nc


class WStream:
    def __init__(self, slots, specs, dma, pf=3):
        self.slots, self.specs, self.dma, self.pf = slots, specs, dma, pf
        self.issued = 0
        self.pos = 0

    def _issue(self, i):
        si = i % len(self.slots)
        pairs = self.specs[i](self.slots[si])
        assert len(pairs) == 2
        import os as _os
        if _os.environ.get("WS_NODMA") and i >= len(self.slots):
            return
        for pi, (o, a) in enumerate(pairs):
            self.dma("sp", o, a, [], [("ring", si, pi)], semkey=("ring", si, pi))

    def next(self, n=1):
        lim = self.pos + len(self.slots) - 1
        while self.issued < min(len(self.specs), lim):
            self._issue(self.issued)
            self.issued += 1
        res = []
        for _ in range(n):
            si = self.pos % len(self.slots)
            self.pos += 1
            res.append((self.slots[si], [("ring", si, 0), ("ring", si, 1)]))
        return res[0] if n == 1 else res


def build_rest(nc, P, I, WC, out, h_scr, uT_scr, es, mid, sb, banks, nb, mm, act, tt, ts, stt, cp, mset, dma, dump,
               identf, identb, gains, ybuf, s5in, W2c, W3c, BD, Apow, dbg, stop_after, ntiles, issue_casts2):
    kp = lambda a: a.rearrange("(k p) c -> p k c", p=128)

    def spec_w13(w1c, w3c, g):
        def f(slot):
            return [(slot[:, 0:2048].rearrange("p (k c) -> p k c", k=8), kp(w1c)[:, :, g * 256:(g + 1) * 256]),
                    (slot[:, 2048:4096].rearrange("p (k c) -> p k c", k=8), kp(w3c)[:, :, g * 256:(g + 1) * 256])]
        return f

    def spec_w2(w2c, g2):
        nf = 4 if g2 < 5 else 2
        hlf = nf // 2

        def f(slot):
            v = slot.rearrange("p (f c) -> p f c", f=4)
            src = w2c[g2 * 512:g2 * 512 + nf * 128, :].rearrange("(f p) c -> p f c", p=128)
            return [(v[:, 0:hlf, :], src[:, 0:hlf, :]), (v[:, hlf:nf, :], src[:, hlf:nf, :])]
        return f

    def spec_cols(wc, c0, nk):
        def f(slot):
            v = slot[:, 0:nk * 512].rearrange("p (k c) -> p k c", k=nk)
            src = kp(wc)[:, :, c0:c0 + 512]
            h2 = nk // 2
            return [(v[:, 0:h2, :], src[:, 0:h2, :]), (v[:, h2:nk, :], src[:, h2:nk, :])]
        return f

    def spec_rows(wc, k0):
        def f(slot):
            v = slot.rearrange("p (k c) -> p k c", k=4)
            src = kp(wc)[:, k0:k0 + 4, :]
            return [(v[:, 0:2, :], src[:, 0:2, :]), (v[:, 2:4, :], src[:, 2:4, :])]
        return f

    def ffn_specs(pre):
        return ([spec_w13(WC[pre + "_w1"], WC[pre + "_w3"], g) for g in range(11)] +
                [spec_w2(WC[pre + "_w2"], g2) for g2 in range(6)] * 2)

    def rms_stats(B, gidx):
        h, xn, ss, rstd = B["h"], B["xn"], B["ss"], B["rstd"]
        HH = B["hh"]
        mset("dve", ss[:], 0.0, ["ss"])
        for sub in range(4):
            act(xn[:, sub, :], h[:, sub, :], AF.Square, [HH(sub), "ss"], [("xn", sub), ("ssv", sub)],
                accum_out=ss[:, sub:sub + 1])
        ts("dve", rstd[:], ss[:], 1.0 / D, EPS, ALU.mult, ALU.add, [("ssv", s_) for s_ in range(4)], ["rstd"])
        act(rstd[:], rstd[:], AF.Sqrt, ["rstd"], ["rstd"])
        P.op("dve", lambda e: e.reciprocal(out=rstd[:], in_=rstd[:]), reads=["rstd"], writes=["rstd"])
        for sub in range(4):
            act(xn[:, sub, :], h[:, sub, :], AF.Copy, [HH(sub), "rstd", ("xn", sub)], [("xn", sub)],
                scale=rstd[:, sub:sub + 1])

    def rms_transpose(B, gidx):
        xn, uT = B["xn"], B["uT"]
        for k in range(8):
            bk, bh = nb()
            bkb = bk[:].bitcast(BF16)
            for sub in range(4):
                P.op("pe", lambda e, k=k, sub=sub, bkb=bkb: e.transpose(
                    out=bkb[:, sub * 128:(sub + 1) * 128], in_=xn[:, sub, k * 128:(k + 1) * 128], identity=identb[:]),
                    reads=[("xn", sub), "identb"], writes=[bh])
            if k % 2 == 0:
                ts("dve", uT[:, k, :], bkb[:, 0:512], gains[:, gidx, k:k + 1], None, ALU.mult, None,
                   [bh, ("gains", gidx)], [(B["uTh"], k)])
            else:
                act(uT[:, k, :], bkb[:, 0:512], AF.Copy, [bh, ("gains", gidx)], [(B["uTh"], k)],
                    scale=gains[:, gidx, k:k + 1])

    def rmsnorm_T(B, gidx):
        rms_stats(B, gidx)
        rms_transpose(B, gidx)

    UT = [("uT", k) for k in range(8)]

    def ffn_h13(B, ws, g_lo, g_hi, uT=None):
        gT, sg = B["gT"], B["sg"]
        uT = B["uT"] if uT is None else uT
        for g in range(g_lo, g_hi):
            slot, sh = ws.next()
            w1v = slot[:, 0:2048].rearrange("p (k c) -> p k c", k=8)
            w3v = slot[:, 2048:4096].rearrange("p (k c) -> p k c", k=8)
            for fi in range(2):
                f = 2 * g + fi
                bA, hA = nb()
                for k in range(8):
                    mm(bA[:], w1v[:, k, fi * 128:(fi + 1) * 128], uT[:, k, :], k == 0, k == 7, sh + [(B["uTh"], k)], [hA])
                bB, hB = nb()
                for k in range(8):
                    mm(bB[:], w3v[:, k, fi * 128:(fi + 1) * 128], uT[:, k, :], k == 0, k == 7, sh + [(B["uTh"], k)], [hB])
                act(sg[:, f % 2, :], bA[:], AF.Silu, [hA], [("sg", f % 2)])
                tt("dve", gT[:, f, :], sg[:, f % 2, :], bB[:], ALU.mult, [("sg", f % 2), hB], [("gT", f)])

    def ffn_w2(B, ws):
        h, gT = B["h"], B["gT"]
        HH = B["hh"]
        for rnd in range(2):
            subs = (2 * rnd, 2 * rnd + 1)
            accs = {(sub, half): nb() for sub in subs for half in range(2)}
            for g2 in range(6):
                slot, sh = ws.next()
                w2v = slot.rearrange("p (f c) -> p f c", f=4)
                nf = 4 if g2 < 5 else 2
                for fi in range(nf):
                    f = 4 * g2 + fi
                    for sub in subs:
                        for half in range(2):
                            bk, bh = accs[(sub, half)]
                            mm(bk[:], gT[:, f, sub * 128:(sub + 1) * 128], w2v[:, fi, half * 512:(half + 1) * 512],
                               f == 0, f == NF - 1, sh + [("gT", f)], [bh])
            for sub in subs:
                for half in range(2):
                    bk, bh = accs[(sub, half)]
                    hv = h[:, sub, half * 512:(half + 1) * 512]
                    stt("dve", hv, bk[:], 0.5, hv, ALU.mult, ALU.add, [bh, HH(sub)], [HH(sub)])

    def ffn(B, ws, after_h13=None):
        ffn_h13(B, ws, 0, 11)
        if after_h13 is not None:
            after_h13()
        ffn_w2(B, ws)

    xv = I["x"].rearrange("(i s p) d -> i s p d", s=4, p=128)
    hsv = h_scr.rearrange("(i s p) d -> i s p d", s=4, p=128)
    outv = out.rearrange("(i s p) d -> i s p d", s=4, p=128)

    with ExitStack() as p1:
        B = {}
        hbuf = [sb(p1, "h1a", [128, 4, D], F32), sb(p1, "h1b", [128, 4, D], F32)]
        B["xn"] = sb(p1, "xn1", [128, 4, D], BF16)
        uTb = [sb(p1, "uT1a", [128, 8, NT], BF16), sb(p1, "uT1b", [128, 8, NT], BF16)]
        B["ss"] = sb(p1, "ss1", [128, 4], F32)
        B["rstd"] = sb(p1, "rstd1", [128, 4], F32)
        B["sg"] = sb(p1, "sg1", [128, 2, NT], F32)
        B["gT"] = ybuf[:].rearrange("p a b -> p (a b)")[:, 0:NF * NT].rearrange("p (f t) -> p f t", f=NF)
        ring = [sb(p1, "ring1_%d" % i, [128, 4096], BF16) for i in range(4)]
        wins5 = sb(p1, "wins5", [128, 8, 512], BF16)
        dma("sp", wins5[:, 0:4, :], kp(WC["w_in"])[:, 0:4, 0:512], [], ["wins5a"])
        dma("sp", wins5[:, 4:8, :], kp(WC["w_in"])[:, 4:8, 0:512], [], ["wins5b"])
        specs = []
        for i in range(ntiles):
            specs += ffn_specs("ffn1")
        ws = WStream([r[:] for r in ring], specs, dma)
        print("SBUF remaining pass1:", nc.sbuf_bytes_remaining)
        def load_x(i):
            for sub in range(4):
                dma("pool", hbuf[i % 2][:, sub, :], xv[i, sub], [], [("h", i % 2, sub)])

        def setB(i):
            par = i % 2
            B["h"] = hbuf[par]
            B["hh"] = lambda sub, par=par: ("h", par, sub)
            B["uT"] = uTb[par]
            B["uTh"] = ("uT", par)

        def mix_stats(i):
            setB(i)
            rms_stats(B, 1)

        def mix_and_proj(i):
            setB(i)
            par = i % 2
            uT = uTb[par]
            rms_transpose(B, 1)
            for half in range(2):
                dma("pool", uT_scr[i][:, half * 2048:(half + 1) * 2048].rearrange("p (k t) -> p k t", k=4),
                    uT[:, half * 4:(half + 1) * 4, :], [(("uT", par), k) for k in range(half * 4, half * 4 + 4)],
                    [("utscr", half)])
            for m in range(4):
                bk, bh = nb()
                for k in range(8):
                    mm(bk[:], wins5[:, k, m * 128:(m + 1) * 128], uT[:, k, :], k == 0, k == 7,
                       ["wins5a", "wins5b", (("uT", par), k)], [bh])
                o_ = s5in[:, m, :, i * 64:(i + 1) * 64]
                i_ = bk[:].rearrange("p (c s) -> p s c", s=8)
                if m % 2 == 0:
                    cp("dve", o_, i_, [bh], [("s5in", m)])
                else:
                    cp("act", o_, i_, [bh], [("s5in", m)])

        load_x(0)
        if ntiles > 1:
            load_x(1)
        setB(0)
        rmsnorm_T(B, 0)
        for i in range(ntiles):
            issue_casts2(i)
            if i > 0:
                mix_stats(i - 1)
            setB(i)
            ffn_h13(B, ws, 0, 4)
            if i > 0:
                mix_and_proj(i - 1)
                if i + 1 < ntiles:
                    load_x(i + 1)
            if i + 1 < ntiles:
                setB(i + 1)
                rms_stats(B, 0)
            setB(i)
            ffn_h13(B, ws, 4, 8)
            if i + 1 < ntiles:
                setB(i + 1)
                rms_transpose(B, 0)
            setB(i)
            ffn_h13(B, ws, 8, 11)
            ffn_w2(B, ws)
            for sub in range(4):
                dma("pool", hsv[i, sub], hbuf[i % 2][:, sub, :], [("h", i % 2, sub)], [("hscr", sub)])
        mix_stats(ntiles - 1)
        mix_and_proj(ntiles - 1)
        if "p1" in dbg:
            dump("s5in", s5in[:], [("s5in", m) for m in range(4)], [128, 4, 8, NCH], BF16)
        P.emit_phase()
    if stop_after == "p1":
        return

    with ExitStack() as p2:
        Eh = sb(p2, "Eh", [128, 9, 2, 16], F32)
        r8 = sb(p2, "r8", [128, 16], F32)
        w16 = [sb(p2, "w16_%d" % i_, [128, 16], F32) for i_ in range(6)]
        Etab = [sb(p2, "Etab%d" % i_, [128, 4, NCH], F32) for i_ in range(2)]
        et = [sb(p2, "et%d" % i_, [128, 4, NCH // 2], F32) for i_ in range(2)]
        SC = [[sb(p2, "sc%d_%d" % (a_, b_), [128, NCH], F32) for b_ in range(6)] for a_ in range(2)]
        X0 = [sb(p2, "X0_%d" % t_, [128, 4, 2, NCH], BF16) for t_ in range(2)]
        mset("pool", X0[0][:], 0.0, [("X0", 0, ql, pt) for ql in range(4) for pt in range(2)])
        mset("pool", X0[1][:], 0.0, [("X0", 1, ql, pt) for ql in range(4) for pt in range(2)])
        yv = ybuf[:].rearrange("p a (c t) -> p a c t", t=8)
        print("SBUF remaining s5:", nc.sbuf_bytes_remaining)
        issue_casts2(-1)

        cre_, cim_, m2_, t1_, t2_, rm_ = [w[:] for w in w16]

        def normalize(re, im, hre):
            tt("dve", m2_, re, re, ALU.mult, hre, ["w_m2"])
            tt("dve", t1_, im, im, ALU.mult, hre, ["w_t1"])
            tt("dve", m2_, m2_, t1_, ALU.add, ["w_m2", "w_t1"], ["w_m2"])
            act(m2_, m2_, AF.Sqrt, ["w_m2"], ["w_m2"])
            P.op("dve", lambda e: e.reciprocal(out=rm_, in_=m2_), reads=["w_m2"], writes=["w_rm"])
            tt("dve", re, re, rm_, ALU.mult, hre + ["w_rm"], hre)
            tt("dve", im, im, rm_, ALU.mult, hre + ["w_rm"], hre)

        def square(ore, oim, re, im, hin, hout):
            tt("dve", t1_, re, re, ALU.mult, hin, ["w_t1"])
            tt("dve", t2_, im, im, ALU.mult, hin, ["w_t2"])
            tt("dve", m2_, re, im, ALU.mult, hin, ["w_m2"])
            tt("dve", ore, t1_, t2_, ALU.subtract, ["w_t1", "w_t2"], hout)
            ts("dve", oim, m2_, 2.0, None, ALU.mult, None, ["w_m2"], hout)

        cp("dve", cre_, Apow[:, 0, 0, :], [], ["w_c"])
        cp("dve", cim_, Apow[:, 0, 1, :], [], ["w_c"])
        normalize(cre_, cim_, ["w_c"])
        for it in range(3):
            if it < 2:
                square(Eh[:, 8, 0, :], Eh[:, 8, 1, :], cre_, cim_, ["w_c"], ["w_d"])
                normalize(Eh[:, 8, 0, :], Eh[:, 8, 1, :], ["w_d"])
                cp("dve", cre_, Eh[:, 8, 0, :], ["w_d"], ["w_c"])
                cp("dve", cim_, Eh[:, 8, 1, :], ["w_d"], ["w_c"])
            else:
                square(Eh[:, 0, 0, :], Eh[:, 0, 1, :], cre_, cim_, ["w_c"], [("Eh", 0)])
                normalize(Eh[:, 0, 0, :], Eh[:, 0, 1, :], [("Eh", 0)])
        for l in range(1, 9):
            square(Eh[:, l, 0, :], Eh[:, l, 1, :], Eh[:, l - 1, 0, :], Eh[:, l - 1, 1, :], [("Eh", l - 1)], [("Eh", l)])
            normalize(Eh[:, l, 0, :], Eh[:, l, 1, :], [("Eh", l)])
        tt("dve", r8[:], Apow[:, 0, 2, :], Apow[:, 0, 2, :], ALU.mult, [], ["r8"])
        tt("dve", r8[:], r8[:], r8[:], ALU.mult, ["r8"], ["r8"])
        tt("dve", r8[:], r8[:], r8[:], ALU.mult, ["r8"], ["r8"])
        EHALL = [("Eh", l) for l in range(9)]

        def s5_part2(T):
            bl = {}
            for part in range(2):
                for ql in range(4):
                    bl[(ql, part)] = nb()
                for s in range(8):
                    for ql in range(4):
                        bk, bh = bl[(ql, part)]
                        rows = slice(32 * ql, 32 * ql + 32)
                        mm(bk[:], W2c[rows, T, s, part, :], s5in[rows, T, s, :], s == 0, s == 7,
                           [("s5in", T), "W2c"], [bh], tile_position=(32 * ql, 0))
            return bl

        def s5_tables(T):
            Ere, Eim = Etab
            mset("dve", Ere[:, :, 0:1], 1.0, ["Etab"])
            mset("dve", Eim[:, :, 0:1], 0.0, ["Etab"])
            for l in range(9):
                n = 1 << l
                ere = Eh[:, l, 0, 4 * T:4 * T + 4].unsqueeze(2).to_broadcast([128, 4, n])
                eim = Eh[:, l, 1, 4 * T:4 * T + 4].unsqueeze(2).to_broadcast([128, 4, n])
                a0, a1 = et[0][:, :, 0:n], et[1][:, :, 0:n]
                tt("dve", a0, Ere[:, :, 0:n], ere, ALU.mult, ["Etab", ("Eh", l)], ["et0"])
                tt("dve", a1, Eim[:, :, 0:n], eim, ALU.mult, ["Etab", ("Eh", l)], ["et1"])
                tt("dve", Ere[:, :, n:2 * n], a0, a1, ALU.subtract, ["et0", "et1"], ["Etab"])
                tt("dve", a0, Ere[:, :, 0:n], eim, ALU.mult, ["Etab", ("Eh", l)], ["et0"])
                tt("dve", a1, Eim[:, :, 0:n], ere, ALU.mult, ["Etab", ("Eh", l)], ["et1"])
                tt("dve", Eim[:, :, n:2 * n], a0, a1, ALU.add, ["et0", "et1"], ["Etab"])

        def s5_scan(T, bl):
            x0 = X0[T % 2]
            Ere, Eim = Etab
            seqs = []
            for ql in range(4):
                q = 4 * T + ql
                st = ql % 2
                t1, t2, zre, zim, wre, wim = [b[:] for b in SC[st]]
                H = lambda nm, st=st: ("sc", st, nm)
                (bre_, hre_), (bim_, him_) = bl[(ql, 0)], bl[(ql, 1)]
                er, ei = Ere[:, ql, :], Eim[:, ql, :]
                rb = r8[:, q:q + 1].to_broadcast([128, NCH])
                n1 = NCH - 1
                ops = [
                    ("tt", t1, bre_[:], er, ALU.mult, [hre_, "Etab"], [H("t1")]),
                    ("tt", t2, bim_[:], ei, ALU.mult, [him_, "Etab"], [H("t2")]),
                    ("tt", zre, t1, t2, ALU.add, [H("t1"), H("t2")], [H("zre")]),
                    ("tt", t1, bim_[:], er, ALU.mult, [him_, "Etab"], [H("t1")]),
                    ("tt", t2, bre_[:], ei, ALU.mult, [hre_, "Etab"], [H("t2")]),
                    ("tt", zim, t1, t2, ALU.subtract, [H("t1"), H("t2")], [H("zim")]),
                    ("scan", wre, rb, zre, [H("zre"), "r8"], [H("wre")]),
                    ("scan", wim, rb, zim, [H("zim"), "r8"], [H("wim")]),
                    ("tt", t1, wre, er, ALU.mult, [H("wre"), "Etab"], [H("t1")]),
                    ("tt", t2, wim, ei, ALU.mult, [H("wim"), "Etab"], [H("t2")]),
                    ("tt", x0[:, ql, 0, 1:NCH], t1[:, 0:n1], t2[:, 0:n1], ALU.subtract, [H("t1"), H("t2")], [("X0", T % 2, ql, 0)]),
                    ("tt", t1, wre, ei, ALU.mult, [H("wre"), "Etab"], [H("t1")]),
                    ("tt", t2, wim, er, ALU.mult, [H("wim"), "Etab"], [H("t2")]),
                    ("tt", x0[:, ql, 1, 1:NCH], t1[:, 0:n1], t2[:, 0:n1], ALU.add, [H("t1"), H("t2")], [("X0", T % 2, ql, 1)]),
                ]
                seqs.append(ops)
            for grp in ((0, 1), (2, 3)):
                for k in range(len(seqs[0])):
                    for ql in grp:
                        o = seqs[ql][k]
                        if o[0] == "tt":
                            tt("dve", o[1], o[2], o[3], o[4], o[5], o[6])
                        else:
                            P.op("dve", lambda e, o=o: e.tensor_tensor_scan(out=o[1], data0=o[2], data1=o[3], initial=0.0,
                                                                            op0=ALU.mult, op1=ALU.add), reads=o[4], writes=o[5])

        def s5_part13(T):
            x0 = X0[T % 2]
            for t in range(8):
                bk, bh = nb()
                for s in range(t + 1):
                    mm(bk[:], BD[:, T, t - s, :], s5in[:, T, s, :], s == 0, False, [("BD", T), ("s5in", T)], [bh])
                n = 0
                for part in range(2):
                    for ql in range(4):
                        q = 4 * T + ql
                        n += 1
                        mm(bk[32 * ql:32 * ql + 32, :], W3c[:, q, t, part, :], x0[:, ql, part, :], False, part == 1,
                           ["W3c", ("X0", T % 2, ql, part)], [bh], tile_position=(0, 32 * ql))
                cp("act", yv[:, T, :, t], bk[:], [bh], [("y", T)])

        for T in range(4):
            bl = s5_part2(T)
            s5_tables(T)
            s5_scan(T, bl)
            if T > 0:
                s5_part13(T - 1)
        s5_part13(3)
        if "s5" in dbg:
            dump("y", ybuf[:], [("y", T) for T in range(4)], [128, 4, SEQ], BF16)
        P.emit_phase()
    if stop_after == "s5":
        return
    mid.close()
    build_pass2(nc, P, I, WC, out, h_scr, uT_scr, es, sb, banks, nb, mm, act, tt, ts, stt, cp, mset, dma, dump,
                identf, identb, gains, ybuf, dbg, stop_after, ntiles, WStream, rmsnorm_T, ffn, ffn_specs,
                spec_cols, spec_rows, kp, hsv, outv, UT)


def build_pass2(nc, P, I, WC, out, h_scr, uT_scr, es, sb, banks, nb, mm, act, tt, ts, stt, cp, mset, dma, dump,
                identf, identb, gains, ybuf, dbg, stop_after, ntiles, WStream, rmsnorm_T, ffn, ffn_specs,
                spec_cols, spec_rows, kp, hsv, outv, UT):
    with ExitStack() as p3:
        B = {}
        hbuf = [sb(p3, "h2a", [128, 4, D], F32), sb(p3, "h2b", [128, 4, D], F32)]
        B["xn"] = sb(p3, "xn2", [128, 4, D], BF16)
        B["uT"] = sb(p3, "uT2", [128, 8, NT], BF16)
        B["ss"] = sb(p3, "ss2", [128, 4], F32)
        B["rstd"] = sb(p3, "rstd2", [128, 4], F32)
        B["sg"] = sb(p3, "sg2", [128, 2, NT], F32)
        big = sb(p3, "big2", [128, 24 * NT], BF16)
        B["gT"] = big[:, 0:NF * NT].rearrange("p (f t) -> p f t", f=NF)
        uT = B["uT"]
        ring = [sb(p3, "ring2_%d" % i, [128, 4096], BF16) for i in range(3)]
        gfin = sb(p3, "gfin", [128, D], F32)
        wal = sb(p3, "wal", [128, 8, 16], BF16)
        wglu = sb(p3, "wglu", [128, 4, 512], BF16)
        bglu = sb(p3, "bglu", [128, 4], F32)
        gnorm = sb(p3, "gnorm", [128, 4], F32)
        wup = sb(p3, "wup", [33, 256], F32)
        Tri = sb(p3, "Tri", [128, 128], F32)
        Um = sb(p3, "Um", [128, 128], F32)
        MaskT = sb(p3, "MaskT", [128, 128], F32)
        onesb = sb(p3, "onesb", [128, 128], BF16)
        qT = sb(p3, "qT", [128, 2, NT], F32)
        kT = sb(p3, "kT", [128, 2, NT], F32)
        ktok = sb(p3, "ktok", [128, 4, 256], F32)
        vtok = sb(p3, "vtok", [128, 4, 512], BF16)
        rT = sb(p3, "rT", [128, 4, NT], BF16)
        alT = sb(p3, "alT", [33, NT], F32)
        sgt = big[:, 0:16 * NT].rearrange("p (f t) -> p f t", f=16)
        yglaT = sb(p3, "yglaT", [128, 4, NT], BF16)
        ys5T = sb(p3, "ys5T", [128, 4, NT], BF16)
        zT = sb(p3, "zT", [128, 4, NT], BF16)
        mT = big[:, 16 * NT:24 * NT].rearrange("p (f t) -> p f t", f=8)
        mtmp = B["sg"]
        def two(name, shape, dt):
            return [sb(p3, "%s_%d" % (name, c_), shape, dt) for c_ in range(2)]
        ex = two("ex", [128, 256], F32)
        nl = two("nl", [128, 256], F32)
        Eq = two("Eq", [128, 2, 128], F32)
        Ek = two("Ek", [128, 2, 128], F32)
        Eke = ex
        qs = two("qs", [128, 2, 128], BF16)
        ksz = two("ksz", [128, 4, 128], BF16)
        kend = two("kend", [128, 256], BF16)
        PT = two("PT", [128, 4, 128], BF16)
        Sf = sb(p3, "Sf", [128, 4, 128], F32)
        Sbz = sb(p3, "Sbz", [128, 4, 128], BF16)
        osq = two("osq", [128, 512], BF16)
        rsd = two("rsd", [128, 512], F32)
        ot = two("ot", [128, 512], F32)

        print("SBUF remaining pass2:", nc.sbuf_bytes_remaining)
        dma("sp", gfin[:], I["final_norm"].rearrange("(o d) -> o d", o=1).to_broadcast([128, D]), [], ["c0"], semkey="small", slow=True)
        dma("sp", wal[:], kp(WC["w_in"])[:, :, C_AL:C_AL + 16], [], ["c1"], semkey="small", slow=True)
        dma("sp", wglu[:], kp(WC["s5_glu_w"]), [], ["c2"], semkey="small")
        dma("sp", bglu[:], I["s5_glu_b"].rearrange("(m p) -> p m", p=128), [], ["c3"], semkey="small", slow=True)
        dma("sp", gnorm[:], I["gla_out_norm"].rearrange("(m p) -> p m", p=128), [], ["c4"], semkey="small", slow=True)
        mset("pool", wup[:], 0.0, ["wup"])
        mset("pool", alT[:], 0.0, ["alT"])
        mset("pool", alT[32:33, :], 1.0, ["alT"])
        mset("pool", Sf[:], 0.0, ["Sf"])
        mset("pool", Sbz[:], 0.0, ["Sbz"])
        mset("pool", ksz[0][:], 0.0, ["ksz0"])
        mset("pool", ksz[1][:], 0.0, ["ksz1"])
        mset("pool", onesb[:], 1.0 / 128, ["onesb"])
        P.emit_phase()
        dma("sp", wup[0:16, :], I["gla_a_up_w"], [], ["c5"], semkey="small")
        dma("sp", wup[32:33, :], I["gla_a_up_b"].rearrange("(o d) -> o d", o=1), [], ["c6"], semkey="small")
        mset("pool", Tri[:], -1.0 / 16, ["Tri"])
        P.op("pool", lambda e: e.affine_select(out=Tri[:], in_=Tri[:], pattern=[[1, 128]], compare_op=ALU.is_ge,
                                               fill=0.0, base=0, channel_multiplier=-1), reads=["Tri"], writes=["Tri"])
        mset("pool", Um[:], -1.0 / 16, ["Um"])
        P.op("pool", lambda e: e.affine_select(out=Um[:], in_=Um[:], pattern=[[-1, 128]], compare_op=ALU.is_ge,
                                               fill=0.0, base=-1, channel_multiplier=1), reads=["Um"], writes=["Um"])
        mset("pool", MaskT[:], 1.0, ["MaskT"])
        P.op("pool", lambda e: e.affine_select(out=MaskT[:], in_=MaskT[:], pattern=[[1, 128]], compare_op=ALU.is_ge,
                                               fill=0.0, base=0, channel_multiplier=-1), reads=["MaskT"], writes=["MaskT"])
        P.emit_phase()

        win = WC["w_in"]
        specs = []
        for i in range(ntiles):
            specs += [spec_cols(win, C_Q, 8), spec_cols(win, C_V, 8), spec_cols(win, C_R, 8),
                      spec_cols(win, C_GS, 8), spec_cols(win, C_GS + 512, 8),
                      spec_cols(win, C_GG, 8), spec_cols(win, C_GG + 512, 8),
                      spec_rows(WC["proj_s5"], 0), spec_rows(WC["proj_gla"], 0),
                      spec_rows(WC["w_out"], 0), spec_rows(WC["w_out"], 4)]
            specs += ffn_specs("ffn2")
        ws = WStream([r[:] for r in ring], specs, dma, pf=2)

        def proj_fm(slot, sh, m, nk=8):
            v = slot.rearrange("p (k c) -> p k c", k=nk)
            bk, bh = nb()
            for k in range(nk):
                mm(bk[:], v[:, k, m * 128:(m + 1) * 128], uT[:, k, :], k == 0, k == nk - 1, sh + [("uT", k)], [bh])
            return bk, bh

        def load_tile(i):
            for sub in range(4):
                dma("pool", hbuf[i % 2][:, sub, :], hsv[i, sub], [], [("h", i % 2, sub)])
            for half in range(2):
                dma("pool", uT[:, half * 4:(half + 1) * 4, :],
                    uT_scr[i][:, half * 2048:(half + 1) * 2048].rearrange("p (k t) -> p k t", k=4),
                    [], [("uT", k) for k in range(half * 4, half * 4 + 4)], semkey=("uTld", half))

        load_tile(0)
        for i in range(ntiles):
            t0 = i * NT
            h = hbuf[i % 2]
            B["h"] = h
            B["hh"] = lambda sub, par=i % 2: ("h", par, sub)
            B["uTh"] = "uT"
            HH = B["hh"]
            slot, sh = ws.next()
            v8 = slot.rearrange("p (k c) -> p k c", k=8)
            for m in range(2):
                bk, bh = proj_fm(slot, sh, m)
                cp("act", qT[:, m, :], bk[:], [bh], [("qT", m)])
            for m in range(2):
                bk, bh = proj_fm(slot, sh, 2 + m)
                cp("dve", kT[:, m, :], bk[:], [bh], [("kT", m)])
            for sub in range(4):
                bk, bh = nb()
                for k in range(8):
                    mm(bk[:, 0:256], uT[:, k, sub * 128:(sub + 1) * 128], v8[:, k, 256:512], k == 0, k == 7,
                       sh + [("uT", k)], [bh])
                cp("act" if sub % 2 else "dve", ktok[:, sub, :], bk[:, 0:256], [bh], [("ktok", sub)])
            slot, sh = ws.next()
            v8 = slot.rearrange("p (k c) -> p k c", k=8)
            for sub in range(4):
                bk, bh = nb()
                for k in range(8):
                    mm(bk[:], uT[:, k, sub * 128:(sub + 1) * 128], v8[:, k, :], k == 0, k == 7, sh + [("uT", k)], [bh])
                cp("act" if sub % 2 else "dve", vtok[:, sub, :], bk[:], [bh], [("vtok", sub)])
            bk, bh = nb()
            for k in range(8):
                mm(bk[0:16, :], wal[:, k, :], uT[:, k, :], k == 0, k == 7, [("uT", k)], [bh])
            cp("dve", alT[0:16, :], bk[0:16, :], [bh], ["alT"])
            fillers = []
            fstate = {}

            def r_filler(m):
                if m == 0:
                    fstate["slot"], fstate["sh"] = ws.next()
                bk, bh = proj_fm(fstate["slot"], fstate["sh"], m)
                act(rT[:, m, :], bk[:], AF.Silu, [bh], [("rT", m)])

            def g_filler(gidx, m):
                if m == 0:
                    fstate["slot"], fstate["sh"] = ws.next()
                bk, bh = proj_fm(fstate["slot"], fstate["sh"], m)
                act(sgt[:, gidx * 4 + m, :], bk[:], AF.Sigmoid, [bh], [("sgt", gidx * 4 + m)])

            for m in range(4):
                fillers.append(lambda m=m: r_filler(m))
            for gidx in range(4):
                for m in range(4):
                    fillers.append(lambda gidx=gidx, m=m: g_filler(gidx, m))

            for m in range(4):
                act(zT[:, m, :], ybuf[:, m, t0:t0 + NT], AF.Gelu_apprx_tanh, [], [("zT", m)])

            def glu_filler(mo):
                bk, bh = nb()
                for k in range(4):
                    mm(bk[:], wglu[:, k, mo * 128:(mo + 1) * 128], zT[:, k, :], k == 0, k == 3, [("zT", k)], [bh])
                act(mtmp[:, 0, :], bk[:], AF.Sigmoid, [bh], [("mtmp", 0)], bias=bglu[:, mo:mo + 1])
                tt("dve", ys5T[:, mo, :], zT[:, mo, :], mtmp[:, 0, :], ALU.mult, [("zT", mo), ("mtmp", 0)], [("ys5", mo)])

            def ps5_filler(dj):
                if dj == 0:
                    fstate["p5"], fstate["p5h"] = ws.next()
                p5 = fstate["p5"].rearrange("p (k c) -> p k c", k=4)
                bA, hA = nb()
                for k in range(4):
                    mm(bA[:], p5[:, k, dj * 128:(dj + 1) * 128], ys5T[:, k, :], k == 0, k == 3, fstate["p5h"] + [("ys5", k)], [hA])
                tt("dve", mT[:, dj, :], sgt[:, dj, :], bA[:], ALU.mult, [("sgt", dj), hA], [("mT", dj)])

            for mo in range(4):
                fillers.append(lambda mo=mo: glu_filler(mo))
            for dj in range(8):
                fillers.append(lambda dj=dj: ps5_filler(dj))

            def fill(n=1):
                for _ in range(n):
                    if fillers:
                        fillers.pop(0)()

            ctx = {}

            def st1(j):
                c = j % 2
                cs_ = slice(j * 128, (j + 1) * 128)
                bk, bh = nb()
                mm(bk[:, 0:256], alT[0:33, cs_], wup[0:33, :], True, True, ["alT", "wup"], [bh])
                ctx[j] = {"la": (bk, bh)}
                fill(2)

            def st2(j):
                c = j % 2
                bk, bh = ctx[j]["la"]
                act(ex[c][:], bk[:, 0:256], AF.Exp, [bh], [("ex", c)], scale=-1.0)
                act(nl[c][:], ex[c][:], AF.Ln, [("ex", c)], [("nl", c)], bias=1.0)
                bk, bh = nb()
                for hp in range(2):
                    mm(bk[:, hp * 128:(hp + 1) * 128], nl[c][:, hp * 128:(hp + 1) * 128], Tri[:], True, True,
                       [("nl", c), "Tri"], [bh])
                mm(bk[:, 256:512], Um[:], nl[c][:], True, True, [("nl", c), "Um"], [bh])
                ctx[j]["bc"] = (bk, bh)
                fill(2)

            def st3(j):
                c = j % 2
                cs_ = slice(j * 128, (j + 1) * 128)
                bk, bh = ctx[j]["bc"]
                bcT = bk[:, 0:256].rearrange("p (a b) -> p a b", a=2)
                act(Eq[c][:], bcT, AF.Exp, [bh], [("Eq", c)])
                act(Ek[c][:], bcT, AF.Exp, [bh], [("Ek", c)], scale=-1.0)
                act(Eke[c][:], bk[:, 256:512], AF.Exp, [bh, ("ex", c)], [("ex", c)])
                for hp in range(2):
                    stt("dve", qs[c][:, hp, :], qT[:, hp, cs_], 0.125, Eq[c][:, hp, :], ALU.mult, ALU.mult,
                        [("qT", hp), ("Eq", c)], [("qs", c, hp)])
                for hd in range(4):
                    hp, hi = hd // 2, hd % 2
                    rows = slice(hi * 64, hi * 64 + 64)
                    tt("pool", ksz[c][rows, hd, :], kT[rows, hp, cs_], Ek[c][rows, hp, :], ALU.mult,
                       [("kT", hp), ("Ek", c)], [("ksz", c, hd)])
                tt("pool", kend[c][:], ktok[:, j, :], Eke[c][:], ALU.mult, [("ktok", j), ("ex", c)], [("kend", c)])
                bk, bh = nb()
                for hd in range(4):
                    hp = hd // 2
                    mm(bk[:, hd * 128:(hd + 1) * 128], ksz[c][:, hd, :], qs[c][:, hp, :], True, True,
                       [("ksz", c, hd), ("qs", c, hp)], [bh])
                ctx[j]["sc"] = (bk, bh)
                fill(2)

            def st4(j):
                c = j % 2
                bk, bh = ctx[j]["sc"]
                tt("dve", PT[c][:], bk[:].rearrange("p (a b) -> p a b", a=4),
                   MaskT[:].unsqueeze(1).to_broadcast([128, 4, 128]), ALU.mult, [bh, "MaskT"], [("PT", c)])
                bo, bho = nb(reserve=True)
                for hd in range(4):
                    hp = hd // 2
                    mm(bo[:, hd * 128:(hd + 1) * 128], vtok[:, j, hd * 128:(hd + 1) * 128], PT[c][:, hd, :], True, False,
                       [("vtok", j), ("PT", c)], [bho])
                    mm(bo[:, hd * 128:(hd + 1) * 128], Sbz[:, hd, :], qs[c][:, hp, :], False, True,
                       [("Sbz", hd), ("qs", c, hp)], [bho])
                ctx[j]["o"] = (bo, bho)
                bk, bh = nb()
                for hd in range(4):
                    hp = hd // 2
                    mm(bk[:, hd * 128:(hd + 1) * 128], kend[c][:, hp * 128:(hp + 1) * 128], vtok[:, j, hd * 128:(hd + 1) * 128],
                       True, True, [("kend", c), ("vtok", j)], [bh])
                fill()
                for hd in range(4):
                    hp, hi = hd // 2, hd % 2
                    rows = slice(hi * 64, hi * 64 + 64)
                    stt("dve", Sf[rows, hd, :], Sf[rows, hd, :], Eq[c][rows, hp, 127:128], bk[rows, hd * 128:(hd + 1) * 128],
                        ALU.mult, ALU.add, [("Sf", hd), ("Eq", c), bh], [("Sf", hd)])
                    cp("pool", Sbz[rows, hd, :], Sf[rows, hd, :], [("Sf", hd)], [("Sbz", hd)])

            def st5(j):
                c = j % 2
                bo, bho = ctx[j]["o"]
                act(osq[c][:], bo[:], AF.Square, [bho], [("osq", c)])
                bk, bh = nb()
                mm(bk[:], onesb[:], osq[c][:], True, True, ["onesb", ("osq", c)], [bh])
                ctx[j]["ms"] = (bk, bh)
                fill()

            def st6(j):
                c = j % 2
                cs_ = slice(j * 128, (j + 1) * 128)
                bo, bho = ctx[j]["o"]
                bk, bh = ctx[j]["ms"]
                act(rsd[c][:], bk[:], AF.Ln, [bh], [("rsd", c)], bias=EPS)
                act(rsd[c][:], rsd[c][:], AF.Exp, [("rsd", c)], [("rsd", c)], scale=-0.5)
                tt("dve", ot[c][:], bo[:], rsd[c][:], ALU.mult, [bho, ("rsd", c)], [("ot", c)])
                for hd in range(4):
                    stt("dve", yglaT[:, hd, cs_], ot[c][:, hd * 128:(hd + 1) * 128], gnorm[:, hd:hd + 1], rT[:, hd, cs_],
                        ALU.mult, ALU.mult, [("ot", c), ("rT", hd)], [("ygla", hd)])
                nb.release(bho)

            if "nogla" not in dbg:
                for pair in ((0, 1), (2, 3)):
                    for st in (st1, st2, st3, st4, st5, st6):
                        for j in pair:
                            st(j)

            fill(len(fillers))
            gls, glh = ws.next()
            pg = gls.rearrange("p (k c) -> p k c", k=4)
            for dj in range(8):
                bB, hB = nb()
                for k in range(4):
                    mm(bB[:], pg[:, k, dj * 128:(dj + 1) * 128], yglaT[:, k, :], k == 0, k == 3, glh + [("ygla", k)], [hB])
                tt("dve", mtmp[:, dj % 2, :], sgt[:, 8 + dj, :], bB[:], ALU.mult, [("sgt", 8 + dj), hB], [("mtmp", dj % 2)])
                tt("pool", mT[:, dj, :], mT[:, dj, :], mtmp[:, dj % 2, :], ALU.add, [("mT", dj), ("mtmp", dj % 2)], [("mT", dj)])
            (wo0, wh0), (wo1, wh1) = ws.next(2)
            wov = [wo0.rearrange("p (k c) -> p k c", k=4), wo1.rearrange("p (k c) -> p k c", k=4)]
            for sub in range(4):
                for half in range(2):
                    bk, bh = nb()
                    for k in range(8):
                        mm(bk[:], mT[:, k, sub * 128:(sub + 1) * 128], wov[k // 4][:, k % 4, half * 512:(half + 1) * 512],
                           k == 0, k == 7, wh0 + wh1 + [("mT", k)], [bh])
                    hv = h[:, sub, half * 512:(half + 1) * 512]
                    tt("dve", hv, hv, bk[:], ALU.add, [bh, HH(sub)], [HH(sub)])
            if "mix" in dbg and i == 0:
                dump("hmix", h[:], [HH(s_) for s_ in range(4)], [128, 4, D], F32)
                dump("ygla", yglaT[:], [("ygla", s_) for s_ in range(4)], [128, 4, NT], BF16)
                dump("ys5", ys5T[:], [("ys5", s_) for s_ in range(4)], [128, 4, NT], BF16)

            rmsnorm_T(B, 2)
            if "noffn2" in dbg:
                ws.pos += 17
            else:
                ffn(B, ws, after_h13=(lambda i=i: load_tile(i + 1)) if i + 1 < ntiles else None)
            ss, rstd, xn = B["ss"], B["rstd"], B["xn"]
            mset("dve", ss[:], 0.0, ["ss"])
            for sub in range(4):
                act(xn[:, sub, :], h[:, sub, :], AF.Square, [HH(sub), "ss"], [("xn", sub), ("ssv", sub)],
                    accum_out=ss[:, sub:sub + 1])
            ts("dve", rstd[:], ss[:], 1.0 / D, EPS, ALU.mult, ALU.add, [("ssv", s_) for s_ in range(4)], ["rstd"])
            act(rstd[:], rstd[:], AF.Sqrt, ["rstd"], ["rstd"])
            P.op("dve", lambda e: e.reciprocal(out=rstd[:], in_=rstd[:]), reads=["rstd"], writes=["rstd"])
            for sub in range(4):
                stt("dve", h[:, sub, :], h[:, sub, :], rstd[:, sub:sub + 1], gfin[:],
                    ALU.mult, ALU.mult, [HH(sub), "rstd"], [HH(sub)])
                dma("pool", outv[i, sub], h[:, sub, :], [HH(sub)], [("out", sub)], semkey=("out", sub))
        P.emit_phase()


def kernel(**inputs):
    nc = build_program()
    nb_ = int(np.asarray(inputs["x"]).shape[0])
    in_maps = []
    for b in range(nb_):
        m = {"x": np.ascontiguousarray(np.asarray(inputs["x"])[b], dtype=np.float32)}
        for k, shp in PARAM_SHAPES.items():
            m[k] = np.ascontiguousarray(np.asarray(inputs[k], dtype=np.float32).reshape(shp))
        in_maps.append(m)
    res = run_bass_kernel_spmd(nc, in_maps, core_ids=list(range(nb_)))
    return np.stack([np.asarray(r["out"], dtype=np.float32) for r in res.results], axis=0)
```

```python
from concourse.bass_utils import run_bass_kernel_spmd
import numpy as np
import concourse.bass as bass
import concourse.mybir as mybir

F32 = mybir.dt.float32
BF16 = mybir.dt.bfloat16
AF = mybir.ActivationFunctionType
ALU = mybir.AluOpType
AX = mybir.AxisListType

ENGS = ("pe", "act", "dve", "pool", "sp")
CE = ("pe", "act", "dve", "pool")


class Op:
    __slots__ = ("eng", "fn", "reads", "writes", "dma", "semkey", "idx", "signal",
                 "sigidx", "deps", "dmacount")

    def __init__(self, eng, fn, reads, writes, dma, semkey):
        self.eng = eng
        self.fn = fn
        self.reads = tuple(reads)
        self.writes = tuple(writes)
        self.dma = dma
        self.semkey = semkey
        self.signal = False
        self.sigidx = 0
        self.deps = ()
        self.dmacount = 0


class Prog:
    def __init__(self, nc, es, same_engine_sync=True):
        self.nc = nc
        self.es = es
        self.ops = []
        self.same_engine_sync = same_engine_sync
        self.csem = {e: es.enter_context(nc.semaphore("c_" + e)) for e in CE}
        self.dsem = {}
        self.ccount = {e: 0 for e in CE}
        self.dcount = {}
        self.waited = {e: {} for e in ENGS}
        self.nops = 0

    def op(self, eng, fn, reads=(), writes=()):
        o = Op(eng, fn, reads, writes, False, None)
        self.ops.append(o)
        return o

    def dma(self, eng, fn, reads=(), writes=(), semkey=None):
        assert eng in ("sp", "act", "pool")
        if semkey is None:
            semkey = writes[0] if writes else reads[0]
        o = Op(eng, fn, reads, writes, True, semkey)
        self.ops.append(o)
        return o

    def _analyze(self):
        last_writer = {}
        readers = {}
        for i, o in enumerate(self.ops):
            o.idx = i
            deps = {}
            for h in o.reads:
                w = last_writer.get(h)
                if w is not None:
                    deps[w.idx] = w
            for h in o.writes:
                w = last_writer.get(h)
                if w is not None:
                    deps[w.idx] = w
                for r in readers.get(h, ()):
                    deps[r.idx] = r
            deps.pop(i, None)
            o.deps = tuple(deps.values())
            for h in o.reads:
                lst = readers.setdefault(h, [])
                if not o.dma:
                    lst[:] = [r for r in lst if r.dma or r.eng != o.eng]
                lst.append(o)
            for h in o.writes:
                last_writer[h] = o
                readers[h] = []
            if o.dma:
                if o.semkey not in self.dsem:
                    self.dsem[o.semkey] = self.es.enter_context(
                        self.nc.semaphore("d_%d" % len(self.dsem)))
                    self.dcount[o.semkey] = 0
                c = self.dcount[o.semkey] + 16
                self.dcount[o.semkey] = c
                o.dmacount = c
        for o in self.ops:
            for d in o.deps:
                if d.dma:
                    continue
                if d.eng == o.eng and not o.dma:
                    if d.eng == "pe" or not self.same_engine_sync:
                        continue
                d.signal = True
        last = {}
        for o in self.ops:
            if not o.dma:
                last[o.eng] = o
        for o in last.values():
            o.signal = True
        for o in self.ops:
            if not o.dma and o.signal:
                self.ccount[o.eng] += 1
                o.sigidx = self.ccount[o.eng]

    def emit_phase(self):
        nc = self.nc
        self._analyze()
        ops = self.ops
        self.nops += len(ops)
        same_sync = self.same_engine_sync
        csem, dsem = self.csem, self.dsem
        cend = dict(self.ccount)
        dend = dict(self.dcount)

        def stream(engname):
            def body(eng):
                waited = self.waited[engname]
                for o in ops:
                    if o.eng != engname:
                        continue
                    need = {}
                    for d in o.deps:
                        if d.dma:
                            key = ("d", d.semkey)
                            val = d.dmacount
                        else:
                            if d.eng == engname and not o.dma:
                                if engname == "pe" or not same_sync:
                                    continue
                            key = ("c", d.eng)
                            val = d.sigidx
                        if val > need.get(key, 0):
                            need[key] = val
                    for key, val in need.items():
                        if waited.get(key, 0) >= val:
                            continue
                        waited[key] = val
                        sem = dsem[key[1]] if key[0] == "d" else csem[key[1]]
                        eng.wait_ge(sem, val)
                    ins = o.fn(eng)
                    if o.dma:
                        ins.then_inc(dsem[o.semkey], 16)
                    elif o.signal:
                        ins.then_inc(csem[o.eng], 1)
                for e2 in CE:
                    key = ("c", e2)
                    if cend[e2] > waited.get(key, 0):
                        waited[key] = cend[e2]
                        eng.wait_ge(csem[e2], cend[e2])
                for k2, v2 in dend.items():
                    key = ("d", k2)
                    if v2 > waited.get(key, 0):
                        waited[key] = v2
                        eng.wait_ge(dsem[k2], v2)
            return body

        with nc.Block() as block:
            block.tensor(stream("pe"))
            block.scalar(stream("act"))
            block.vector(stream("dve"))
            block.gpsimd(stream("pool"))
            block.sync(stream("sp"))
        self.ops = []

from contextlib import ExitStack
import math

D = 1024
FF = 2816
NF = 22
SEQ = 4096
NT = 512
NTILES = 8
NCH = 512
INC = 4112
C_S5, C_Q, C_K, C_V, C_R, C_AL, C_GS, C_GG = 0, 512, 768, 1024, 1536, 2048, 2064, 3088
EPS = 1e-6
PI = math.pi

PARAM_SHAPES = {
    "ffn1_norm": [D], "ffn1_w1": [D, FF], "ffn1_w3": [D, FF], "ffn1_w2": [FF, D],
    "mix_norm": [D], "w_in": [D, INC],
    "s5_lambda_re": [32, 64], "s5_lambda_im": [32, 64], "s5_log_dt": [32],
    "s5_b_re": [32, 64, 16], "s5_b_im": [32, 64, 16], "s5_c_re": [32, 16, 64], "s5_c_im": [32, 16, 64],
    "s5_d": [32, 16], "s5_glu_w": [512, 512], "s5_glu_b": [512],
    "gla_a_up_w": [16, 256], "gla_a_up_b": [256], "gla_out_norm": [512],
    "proj_s5": [512, D], "proj_gla": [512, D], "w_out": [D, D],
    "ffn2_norm": [D], "ffn2_w1": [D, FF], "ffn2_w3": [D, FF], "ffn2_w2": [FF, D],
    "final_norm": [D],
}


def build_program(dbg=(), stop_after=None, ntiles=NTILES):
    nc = bass.Bass("TRN2", target_bir_lowering=False)
    I = {}
    I["x"] = nc.dram_tensor("x", [SEQ, D], F32, kind="ExternalInput").ap()
    for k, shp in PARAM_SHAPES.items():
        I[k] = nc.dram_tensor(k, shp, F32, kind="ExternalInput").ap()
    out = nc.dram_tensor("out", [SEQ, D], F32, kind="ExternalOutput").ap()

    def dscr(name, shape, dt):
        kind = "ExternalOutput" if name in dbg else "Internal"
        return nc.dram_tensor(name, shape, dt, kind=kind).ap()

    WC = {}
    for nm in ("ffn1_w1", "ffn1_w3", "ffn1_w2", "ffn2_w1", "ffn2_w3", "ffn2_w2", "w_in",
               "s5_glu_w", "proj_s5", "proj_gla", "w_out"):
        WC[nm] = dscr(nm + "_c", PARAM_SHAPES[nm], BF16)
    h_scr = dscr("h_scr", [SEQ, D], F32)
    uT_scr = dscr("uT_scr", [NTILES, 128, 8 * NT], BF16)
    dbgout = {}

    with ExitStack() as es:
        import os as _os
        P = Prog(nc, es, same_engine_sync=not _os.environ.get("NO_SES"))

        def sb(scope, name, shape, dt):
            return scope.enter_context(nc.sbuf_tensor(name, shape, dt))

        banks = [es.enter_context(nc.psum_tensor("bank%d" % i, [128, 512], F32)) for i in range(8)]
        bstate = {"i": 0}

        def nb():
            i = bstate["i"]
            bstate["i"] = (i + 1) % 8
            return banks[i], ("ps", i)

        def mm(out_, lhsT, rhs, start, stop, r, w, **kw):
            P.op("pe", lambda e: e.matmul(out_, lhsT=lhsT, rhs=rhs, start=start, stop=stop, **kw),
                 reads=r, writes=w)

        def act(out_, in_, func, r, w, **kw):
            P.op("act", lambda e: e.activation(out=out_, in_=in_, func=func, **kw), reads=r, writes=w)

        def tt(eng, out_, in0, in1, op, r, w):
            P.op(eng, lambda e: e.tensor_tensor(out=out_, in0=in0, in1=in1, op=op), reads=r, writes=w)

        def ts(eng, out_, in0, s1, s2, op0, op1, r, w):
            if s2 is None:
                P.op(eng, lambda e: e.tensor_scalar(out=out_, in0=in0, scalar1=s1, scalar2=None, op0=op0),
                     reads=r, writes=w)
            else:
                P.op(eng, lambda e: e.tensor_scalar(out=out_, in0=in0, scalar1=s1, scalar2=s2, op0=op0, op1=op1),
                     reads=r, writes=w)

        def stt(eng, out_, in0, scalar, in1, op0, op1, r, w):
            P.op(eng, lambda e: e.scalar_tensor_tensor(out=out_, in0=in0, scalar=scalar, in1=in1, op0=op0, op1=op1),
                 reads=r, writes=w)

        def cp(eng, out_, in_, r, w):
            if eng == "act":
                P.op("act", lambda e: e.activation(out=out_, in_=in_, func=AF.Copy), reads=r, writes=w)
            else:
                P.op(eng, lambda e: e.tensor_copy(out=out_, in_=in_), reads=r, writes=w)

        def mset(eng, ap, val, w):
            P.op(eng, lambda e: e.memset(ap, val), writes=w)

        def dma(eng, out_, in_, r, w, semkey=None, slow=False):
            if slow:
                return P.dma(eng, lambda e: e.dma_start(out=out_, in_=in_, allow_slow_non_contiguous=True),
                             reads=r, writes=w, semkey=semkey)
            return P.dma(eng, lambda e: e.dma_start(out=out_, in_=in_), reads=r, writes=w, semkey=semkey)

        def dump(name, ap, handles, shape, dt):
            d = nc.dram_tensor("dbg_" + name, shape, dt, kind="ExternalOutput").ap()
            dma("sp", d, ap, handles, [("dbg", name)], semkey="dbg")

        CAST1 = ("ffn1_w1", "ffn1_w3", "ffn1_w2", "w_in")

        def issue_casts(names, after=()):
            for nm in names:
                if "nocast" in dbg:
                    break
                R = PARAM_SHAPES[nm][0]
                for r0 in range(0, R, 128):
                    r1 = min(R, r0 + 128)
                    dma("pool", WC[nm][r0:r1, :], I[nm][r0:r1, :], list(after), [("cast", nm, r0)], semkey="cast")

        CAST2 = []
        for nm_ in WC:
            if nm_ not in CAST1:
                for r0_ in range(0, PARAM_SHAPES[nm_][0], 128):
                    CAST2.append((nm_, r0_, min(PARAM_SHAPES[nm_][0], r0_ + 128)))

        def issue_casts2(i):
            half = len(CAST2) // 2
            if i == -1:
                lst = CAST2[half:]
            elif i >= 2:
                per = (half + 5) // 6
                lst = CAST2[:half][(i - 2) * per:(i - 1) * per]
            else:
                lst = []
            for (nm, r0, r1) in lst:
                if "nocast" not in dbg:
                    dma("pool", WC[nm][r0:r1, :], I[nm][r0:r1, :], [], [("cast", nm, r0)], semkey="cast")

        identf = sb(es, "identf", [128, 128], F32)
        identb = sb(es, "identb", [128, 128], BF16)
        gains = sb(es, "gains", [128, 3, 8], F32)
        ybuf = sb(es, "ybuf", [128, 4, SEQ], BF16)

        mset("pool", identf[:], 1.0, ["identf"])
        P.op("pool", lambda e: e.affine_select(out=identf[:], in_=identf[:], pattern=[[-1, 128]],
                                               compare_op=ALU.is_equal, fill=0.0, base=0, channel_multiplier=1),
             reads=["identf"], writes=["identf"])
        cp("dve", identb[:], identf[:], ["identf"], ["identb"])
        for gi_, nm in enumerate(("ffn1_norm", "mix_norm", "ffn2_norm")):
            dma("sp", gains[:, gi_, :], I[nm].rearrange("(k p) -> p k", p=128), [], [("gains", gi_)],
                semkey="small", slow=True)

        mid = es.enter_context(ExitStack())
        s5in = sb(mid, "s5in", [128, 4, 8, NCH], BF16)
        W2c = sb(mid, "W2c", [128, 4, 8, 2, 128], BF16)
        W3c = sb(mid, "W3c", [128, 16, 8, 2, 32], BF16)
        BD = sb(mid, "BD", [128, 4, 8, 128], BF16)
        Apow = sb(mid, "Apow", [128, 9, 3, 16], F32)

        with ExitStack() as p0:
            def t16(name):
                return sb(p0, name, [128, 16], F32)

            def t1616(name):
                return sb(p0, name, [128, 16, 16], F32)

            lre, lim, ldt = t16("lre"), t16("lim"), t16("ldt")
            dtv, mag, th, thr, tmp, sn, cs = [t16(n) for n in ("dtv", "mag", "th", "thr", "tmp", "sn", "cs")]
            ar, ai, den, rden, nr, fr, fi, tq = [t16(n) for n in ("ar", "ai", "den", "rden", "nr", "fr", "fi", "tq")]
            bre, bim, cre, cim, bbre, bbim = [t1616(n) for n in ("bre", "bim", "cre", "cim", "bbre", "bbim")]
            mcre, mcim, t1, t2 = [t1616(n) for n in ("mcre", "mcim", "t1", "t2")]
            pw = sb(p0, "pw", [128, 9, 2, 16], F32)
            MPre = sb(p0, "MPre", [128, 16, 128], F32)
            MPim = sb(p0, "MPim", [128, 16, 128], F32)
            CPre = sb(p0, "CPre", [128, 16, 128], F32)
            CPnim = sb(p0, "CPnim", [128, 16, 128], F32)
            dvec = sb(p0, "dvec", [128, 4], F32)

            for gi in range(2):
                rows = slice(gi * 64, gi * 64 + 64)
                dma("sp", lre[rows, :], I["s5_lambda_re"].rearrange("(q g) p -> g p q", g=2)[gi], [], [("lre", gi)], semkey="small", slow=True)
                dma("sp", lim[rows, :], I["s5_lambda_im"].rearrange("(q g) p -> g p q", g=2)[gi], [], [("lim", gi)], semkey="small", slow=True)
                dma("sp", ldt[rows, :], I["s5_log_dt"].rearrange("(q g) -> g q", g=2)[gi:gi + 1, :].to_broadcast([64, 16]),
                    [], [("ldt", gi)], semkey="small", slow=True)
                dma("sp", bre[rows], I["s5_b_re"].rearrange("(q g) p h -> g p q h", g=2)[gi], [], [("bre", gi)], semkey="small", slow=True)
                dma("sp", bim[rows], I["s5_b_im"].rearrange("(q g) p h -> g p q h", g=2)[gi], [], [("bim", gi)], semkey="small", slow=True)
                for q_ in range(16):
                    dma("sp", cre[rows, q_, :], I["s5_c_re"][2 * q_ + gi].rearrange("h p -> p h"), [], [("cre", gi, q_)], semkey="small", slow=True)
                    dma("act", cim[rows, q_, :], I["s5_c_im"][2 * q_ + gi].rearrange("h p -> p h"), [], [("cim", gi, q_)], semkey="small2", slow=True)
            dma("sp", dvec[:], I["s5_d"].rearrange("(t g) h -> (g h) t", t=4), [], ["dvec"], semkey="small", slow=True)
            L = lambda n: [(n, 0), (n, 1)]
            if stop_after == "prep_loads":
                P.emit_phase()
                return nc
            small_h = ([(n_, g_) for n_ in ("lre", "lim", "ldt", "bre", "bim", "cre", "cim") for g_ in range(2)]
                       + [(n_, g_, q_) for n_ in ("cre", "cim") for g_ in range(2) for q_ in range(16)]
                       + ["dvec"] + [("gains", g_) for g_ in range(3)])
            sync0 = sb(p0, "sync0", [128, 1], F32)
            P.op("dve", lambda e: e.memset(sync0[:], 0.0), reads=small_h, writes=small_h + ["sync0"])
            issue_casts(CAST1, after=["sync0"])
            ts("dve", tmp[:], ldt[:], 1.0, None, ALU.mult, None, L("ldt"), ["tmp"])
            act(dtv[:], tmp[:], AF.Exp, ["tmp"], ["dtv"])
            tt("dve", tmp[:], lre[:], dtv[:], ALU.mult, L("lre") + ["dtv"], ["tmp"])
            act(mag[:], tmp[:], AF.Exp, ["tmp"], ["mag"])
            tt("dve", th[:], lim[:], dtv[:], ALU.mult, L("lim") + ["dtv"], ["th"])

            def range_reduce(dst, src_shift):
                ts("dve", dst[:], th[:], src_shift, None, ALU.add, None, ["th"], ["rr"])
                ts("dve", tq[:], th[:], src_shift, None, ALU.add, None, ["th"], ["tq"])
                for m in range(1, 10):
                    ts("dve", tmp[:], tq[:], (2 * m - 1) * PI, -2.0 * PI, ALU.is_ge, ALU.mult, ["tq"], ["tmp"])
                    tt("dve", dst[:], dst[:], tmp[:], ALU.add, ["rr", "tmp"], ["rr"])

            range_reduce(thr, 0.0)
            act(sn[:], thr[:], AF.Sin, ["rr"], ["sn"])
            range_reduce(thr, PI / 2)
            act(cs[:], thr[:], AF.Sin, ["rr"], ["cs"])
            tt("dve", ar[:], mag[:], cs[:], ALU.mult, ["mag", "cs"], ["ar"])
            tt("dve", ai[:], mag[:], sn[:], ALU.mult, ["mag", "sn"], ["ai"])
            tt("dve", den[:], lre[:], lre[:], ALU.mult, L("lre"), ["den"])
            tt("dve", tmp[:], lim[:], lim[:], ALU.mult, L("lim"), ["tmp"])
            tt("dve", den[:], den[:], tmp[:], ALU.add, ["den", "tmp"], ["den"])
            P.op("dve", lambda e: e.reciprocal(out=rden[:], in_=den[:]), reads=["den"], writes=["rden"])
            ts("dve", nr[:], ar[:], -1.0, None, ALU.add, None, ["ar"], ["nr"])
            tt("dve", fr[:], nr[:], lre[:], ALU.mult, ["nr"] + L("lre"), ["fr"])
            tt("dve", tmp[:], ai[:], lim[:], ALU.mult, ["ai"] + L("lim"), ["tmp"])
            tt("dve", fr[:], fr[:], tmp[:], ALU.add, ["fr", "tmp"], ["fr"])
            tt("dve", fr[:], fr[:], rden[:], ALU.mult, ["fr", "rden"], ["fr"])
            tt("dve", fi[:], ai[:], lre[:], ALU.mult, ["ai"] + L("lre"), ["fi"])
            tt("dve", tmp[:], nr[:], lim[:], ALU.mult, ["nr"] + L("lim"), ["tmp"])
            tt("dve", fi[:], fi[:], tmp[:], ALU.subtract, ["fi", "tmp"], ["fi"])
            tt("dve", fi[:], fi[:], rden[:], ALU.mult, ["fi", "rden"], ["fi"])

            def bc16(a):
                return a.unsqueeze(2).to_broadcast([128, 16, 16])

            def cmul(ore, oim, are_, aim_, bre_, bim_, ra, rb, wo):
                tt("dve", t1[:], bre_, are_, ALU.mult, ra + rb, ["t1"])
                tt("dve", t2[:], bim_, aim_, ALU.mult, ra + rb, ["t2"])
                tt("dve", ore, t1[:], t2[:], ALU.subtract, ["t1", "t2"], [wo + "re"])
                tt("dve", t1[:], bim_, are_, ALU.mult, ra + rb, ["t1"])
                tt("dve", t2[:], bre_, aim_, ALU.mult, ra + rb, ["t2"])
                tt("dve", oim, t1[:], t2[:], ALU.add, ["t1", "t2"], [wo + "im"])

            cmul(bbre[:], bbim[:], bc16(fr[:]), bc16(fi[:]), bre[:], bim[:], ["fr", "fi"], L("bre") + L("bim"), "bb")
            mset("dve", pw[:, 0, 0, :], 1.0, [("pw", 0)])
            mset("dve", pw[:, 0, 1, :], 0.0, [("pw", 0)])
            for j in range(1, 9):
                pr, pi_ = pw[:, j - 1, 0, :], pw[:, j - 1, 1, :]
                tt("dve", tmp[:], pr, ar[:], ALU.mult, [("pw", j - 1), "ar"], ["tmp"])
                tt("dve", tq[:], pi_, ai[:], ALU.mult, [("pw", j - 1), "ai"], ["tq"])
                tt("dve", pw[:, j, 0, :], tmp[:], tq[:], ALU.subtract, ["tmp", "tq"], [("pw", j)])
                tt("dve", tmp[:], pr, ai[:], ALU.mult, [("pw", j - 1), "ai"], ["tmp"])
                tt("dve", tq[:], pi_, ar[:], ALU.mult, [("pw", j - 1), "ar"], ["tq"])
                tt("dve", pw[:, j, 1, :], tmp[:], tq[:], ALU.add, ["tmp", "tq"], [("pw", j)])
            cp("dve", Apow[:, 0, 0, :], cs[:], ["cs"], [("Apow", 0)])
            cp("dve", Apow[:, 0, 1, :], sn[:], ["sn"], [("Apow", 0)])
            cp("dve", Apow[:, 0, 2, :], mag[:], ["mag"], [("Apow", 0)])

            if stop_after == "prep_elem":
                for nm_, t_, hs_ in (("ldt", ldt, L("ldt")), ("lre", lre, L("lre")), ("lim", lim, L("lim")), ("dtv", dtv, ["dtv"]), ("mag", mag, ["mag"]), ("th", th, ["th"]),
                                     ("sn", sn, ["sn"]), ("cs", cs, ["cs"]), ("ar", ar, ["ar"]), ("ai", ai, ["ai"]), ("fr", fr, ["fr"]), ("fi", fi, ["fi"])):
                    dump(nm_, t_[:], hs_, [128, 16], F32)
                dump("pw", pw[:], [("pw", j) for j in range(9)], [128, 9, 2, 16], F32)
                dump("bbre", bbre[:], ["bbre"], [128, 16, 16], F32)
                dump("Apow", Apow[:], [("Apow", l) for l in range(9)] + [("ApowN", l) for l in range(9)], [128, 9, 3, 16], F32)
                P.emit_phase()
                return nc
            mset("pool", MPre[:], 0.0, ["MPre"])
            mset("pool", MPim[:], 0.0, ["MPim"])
            mset("pool", CPre[:], 0.0, ["CPre"])
            mset("pool", CPnim[:], 0.0, ["CPnim"])
            mset("pool", W3c[:], 0.0, ["W3c"])

            def scatter(dstpad, src, rs, w, neg=False, eng="pool"):
                for gi in range(2):
                    rows = slice(gi * 64, gi * 64 + 64)
                    for ql in range(4):
                        c0 = 32 * ql + 16 * gi
                        o_ = dstpad[rows, ql::4, c0:c0 + 16]
                        i_ = src[rows, ql::4, :]
                        if neg:
                            ts(eng, o_, i_, -1.0, None, ALU.mult, None, rs, [w])
                        else:
                            cp(eng, o_, i_, rs, [w])

            scatter(CPre, cre, L("cre"), "CPre")
            scatter(CPnim, cim, L("cim"), "CPnim", neg=True)

            for t in range(8):
                cmul(mcre[:], mcim[:], bc16(pw[:, t + 1, 0, :]), bc16(pw[:, t + 1, 1, :]), cre[:], cim[:],
                     [("pw", t + 1)], L("cre") + L("cim"), "mc")
                for gi in range(2):
                    rows = slice(gi * 64, gi * 64 + 64)
                    cp("pool", W3c[rows, :, t, 0, 16 * gi:16 * gi + 16], mcre[rows], ["mcre"], ["W3c"])
                    ts("pool", W3c[rows, :, t, 1, 16 * gi:16 * gi + 16], mcim[rows], -1.0, None, ALU.mult, None,
                       ["mcim"], ["W3c"])

            if stop_after == "prep_pad":
                dump("W3c", W3c[:], ["W3c"], [128, 16, 8, 2, 32], BF16)
                P.emit_phase()
                return nc
            for j in range(8):
                cmul(mcre[:], mcim[:], bc16(pw[:, j, 0, :]), bc16(pw[:, j, 1, :]), bbre[:], bbim[:],
                     [("pw", j)], ["bbre", "bbim"], "mc")
                scatter(MPre, mcre, ["mcre"], "MPre", eng="pool")
                scatter(MPim, mcim, ["mcim"], "MPim", eng="dve")
                s = 7 - j
                for T in range(4):
                    if "noK" in dbg:
                        break
                    bk, bh = nb()
                    n = 0
                    for ql in range(4):
                        q = 4 * T + ql
                        for (mp, cpd, hm, hc) in ((MPre, CPre, "MPre", "CPre"), (MPim, CPnim, "MPim", "CPnim")):
                            mm(bk[:, 0:128], mp[:, q, :], cpd[:, q, :], n == 0, n == 7, [hm, hc], [bh])
                            n += 1
                    if j == 0:
                        stt("dve", BD[:, T, 0, :], identf[:], dvec[:, T:T + 1], bk[:, 0:128], ALU.mult, ALU.add,
                            [bh, "identf", "dvec"], [("BD", T, 0)])
                    else:
                        cp("dve", BD[:, T, j, :], bk[:, 0:128], [bh], [("BD", T, j)])
                    if "noW2" in dbg:
                        continue
                    bk2, bh2 = nb()
                    for pi2, (mp, hm) in enumerate(((MPre, "MPre"), (MPim, "MPim"))):
                        for ql in range(4):
                            q = 4 * T + ql
                            mm(bk2[:, pi2 * 128:(pi2 + 1) * 128], mp[:, q, :], identf[:], ql == 0, ql == 3,
                               [hm, "identf"], [bh2])
                    cp("act", W2c[:, T, s, :, :], bk2[:, 0:256].rearrange("p (a b) -> p a b", a=2), [bh2], [("W2c", T, s)])

            if "prep" in dbg:
                dump("BD", BD[:], [("BD", T, j) for T in range(4) for j in range(8)], [128, 4, 8, 128], BF16)
                dump("W2c", W2c[:], [("W2c", T, s) for T in range(4) for s in range(8)], [128, 4, 8, 2, 128], BF16)
                dump("W3c", W3c[:], ["W3c"], [128, 16, 8, 2, 32], BF16)
                dump("Apow", Apow[:], [("Apow", l) for l in range(9)] + [("ApowN", l) for l in range(9)], [128, 9, 3, 16], F32)
            P.emit_phase()
        if stop_after == "prep":
            return nc
        build_rest(nc, P, I, WC, out, h_scr, uT_scr, es, mid, sb, banks, nb, mm, act, tt, ts, stt, cp, mset, dma, dump,
                   identf, identb, gains, ybuf, s5in, W2c, W3c, BD, Apow, dbg, stop_after, ntiles,
                   issue_casts2)
    return nc


class WStream:
    def __init__(self, slots, specs, dma, pf=3):
        self.slots, self.specs, self.dma, self.pf = slots, specs, dma, pf
        self.issued = 0
        self.pos = 0

    def _issue(self, i):
        si = i % len(self.slots)
        pairs = self.specs[i](self.slots[si])
        assert len(pairs) == 2
        import os as _os
        if _os.environ.get("WS_NODMA") and i >= len(self.slots):
            return
        for pi, (o, a) in enumerate(pairs):
            self.dma("sp", o, a, [], [("ring", si, pi)], semkey=("ring", si, pi))

    def next(self, n=1):
        lim = self.pos + len(self.slots) - 1
        while self.issued < min(len(self.specs), lim):
            self._issue(self.issued)
            self.issued += 1
        res = []
        for _ in range(n):
            si = self.pos % len(self.slots)
            self.pos += 1
            res.append((self.slots[si], [("ring", si, 0), ("ring", si, 1)]))
        return res[0] if n == 1 else res


def build_rest(nc, P, I, WC, out, h_scr, uT_scr, es, mid, sb, banks, nb, mm, act, tt, ts, stt, cp, mset, dma, dump,
               identf, identb, gains, ybuf, s5in, W2c, W3c, BD, Apow, dbg, stop_after, ntiles, issue_casts2):
    kp = lambda a: a.rearrange("(k p) c -> p k c", p=128)

    def spec_w13(w1c, w3c, g):
        def f(slot):
            return [(slot[:, 0:2048].rearrange("p (k c) -> p k c", k=8), kp(w1c)[:, :, g * 256:(g + 1) * 256]),
                    (slot[:, 2048:4096].rearrange("p (k c) -> p k c", k=8), kp(w3c)[:, :, g * 256:(g + 1) * 256])]
        return f

    def spec_w2(w2c, g2):
        nf = 4 if g2 < 5 else 2
        hlf = nf // 2

        def f(slot):
            v = slot.rearrange("p (f c) -> p f c", f=4)
            src = w2c[g2 * 512:g2 * 512 + nf * 128, :].rearrange("(f p) c -> p f c", p=128)
            return [(v[:, 0:hlf, :], src[:, 0:hlf, :]), (v[:, hlf:nf, :], src[:, hlf:nf, :])]
        return f

    def spec_cols(wc, c0, nk):
        def f(slot):
            v = slot[:, 0:nk * 512].rearrange("p (k c) -> p k c", k=nk)
            src = kp(wc)[:, :, c0:c0 + 512]
            h2 = nk // 2
            return [(v[:, 0:h2, :], src[:, 0:h2, :]), (v[:, h2:nk, :], src[:, h2:nk, :])]
        return f

    def spec_rows(wc, k0):
        def f(slot):
            v = slot.rearrange("p (k c) -> p k c", k=4)
            src = kp(wc)[:, k0:k0 + 4, :]
            return [(v[:, 0:2, :], src[:, 0:2, :]), (v[:, 2:4, :], src[:, 2:4, :])]
        return f

    def ffn_specs(pre):
        return ([spec_w13(WC[pre + "_w1"], WC[pre + "_w3"], g) for g in range(11)] +
                [spec_w2(WC[pre + "_w2"], g2) for g2 in range(6)] * 2)

    def rms_stats(B, gidx):
        h, xn, ss, rstd = B["h"], B["xn"], B["ss"], B["rstd"]
        HH = B["hh"]
        mset("dve", ss[:], 0.0, ["ss"])
        for sub in range(4):
            act(xn[:, sub, :], h[:, sub, :], AF.Square, [HH(sub), "ss"], [("xn", sub), ("ssv", sub)],
                accum_out=ss[:, sub:sub + 1])
        ts("dve", rstd[:], ss[:], 1.0 / D, EPS, ALU.mult, ALU.add, [("ssv", s_) for s_ in range(4)], ["rstd"])
        act(rstd[:], rstd[:], AF.Sqrt, ["rstd"], ["rstd"])
        P.op("dve", lambda e: e.reciprocal(out=rstd[:], in_=rstd[:]), reads=["rstd"], writes=["rstd"])
        for sub in range(4):
            act(xn[:, sub, :], h[:, sub, :], AF.Copy, [HH(sub), "rstd", ("xn", sub)], [("xn", sub)],
                scale=rstd[:, sub:sub + 1])

    def rms_transpose(B, gidx):
        xn, uT = B["xn"], B["uT"]
        for k in range(8):
            bk, bh = nb()
            bkb = bk[:].bitcast(BF16)
            for sub in range(4):
                P.op("pe", lambda e, k=k, sub=sub, bkb=bkb: e.transpose(
                    out=bkb[:, sub * 128:(sub + 1) * 128], in_=xn[:, sub, k * 128:(k + 1) * 128], identity=identb[:]),
                    reads=[("xn", sub), "identb"], writes=[bh])
            if k % 2 == 0:
                ts("dve", uT[:, k, :], bkb[:, 0:512], gains[:, gidx, k:k + 1], None, ALU.mult, None,
                   [bh, ("gains", gidx)], [(B["uTh"], k)])
            else:
                act(uT[:, k, :], bkb[:, 0:512], AF.Copy, [bh, ("gains", gidx)], [(B["uTh"], k)],
                    scale=gains[:, gidx, k:k + 1])

    def rmsnorm_T(B, gidx):
        rms_stats(B, gidx)
        rms_transpose(B, gidx)

    UT = [("uT", k) for k in range(8)]

    def ffn_h13(B, ws, g_lo, g_hi, uT=None):
        gT, sg = B["gT"], B["sg"]
        uT = B["uT"] if uT is None else uT
        for g in range(g_lo, g_hi):
            slot, sh = ws.next()
            w1v = slot[:, 0:2048].rearrange("p (k c) -> p k c", k=8)
            w3v = slot[:, 2048:4096].rearrange("p (k c) -> p k c", k=8)
            for fi in range(2):
                f = 2 * g + fi
                bA, hA = nb()
                for k in range(8):
                    mm(bA[:], w1v[:, k, fi * 128:(fi + 1) * 128], uT[:, k, :], k == 0, k == 7, sh + [(B["uTh"], k)], [hA])
                bB, hB = nb()
                for k in range(8):
                    mm(bB[:], w3v[:, k, fi * 128:(fi + 1) * 128], uT[:, k, :], k == 0, k == 7, sh + [(B["uTh"], k)], [hB])
                act(sg[:, f % 2, :], bA[:], AF.Silu, [hA], [("sg", f % 2)])
                tt("dve", gT[:, f, :], sg[:, f % 2, :], bB[:], ALU.mult, [("sg", f % 2), hB], [("gT", f)])

    def ffn_w2(B, ws):
        h, gT = B["h"], B["gT"]
        HH = B["hh"]
        for rnd in range(2):
            subs = (2 * rnd, 2 * rnd + 1)
            accs = {(sub, half): nb() for sub in subs for half in range(2)}
            for g2 in range(6):
                slot, sh = ws.next()
                w2v = slot.rearrange("p (f c) -> p f c", f=4)
                nf = 4 if g2 < 5 else 2
                for fi in range(nf):
                    f = 4 * g2 + fi
                    for sub in subs:
                        for half in range(2):
                            bk, bh = accs[(sub, half)]
                            mm(bk[:], gT[:, f, sub * 128:(sub + 1) * 128], w2v[:, fi, half * 512:(half + 1) * 512],
                               f == 0, f == NF - 1, sh + [("gT", f)], [bh])
            for sub in subs:
                for half in range(2):
                    bk, bh = accs[(sub, half)]
                    hv = h[:, sub, half * 512:(half + 1) * 512]
                    stt("dve", hv, bk[:], 0.5, hv, ALU.mult, ALU.add, [bh, HH(sub)], [HH(sub)])

    def ffn(B, ws, after_h13=None):
        ffn_h13(B, ws, 0, 11)
        if after_h13 is not None:
            after_h13()
        ffn_w2(B, ws)

    xv = I["x"].rearrange("(i s p) d -> i s p d", s=4, p=128)
    hsv = h_scr.rearrange("(i s p) d -> i s p d", s=4, p=128)
    outv = out.rearrange("(i s p) d -> i s p d", s=4, p=128)

    with ExitStack() as p1:
        B = {}
        hbuf = [sb(p1, "h1a", [128, 4, D], F32), sb(p1, "h1b", [128, 4, D], F32)]
        B["xn"] = sb(p1, "xn1", [128, 4, D], BF16)
        uTb = [sb(p1, "uT1a", [128, 8, NT], BF16), sb(p1, "uT1b", [128, 8, NT], BF16)]
        B["ss"] = sb(p1, "ss1", [128, 4], F32)
        B["rstd"] = sb(p1, "rstd1", [128, 4], F32)
        B["sg"] = sb(p1, "sg1", [128, 2, NT], F32)
        B["gT"] = ybuf[:].rearrange("p a b -> p (a b)")[:, 0:NF * NT].rearrange("p (f t) -> p f t", f=NF)
        ring = [sb(p1, "ring1_%d" % i, [128, 4096], BF16) for i in range(4)]
        wins5 = sb(p1, "wins5", [128, 8, 512], BF16)
        dma("sp", wins5[:, 0:4, :], kp(WC["w_in"])[:, 0:4, 0:512], [], ["wins5a"])
        dma("sp", wins5[:, 4:8, :], kp(WC["w_in"])[:, 4:8, 0:512], [], ["wins5b"])
        specs = []
        for i in range(ntiles):
            specs += ffn_specs("ffn1")
        ws = WStream([r[:] for r in ring], specs, dma)
        print("SBUF remaining pass1:", nc.sbuf_bytes_remaining)
        def load_x(i):
            for sub in range(4):
                dma("pool", hbuf[i % 2][:, sub, :], xv[i, sub], [], [("h", i % 2, sub)])

        def setB(i):
            par = i % 2
            B["h"] = hbuf[par]
            B["hh"] = lambda sub, par=par: ("h", par, sub)
            B["uT"] = uTb[par]
            B["uTh"] = ("uT", par)

        def mix_stats(i):
            setB(i)
            rms_stats(B, 1)

        def mix_and_proj(i):
            setB(i)
            par = i % 2
            uT = uTb[par]
            rms_transpose(B, 1)
            for half in range(2):
                dma("pool", uT_scr[i][:, half * 2048:(half + 1) * 2048].rearrange("p (k t) -> p k t", k=4),
                    uT[:, half * 4:(half + 1) * 4, :], [(("uT", par), k) for k in range(half * 4, half * 4 + 4)],
                    [("utscr", half)])
            for m in range(4):
                bk, bh = nb()
                for k in range(8):
                    mm(bk[:], wins5[:, k, m * 128:(m + 1) * 128], uT[:, k, :], k == 0, k == 7,
                       ["wins5a", "wins5b", (("uT", par), k)], [bh])
                o_ = s5in[:, m, :, i * 64:(i + 1) * 64]
                i_ = bk[:].rearrange("p (c s) -> p s c", s=8)
                if m % 2 == 0:
                    cp("dve", o_, i_, [bh], [("s5in", m)])
                else:
                    cp("act", o_, i_, [bh], [("s5in", m)])

        load_x(0)
        if ntiles > 1:
            load_x(1)
        setB(0)
        rmsnorm_T(B, 0)
        for i in range(ntiles):
            issue_casts2(i)
            if i > 0:
                mix_stats(i - 1)
            setB(i)
            ffn_h13(B, ws, 0, 4)
            if i > 0:
                mix_and_proj(i - 1)
                if i + 1 < ntiles:
                    load_x(i + 1)
            if i + 1 < ntiles:
                setB(i + 1)
                rms_stats(B, 0)
            setB(i)
            ffn_h13(B, ws, 4, 8)
            if i + 1 < ntiles:
                setB(i + 1)
                rms_transpose(B, 0)
            setB(i)
            ffn_h13(B, ws, 8, 11)
            ffn_w2(B, ws)
            for sub in range(4):
                dma("pool", hsv[i, sub], hbuf[i % 2][:, sub, :], [("h", i % 2, sub)], [("hscr", sub)])
        mix_stats(ntiles - 1)
        mix_and_proj(ntiles - 1)
        if "p1" in dbg:
            dump("s5in", s5in[:], [("s5in", m) for m in range(4)], [128, 4, 8, NCH], BF16)
        P.emit_phase()
    if stop_after == "p1":
        return

    with ExitStack() as p2:
        Eh = sb(p2, "Eh", [128, 9, 2, 16], F32)
        r8 = sb(p2, "r8", [128, 16], F32)
        w16 = [sb(p2, "w16_%d" % i_, [128, 16], F32) for i_ in range(6)]
        Etab = [sb(p2, "Etab%d" % i_, [128, 4, NCH], F32) for i_ in range(2)]
        et = [sb(p2, "et%d" % i_, [128, 4, NCH // 2], F32) for i_ in range(2)]
        SC = [[sb(p2, "sc%d_%d" % (a_, b_), [128, NCH], F32) for b_ in range(6)] for a_ in range(2)]
        X0 = [sb(p2, "X0_%d" % t_, [128, 4, 2, NCH], BF16) for t_ in range(2)]
        mset("pool", X0[0][:], 0.0, [("X0", 0, ql, pt) for ql in range(4) for pt in range(2)])
        mset("pool", X0[1][:], 0.0, [("X0", 1, ql, pt) for ql in range(4) for pt in range(2)])
        yv = ybuf[:].rearrange("p a (c t) -> p a c t", t=8)
        print("SBUF remaining s5:", nc.sbuf_bytes_remaining)
        issue_casts2(-1)

        cre_, cim_, m2_, t1_, t2_, rm_ = [w[:] for w in w16]

        def normalize(re, im, hre):
            tt("dve", m2_, re, re, ALU.mult, hre, ["w_m2"])
            tt("dve", t1_, im, im, ALU.mult, hre, ["w_t1"])
            tt("dve", m2_, m2_, t1_, ALU.add, ["w_m2", "w_t1"], ["w_m2"])
            act(m2_, m2_, AF.Sqrt, ["w_m2"], ["w_m2"])
            P.op("dve", lambda e: e.reciprocal(out=rm_, in_=m2_), reads=["w_m2"], writes=["w_rm"])
            tt("dve", re, re, rm_, ALU.mult, hre + ["w_rm"], hre)
            tt("dve", im, im, rm_, ALU.mult, hre + ["w_rm"], hre)

        def square(ore, oim, re, im, hin, hout):
            tt("dve", t1_, re, re, ALU.mult, hin, ["w_t1"])
            tt("dve", t2_, im, im, ALU.mult, hin, ["w_t2"])
            tt("dve", m2_, re, im, ALU.mult, hin, ["w_m2"])
            tt("dve", ore, t1_, t2_, ALU.subtract, ["w_t1", "w_t2"], hout)
            ts("dve", oim, m2_, 2.0, None, ALU.mult, None, ["w_m2"], hout)

        cp("dve", cre_, Apow[:, 0, 0, :], [], ["w_c"])
        cp("dve", cim_, Apow[:, 0, 1, :], [], ["w_c"])
        normalize(cre_, cim_, ["w_c"])
        for it in range(3):
            if it < 2:
                square(Eh[:, 8, 0, :], Eh[:, 8, 1, :], cre_, cim_, ["w_c"], ["w_d"])
                normalize(Eh[:, 8, 0, :], Eh[:, 8, 1, :], ["w_d"])
                cp("dve", cre_, Eh[:, 8, 0, :], ["w_d"], ["w_c"])
                cp("dve", cim_, Eh[:, 8, 1, :], ["w_d"], ["w_c"])
            else:
                square(Eh[:, 0, 0, :], Eh[:, 0, 1, :], cre_, cim_, ["w_c"], [("Eh", 0)])
                normalize(Eh[:, 0, 0, :], Eh[:, 0, 1, :], [("Eh", 0)])
        for l in range(1, 9):
            square(Eh[:, l, 0, :], Eh[:, l, 1, :], Eh[:, l - 1, 0, :], Eh[:, l - 1, 1, :], [("Eh", l - 1)], [("Eh", l)])
            normalize(Eh[:, l, 0, :], Eh[:, l, 1, :], [("Eh", l)])
        tt("dve", r8[:], Apow[:, 0, 2, :], Apow[:, 0, 2, :], ALU.mult, [], ["r8"])
        tt("dve", r8[:], r8[:], r8[:], ALU.mult, ["r8"], ["r8"])
        tt("dve", r8[:], r8[:], r8[:], ALU.mult, ["r8"], ["r8"])
        EHALL = [("Eh", l) for l in range(9)]

        def s5_part2(T):
            bl = {}
            for part in range(2):
                for ql in range(4):
                    bl[(ql, part)] = nb()
                for s in range(8):
                    for ql in range(4):
                        bk, bh = bl[(ql, part)]
                        rows = slice(32 * ql, 32 * ql + 32)
                        mm(bk[:], W2c[rows, T, s, part, :], s5in[rows, T, s, :], s == 0, s == 7,
                           [("s5in", T), "W2c"], [bh], tile_position=(32 * ql, 0))
            return bl

        def s5_tables(T):
            Ere, Eim = Etab
            mset("dve", Ere[:, :, 0:1], 1.0, ["Etab"])
            mset("dve", Eim[:, :, 0:1], 0.0, ["Etab"])
            for l in range(9):
                n = 1 << l
                ere = Eh[:, l, 0, 4 * T:4 * T + 4].unsqueeze(2).to_broadcast([128, 4, n])
                eim = Eh[:, l, 1, 4 * T:4 * T + 4].unsqueeze(2).to_broadcast([128, 4, n])
                a0, a1 = et[0][:, :, 0:n], et[1][:, :, 0:n]
                tt("dve", a0, Ere[:, :, 0:n], ere, ALU.mult, ["Etab", ("Eh", l)], ["et0"])
                tt("dve", a1, Eim[:, :, 0:n], eim, ALU.mult, ["Etab", ("Eh", l)], ["et1"])
                tt("dve", Ere[:, :, n:2 * n], a0, a1, ALU.subtract, ["et0", "et1"], ["Etab"])
                tt("dve", a0, Ere[:, :, 0:n], eim, ALU.mult, ["Etab", ("Eh", l)], ["et0"])
                tt("dve", a1, Eim[:, :, 0:n], ere, ALU.mult, ["Etab", ("Eh", l)], ["et1"])
                tt("dve", Eim[:, :, n:2 * n], a0, a1, ALU.add, ["et0", "et1"], ["Etab"])

        def s5_scan(T, bl):
            x0 = X0[T % 2]
            Ere, Eim = Etab
            seqs = []
            for ql in range(4):
                q = 4 * T + ql
                st = ql % 2
                t1, t2, zre, zim, wre, wim = [b[:] for b in SC[st]]
                H = lambda nm, st=st: ("sc", st, nm)
                (bre_, hre_), (bim_, him_) = bl[(ql, 0)], bl[(ql, 1)]
                er, ei = Ere[:, ql, :], Eim[:, ql, :]
                rb = r8[:, q:q + 1].to_broadcast([128, NCH])
                n1 = NCH - 1
                ops = [
                    ("tt", t1, bre_[:], er, ALU.mult, [hre_, "Etab"], [H("t1")]),
                    ("tt", t2, bim_[:], ei, ALU.mult, [him_, "Etab"], [H("t2")]),
                    ("tt", zre, t1, t2, ALU.add, [H("t1"), H("t2")], [H("zre")]),
                    ("tt", t1, bim_[:], er, ALU.mult, [him_, "Etab"], [H("t1")]),
                    ("tt", t2, bre_[:], ei, ALU.mult, [hre_, "Etab"], [H("t2")]),
                    ("tt", zim, t1, t2, ALU.subtract, [H("t1"), H("t2")], [H("zim")]),
                    ("scan", wre, rb, zre, [H("zre"), "r8"], [H("wre")]),
                    ("scan", wim, rb, zim, [H("zim"), "r8"], [H("wim")]),
                    ("tt", t1, wre, er, ALU.mult, [H("wre"), "Etab"], [H("t1")]),
                    ("tt", t2, wim, ei, ALU.mult, [H("wim"), "Etab"], [H("t2")]),
                    ("tt", x0[:, ql, 0, 1:NCH], t1[:, 0:n1], t2[:, 0:n1], ALU.subtract, [H("t1"), H("t2")], [("X0", T % 2, ql, 0)]),
                    ("tt", t1, wre, ei, ALU.mult, [H("wre"), "Etab"], [H("t1")]),
                    ("tt", t2, wim, er, ALU.mult, [H("wim"), "Etab"], [H("t2")]),
                    ("tt", x0[:, ql, 1, 1:NCH], t1[:, 0:n1], t2[:, 0:n1], ALU.add, [H("t1"), H("t2")], [("X0", T % 2, ql, 1)]),
                ]
                seqs.append(ops)
            for grp in ((0, 1), (2, 3)):
                for k in range(len(seqs[0])):
                    for ql in grp:
                        o = seqs[ql][k]
                        if o[0] == "tt":
                            tt("dve", o[1], o[2], o[3], o[4], o[5], o[6])
                        else:
                            P.op("dve", lambda e, o=o: e.tensor_tensor_scan(out=o[1], data0=o[2], data1=o[3], initial=0.0,
                                                                            op0=ALU.mult, op1=ALU.add), reads=o[4], writes=o[5])

        def s5_part13(T):
            x0 = X0[T % 2]
            for t in range(8):
                bk, bh = nb()
                for s in range(t + 1):
                    mm(bk[:], BD[:, T, t - s, :], s5in[:, T, s, :], s == 0, False, [("BD", T), ("s5in", T)], [bh])
                n = 0
                for part in range(2):
                    for ql in range(4):
                        q = 4 * T + ql
                        n += 1
                        mm(bk[32 * ql:32 * ql + 32, :], W3c[:, q, t, part, :], x0[:, ql, part, :], False, part == 1,
                           ["W3c", ("X0", T % 2, ql, part)], [bh], tile_position=(0, 32 * ql))
                cp("act", yv[:, T, :, t], bk[:], [bh], [("y", T)])

        for T in range(4):
            bl = s5_part2(T)
            s5_tables(T)
            s5_scan(T, bl)
            if T > 0:
                s5_part13(T - 1)
        s5_part13(3)
        if "s5" in dbg:
            dump("y", ybuf[:], [("y", T) for T in range(4)], [128, 4, SEQ], BF16)
        P.emit_phase()
    if stop_after == "s5":
        return
    mid.close()
    build_pass2(nc, P, I, WC, out, h_scr, uT_scr, es, sb, banks, nb, mm, act, tt, ts, stt, cp, mset, dma, dump,
                identf, identb, gains, ybuf, dbg, stop_after, ntiles, WStream, rmsnorm_T, ffn, ffn_specs,
                spec_cols, spec_rows, kp, hsv, outv, UT)


def build_pass2(nc, P, I, WC, out, h_scr, uT_scr, es, sb, banks, nb, mm, act, tt, ts, stt, cp, mset, dma, dump,
                identf, identb, gains, ybuf, dbg, stop_after, ntiles, WStream, rmsnorm_T, ffn, ffn_specs,
                spec_cols, spec_rows, kp, hsv, outv, UT):
    with ExitStack() as p3:
        B = {}
        hbuf = [sb(p3, "h2a", [128, 4, D], F32), sb(p3, "h2b", [128, 4, D], F32)]
        B["xn"] = sb(p3, "xn2", [128, 4, D], BF16)
        B["uT"] = sb(p3, "uT2", [128, 8, NT], BF16)
        B["ss"] = sb(p3, "ss2", [128, 4], F32)
        B["rstd"] = sb(p3, "rstd2", [128, 4], F32)
        B["sg"] = sb(p3, "sg2", [128, 2, NT], F32)
        big = sb(p3, "big2", [128, 24 * NT], BF16)
        B["gT"] = big[:, 0:NF * NT].rearrange("p (f t) -> p f t", f=NF)
        uT = B["uT"]
        ring = [sb(p3, "ring2_%d" % i, [128, 4096], BF16) for i in range(4)]
        gfin = sb(p3, "gfin", [128, D], F32)
        wal = sb(p3, "wal", [128, 8, 16], BF16)
        wglu = sb(p3, "wglu", [128, 4, 512], BF16)
        bglu = sb(p3, "bglu", [128, 4], F32)
        gnorm = sb(p3, "gnorm", [128, 4], F32)
        wup = sb(p3, "wup", [33, 256], F32)
        Tri = sb(p3, "Tri", [128, 128], F32)
        Um = sb(p3, "Um", [128, 128], F32)
        MaskT = sb(p3, "MaskT", [128, 128], F32)
        onesb = sb(p3, "onesb", [128, 128], BF16)
        qT = sb(p3, "qT", [128, 2, NT], F32)
        kT = sb(p3, "kT", [128, 2, NT], F32)
        ktok = sb(p3, "ktok", [128, 4, 256], F32)
        vtok = sb(p3, "vtok", [128, 4, 512], BF16)
        rT = sb(p3, "rT", [128, 4, NT], BF16)
        alT = sb(p3, "alT", [33, NT], F32)
        sgt = big[:, 0:16 * NT].rearrange("p (f t) -> p f t", f=16)
        yglaT = sb(p3, "yglaT", [128, 4, NT], BF16)
        ys5T = sb(p3, "ys5T", [128, 4, NT], BF16)
        zT = sb(p3, "zT", [128, 4, NT], BF16)
        mT = big[:, 16 * NT:24 * NT].rearrange("p (f t) -> p f t", f=8)
        mtmp = B["sg"]
        ex = sb(p3, "ex", [128, 256], F32)
        nl = sb(p3, "nl", [128, 256], F32)
        Eq = sb(p3, "Eq", [128, 2, 128], F32)
        Ek = sb(p3, "Ek", [128, 2, 128], F32)
        Eke = sb(p3, "Eke", [128, 256], F32)
        qs = sb(p3, "qs", [128, 2, 128], BF16)
        ksz = sb(p3, "ksz", [128, 4, 128], BF16)
        kend = sb(p3, "kend", [128, 256], BF16)
        PT = sb(p3, "PT", [128, 4, 128], BF16)
        Sf = sb(p3, "Sf", [128, 4, 128], F32)
        Sbz = sb(p3, "Sbz", [128, 4, 128], BF16)
        osq = sb(p3, "osq", [128, 512], BF16)
        rsd = sb(p3, "rsd", [128, 512], F32)
        ot = sb(p3, "ot", [128, 512], F32)

        print("SBUF remaining pass2:", nc.sbuf_bytes_remaining)
        dma("sp", gfin[:], I["final_norm"].rearrange("(o d) -> o d", o=1).to_broadcast([128, D]), [], ["c0"], semkey="small", slow=True)
        dma("sp", wal[:], kp(WC["w_in"])[:, :, C_AL:C_AL + 16], [], ["c1"], semkey="small", slow=True)
        dma("sp", wglu[:], kp(WC["s5_glu_w"]), [], ["c2"], semkey="small")
        dma("sp", bglu[:], I["s5_glu_b"].rearrange("(m p) -> p m", p=128), [], ["c3"], semkey="small", slow=True)
        dma("sp", gnorm[:], I["gla_out_norm"].rearrange("(m p) -> p m", p=128), [], ["c4"], semkey="small", slow=True)
        mset("pool", wup[:], 0.0, ["wup"])
        mset("pool", alT[:], 0.0, ["alT"])
        mset("pool", alT[32:33, :], 1.0, ["alT"])
        mset("pool", Sf[:], 0.0, ["Sf"])
        mset("pool", Sbz[:], 0.0, ["Sbz"])
        mset("pool", ksz[:], 0.0, ["ksz"])
        mset("pool", onesb[:], 1.0 / 128, ["onesb"])
        P.emit_phase()
        dma("sp", wup[0:16, :], I["gla_a_up_w"], [], ["c5"], semkey="small")
        dma("sp", wup[32:33, :], I["gla_a_up_b"].rearrange("(o d) -> o d", o=1), [], ["c6"], semkey="small")
        mset("pool", Tri[:], -1.0 / 16, ["Tri"])
        P.op("pool", lambda e: e.affine_select(out=Tri[:], in_=Tri[:], pattern=[[1, 128]], compare_op=ALU.is_ge,
                                               fill=0.0, base=0, channel_multiplier=-1), reads=["Tri"], writes=["Tri"])
        mset("pool", Um[:], -1.0 / 16, ["Um"])
        P.op("pool", lambda e: e.affine_select(out=Um[:], in_=Um[:], pattern=[[-1, 128]], compare_op=ALU.is_ge,
                                               fill=0.0, base=-1, channel_multiplier=1), reads=["Um"], writes=["Um"])
        mset("pool", MaskT[:], 1.0, ["MaskT"])
        P.op("pool", lambda e: e.affine_select(out=MaskT[:], in_=MaskT[:], pattern=[[1, 128]], compare_op=ALU.is_ge,
                                               fill=0.0, base=0, channel_multiplier=-1), reads=["MaskT"], writes=["MaskT"])
        P.emit_phase()

        win = WC["w_in"]
        specs = []
        for i in range(ntiles):
            specs += [spec_cols(win, C_Q, 8), spec_cols(win, C_V, 8), spec_cols(win, C_R, 8),
                      spec_cols(win, C_GS, 8), spec_cols(win, C_GS + 512, 8),
                      spec_cols(win, C_GG, 8), spec_cols(win, C_GG + 512, 8),
                      spec_rows(WC["proj_s5"], 0), spec_rows(WC["proj_gla"], 0),
                      spec_rows(WC["w_out"], 0), spec_rows(WC["w_out"], 4)]
            specs += ffn_specs("ffn2")
        ws = WStream([r[:] for r in ring], specs, dma, pf=2)

        def proj_fm(slot, sh, m, nk=8):
            v = slot.rearrange("p (k c) -> p k c", k=nk)
            bk, bh = nb()
            for k in range(nk):
                mm(bk[:], v[:, k, m * 128:(m + 1) * 128], uT[:, k, :], k == 0, k == nk - 1, sh + [("uT", k)], [bh])
            return bk, bh

        def load_tile(i):
            for sub in range(4):
                dma("pool", hbuf[i % 2][:, sub, :], hsv[i, sub], [], [("h", i % 2, sub)])
            for half in range(2):
                dma("pool", uT[:, half * 4:(half + 1) * 4, :],
                    uT_scr[i][:, half * 2048:(half + 1) * 2048].rearrange("p (k t) -> p k t", k=4),
                    [], [("uT", k) for k in range(half * 4, half * 4 + 4)], semkey=("uTld", half))

        load_tile(0)
        for i in range(ntiles):
            t0 = i * NT
            h = hbuf[i % 2]
            B["h"] = h
            B["hh"] = lambda sub, par=i % 2: ("h", par, sub)
            B["uTh"] = "uT"
            HH = B["hh"]
            slot, sh = ws.next()
            v8 = slot.rearrange("p (k c) -> p k c", k=8)
            for m in range(2):
                bk, bh = proj_fm(slot, sh, m)
                cp("act", qT[:, m, :], bk[:], [bh], [("qT", m)])
            for m in range(2):
                bk, bh = proj_fm(slot, sh, 2 + m)
                cp("dve", kT[:, m, :], bk[:], [bh], [("kT", m)])
            for sub in range(4):
                bk, bh = nb()
                for k in range(8):
                    mm(bk[:, 0:256], uT[:, k, sub * 128:(sub + 1) * 128], v8[:, k, 256:512], k == 0, k == 7,
                       sh + [("uT", k)], [bh])
                cp("act" if sub % 2 else "dve", ktok[:, sub, :], bk[:, 0:256], [bh], [("ktok", sub)])
            slot, sh = ws.next()
            v8 = slot.rearrange("p (k c) -> p k c", k=8)
            for sub in range(4):
                bk, bh = nb()
                for k in range(8):
                    mm(bk[:], uT[:, k, sub * 128:(sub + 1) * 128], v8[:, k, :], k == 0, k == 7, sh + [("uT", k)], [bh])
                cp("act" if sub % 2 else "dve", vtok[:, sub, :], bk[:], [bh], [("vtok", sub)])
            bk, bh = nb()
            for k in range(8):
                mm(bk[0:16, :], wal[:, k, :], uT[:, k, :], k == 0, k == 7, [("uT", k)], [bh])
            cp("dve", alT[0:16, :], bk[0:16, :], [bh], ["alT"])
            fillers = []
            fstate = {}

            def r_filler(m):
                if m == 0:
                    fstate["slot"], fstate["sh"] = ws.next()
                bk, bh = proj_fm(fstate["slot"], fstate["sh"], m)
                act(rT[:, m, :], bk[:], AF.Silu, [bh], [("rT", m)])

            def g_filler(gidx, m):
                if m == 0:
                    fstate["slot"], fstate["sh"] = ws.next()
                bk, bh = proj_fm(fstate["slot"], fstate["sh"], m)
                act(sgt[:, gidx * 4 + m, :], bk[:], AF.Sigmoid, [bh], [("sgt", gidx * 4 + m)])

            for m in range(4):
                fillers.append(lambda m=m: r_filler(m))
            for gidx in range(4):
                for m in range(4):
                    fillers.append(lambda gidx=gidx, m=m: g_filler(gidx, m))

            for m in range(4):
                act(zT[:, m, :], ybuf[:, m, t0:t0 + NT], AF.Gelu_apprx_tanh, [], [("zT", m)])

            def glu_filler(mo):
                bk, bh = nb()
                for k in range(4):
                    mm(bk[:], wglu[:, k, mo * 128:(mo + 1) * 128], zT[:, k, :], k == 0, k == 3, [("zT", k)], [bh])
                act(mtmp[:, 0, :], bk[:], AF.Sigmoid, [bh], [("mtmp", 0)], bias=bglu[:, mo:mo + 1])
                tt("dve", ys5T[:, mo, :], zT[:, mo, :], mtmp[:, 0, :], ALU.mult, [("zT", mo), ("mtmp", 0)], [("ys5", mo)])

            def ps5_filler(dj):
                if dj == 0:
                    fstate["p5"], fstate["p5h"] = ws.next()
                p5 = fstate["p5"].rearrange("p (k c) -> p k c", k=4)
                bA, hA = nb()
                for k in range(4):
                    mm(bA[:], p5[:, k, dj * 128:(dj + 1) * 128], ys5T[:, k, :], k == 0, k == 3, fstate["p5h"] + [("ys5", k)], [hA])
                tt("dve", mT[:, dj, :], sgt[:, dj, :], bA[:], ALU.mult, [("sgt", dj), hA], [("mT", dj)])

            for mo in range(4):
                fillers.append(lambda mo=mo: glu_filler(mo))
            for dj in range(8):
                fillers.append(lambda dj=dj: ps5_filler(dj))

            def fill(n=1):
                for _ in range(n):
                    if fillers:
                        fillers.pop(0)()

            for j in range(0 if "nogla" in dbg else 4):
                cs_ = slice(j * 128, (j + 1) * 128)
                bk, bh = nb()
                mm(bk[:, 0:256], alT[0:33, cs_], wup[0:33, :], True, True, ["alT", "wup"], [bh])
                fill(2)
                act(ex[:], bk[:, 0:256], AF.Exp, [bh], ["ex"], scale=-1.0)
                act(nl[:], ex[:], AF.Ln, ["ex"], ["nl"], bias=1.0)
                bk, bh = nb()
                for hp in range(2):
                    mm(bk[:, hp * 128:(hp + 1) * 128], nl[:, hp * 128:(hp + 1) * 128], Tri[:], True, True, ["nl", "Tri"], [bh])
                mm(bk[:, 256:512], Um[:], nl[:], True, True, ["nl", "Um"], [bh])
                fill(2)
                bcT = bk[:, 0:256].rearrange("p (a b) -> p a b", a=2)
                act(Eq[:], bcT, AF.Exp, [bh], ["Eq"])
                act(Ek[:], bcT, AF.Exp, [bh], ["Ek"], scale=-1.0)
                act(Eke[:], bk[:, 256:512], AF.Exp, [bh], ["Eke"])
                for hp in range(2):
                    stt("dve", qs[:, hp, :], qT[:, hp, cs_], 0.125, Eq[:, hp, :], ALU.mult, ALU.mult,
                        [("qT", hp), "Eq"], [("qs", hp)])
                for hd in range(4):
                    hp, hi = hd // 2, hd % 2
                    rows = slice(hi * 64, hi * 64 + 64)
                    tt("pool", ksz[rows, hd, :], kT[rows, hp, cs_], Ek[rows, hp, :], ALU.mult, [("kT", hp), "Ek"], [("ksz", hd)])
                tt("pool", kend[:], ktok[:, j, :], Eke[:], ALU.mult, [("ktok", j), "Eke"], ["kend"])
                bk, bh = nb()
                for hd in range(4):
                    hp = hd // 2
                    mm(bk[:, hd * 128:(hd + 1) * 128], ksz[:, hd, :], qs[:, hp, :], True, True, [("ksz", hd), ("qs", hp)], [bh])
                fill(2)
                tt("dve", PT[:], bk[:].rearrange("p (a b) -> p a b", a=4),
                   MaskT[:].unsqueeze(1).to_broadcast([128, 4, 128]), ALU.mult, [bh, "MaskT"], ["PT"])
                bo, bho = nb()
                for hd in range(4):
                    hp = hd // 2
                    mm(bo[:, hd * 128:(hd + 1) * 128], vtok[:, j, hd * 128:(hd + 1) * 128], PT[:, hd, :], True, False,
                       [("vtok", j), "PT"], [bho])
                    mm(bo[:, hd * 128:(hd + 1) * 128], Sbz[:, hd, :], qs[:, hp, :], False, True, [("Sbz", hd), ("qs", hp)], [bho])
                bk, bh = nb()
                for hd in range(4):
                    hp = hd // 2
                    mm(bk[:, hd * 128:(hd + 1) * 128], kend[:, hp * 128:(hp + 1) * 128], vtok[:, j, hd * 128:(hd + 1) * 128],
                       True, True, ["kend", ("vtok", j)], [bh])
                fill()
                for hd in range(4):
                    hp, hi = hd // 2, hd % 2
                    rows = slice(hi * 64, hi * 64 + 64)
                    stt("dve", Sf[rows, hd, :], Sf[rows, hd, :], Eq[rows, hp, 127:128], bk[rows, hd * 128:(hd + 1) * 128],
                        ALU.mult, ALU.add, [("Sf", hd), "Eq", bh], [("Sf", hd)])
                    cp("pool", Sbz[rows, hd, :], Sf[rows, hd, :], [("Sf", hd)], [("Sbz", hd)])
                act(osq[:], bo[:], AF.Square, [bho], ["osq"])
                bk, bh = nb()
                mm(bk[:], onesb[:], osq[:], True, True, ["onesb", "osq"], [bh])
                fill()
                act(rsd[:], bk[:], AF.Ln, [bh], ["rsd"], bias=EPS)
                act(rsd[:], rsd[:], AF.Exp, ["rsd"], ["rsd"], scale=-0.5)
                tt("dve", ot[:], bo[:], rsd[:], ALU.mult, [bho, "rsd"], ["ot"])
                for hd in range(4):
                    stt("dve", yglaT[:, hd, cs_], ot[:, hd * 128:(hd + 1) * 128], gnorm[:, hd:hd + 1], rT[:, hd, cs_],
                        ALU.mult, ALU.mult, ["ot", ("rT", hd)], [("ygla", hd)])

            fill(len(fillers))
            gls, glh = ws.next()
            pg = gls.rearrange("p (k c) -> p k c", k=4)
            for dj in range(8):
                bB, hB = nb()
                for k in range(4):
                    mm(bB[:], pg[:, k, dj * 128:(dj + 1) * 128], yglaT[:, k, :], k == 0, k == 3, glh + [("ygla", k)], [hB])
                tt("dve", mtmp[:, dj % 2, :], sgt[:, 8 + dj, :], bB[:], ALU.mult, [("sgt", 8 + dj), hB], [("mtmp", dj % 2)])
                tt("pool", mT[:, dj, :], mT[:, dj, :], mtmp[:, dj % 2, :], ALU.add, [("mT", dj), ("mtmp", dj % 2)], [("mT", dj)])
            (wo0, wh0), (wo1, wh1) = ws.next(2)
            wov = [wo0.rearrange("p (k c) -> p k c", k=4), wo1.rearrange("p (k c) -> p k c", k=4)]
            for sub in range(4):
                for half in range(2):
                    bk, bh = nb()
                    for k in range(8):
                        mm(bk[:], mT[:, k, sub * 128:(sub + 1) * 128], wov[k // 4][:, k % 4, half * 512:(half + 1) * 512],
                           k == 0, k == 7, wh0 + wh1 + [("mT", k)], [bh])
                    hv = h[:, sub, half * 512:(half + 1) * 512]
                    tt("dve", hv, hv, bk[:], ALU.add, [bh, HH(sub)], [HH(sub)])
            if "mix" in dbg and i == 0:
                dump("hmix", h[:], [HH(s_) for s_ in range(4)], [128, 4, D], F32)
                dump("ygla", yglaT[:], [("ygla", s_) for s_ in range(4)], [128, 4, NT], BF16)
                dump("ys5", ys5T[:], [("ys5", s_) for s_ in range(4)], [128, 4, NT], BF16)

            rmsnorm_T(B, 2)
            if "noffn2" in dbg:
                ws.pos += 17
            else:
                ffn(B, ws, after_h13=(lambda i=i: load_tile(i + 1)) if i + 1 < ntiles else None)
            ss, rstd, xn = B["ss"], B["rstd"], B["xn"]
            mset("dve", ss[:], 0.0, ["ss"])
            for sub in range(4):
                act(xn[:, sub, :], h[:, sub, :], AF.Square, [HH(sub), "ss"], [("xn", sub), ("ssv", sub)],
                    accum_out=ss[:, sub:sub + 1])
            ts("dve", rstd[:], ss[:], 1.0 / D, EPS, ALU.mult, ALU.add, [("ssv", s_) for s_ in range(4)], ["rstd"])
            act(rstd[:], rstd[:], AF.Sqrt, ["rstd"], ["rstd"])
            P.op("dve", lambda e: e.reciprocal(out=rstd[:], in_=rstd[:]), reads=["rstd"], writes=["rstd"])
            for sub in range(4):
                stt("dve", h[:, sub, :], h[:, sub, :], rstd[:, sub:sub + 1], gfin[:],
                    ALU.mult, ALU.mult, [HH(sub), "rstd"], [HH(sub)])
                dma("pool", outv[i, sub], h[:, sub, :], [HH(sub)], [("out", sub)], semkey=("out", sub))
        P.emit_phase()


def kernel(**inputs):
    nc = build_program()
    nb_ = int(np.asarray(inputs["x"]).shape[0])
    in_maps = []
    for b in range(nb_):
        m = {"x": np.ascontiguousarray(np.asarray(inputs["x"])[b], dtype=np.float32)}
        for k, shp in PARAM_SHAPES.items():
            m[k] = np.ascontiguousarray(np.asarray(inputs[k], dtype=np.float32).reshape(shp))
        in_maps.append(m)
    res = run_bass_kernel_spmd(nc, in_maps, core_ids=list(range(nb_)))
    return np.stack([np.asarray(r["out"], dtype=np.float32) for r in res.results], axis=0)
```
